# Optimizing a Trainium2 kernel written in Bass

```python
import math
import jax, jax.numpy as jnp
from jax import lax
import numpy as np

D_MODEL = 1024
BATCH = 8
SEQ = 2048
DEPTH = 4

N_MIXERS = 2
N_REC_LAYERS = (DEPTH + 1) // 2
N_ATT_LAYERS = DEPTH // 2
EPS = 1e-6

D_RNN = D_MODEL
RG_BLOCKS = 4
RG_BW = D_RNN // RG_BLOCKS
CONV_W = 4
RG_C = 8.0

H_ATT = 16
D_QK = 64
D_V = 64
D_LAT = 128
H_IDX = 8
D_IDX = 64
TOPK_MAX = 256
Q_BLOCK = 128
ATT_SCALE = D_QK ** -0.5
IDX_W_SCALE = (H_IDX ** -0.5) * (D_IDX ** -0.5)
ATT_IN_SPLITS = [H_ATT * D_QK,
                 H_ATT * D_QK + D_LAT,
                 H_ATT * D_QK + D_LAT + H_IDX * D_IDX,
                 H_ATT * D_QK + D_LAT + H_IDX * D_IDX + D_IDX]
ATT_IN_DIM = ATT_IN_SPLITS[-1] + H_IDX

NUM_BUCKETS = 32
MAX_DISTANCE = 128
MAX_EXACT = NUM_BUCKETS // 2

D_FF = 4 * D_MODEL

kernel_name = "hybrid_rglru_dsa_sqrelu_trunk"


def rms_norm(x, g):
    xf = x.astype(jnp.float32)
    y = xf * lax.rsqrt(jnp.mean(xf * xf, axis=-1, keepdims=True) + EPS)
    return (y * g.astype(jnp.float32)).astype(x.dtype)


def t5_bucket(dist):
    n = jnp.maximum(dist, 0)
    nf = jnp.maximum(n, 1).astype(jnp.float32)
    large = MAX_EXACT + (jnp.log(nf / MAX_EXACT) / math.log(MAX_DISTANCE / MAX_EXACT)
                         * (NUM_BUCKETS - MAX_EXACT)).astype(jnp.int32)
    large = jnp.minimum(large, NUM_BUCKETS - 1)
    return jnp.where(n < MAX_EXACT, n, large)


def rglru_mixer(h, w_in, conv_w, conv_b, w_a, b_a, w_x, b_x, lam, w_out):
    B, S, _ = h.shape
    gate, xr = jnp.split(h @ w_in, 2, axis=-1)
    gate = jax.nn.gelu(gate)
    xpad = jnp.pad(xr, ((0, 0), (CONV_W - 1, 0), (0, 0)))
    xc = conv_b + xpad[:, 0:S] * conv_w[0]
    for k in range(1, CONV_W):
        xc = xc + xpad[:, k:k + S] * conv_w[k]
    xb = xc.reshape(B, S, RG_BLOCKS, RG_BW)
    r = jax.nn.sigmoid(jnp.einsum('bsnj,nji->bsni', xb, w_a).reshape(B, S, D_RNN) + b_a)
    i = jax.nn.sigmoid(jnp.einsum('bsnj,nji->bsni', xb, w_x).reshape(B, S, D_RNN) + b_x)
    log_a = (-RG_C * r.astype(jnp.float32)) * jax.nn.softplus(-lam.astype(jnp.float32))
    a = jnp.exp(log_a)
    u = jnp.sqrt(-jnp.expm1(2.0 * log_a)) * (i * xc).astype(jnp.float32)

    def combine(left, right):
        a_l, b_l = left
        a_r, b_r = right
        return a_l * a_r, a_r * b_l + b_r

    _, hs = lax.associative_scan(combine, (a, u), axis=1)
    y = hs.astype(h.dtype) * gate
    return y @ w_out


def dsa_mixer(h, w_in, kv_norm_g, w_uk, w_uv, w_o, rel_bias):
    B, S, _ = h.shape
    n_blk = S // Q_BLOCK
    topk = min(TOPK_MAX, S // 4)
    q, c, qi, ki, wi = jnp.split(h @ w_in, ATT_IN_SPLITS, axis=-1)
    q = q.reshape(B, S, H_ATT, D_QK)
    c = rms_norm(c, kv_norm_g)
    qi = qi.reshape(B, S, H_IDX, D_IDX)
    wi = wi * IDX_W_SCALE
    q_lat = jnp.einsum('bshd,hdc->bshc', q, w_uk) * ATT_SCALE
    key_pos = jnp.arange(S)

    def block(args):
        q_lat_b, qi_b, wi_b, pos_b = args
        dots = jax.nn.relu(jnp.einsum('bqhd,bsd->bqhs', qi_b, ki))
        score = jnp.einsum('bqh,bqhs->bqs', wi_b, dots).astype(jnp.float32)
        causal = key_pos[None, :] <= pos_b[:, None]
        score = jnp.where(causal[None], score, -jnp.inf)
        _, idx = lax.top_k(score, topk)
        c_sel = jax.vmap(lambda cb, ib: cb[ib])(c, idx)
        dist = pos_b[None, :, None] - idx
        bias = jnp.moveaxis(rel_bias[t5_bucket(dist)], -1, 2)
        logits = jnp.einsum('bqhc,bqkc->bqhk', q_lat_b, c_sel).astype(jnp.float32) \
            + bias.astype(jnp.float32)
        logits = jnp.where((dist >= 0)[:, :, None, :], logits, -jnp.inf)
        p = jax.nn.softmax(logits, axis=-1).astype(c.dtype)
        return jnp.einsum('bqhk,bqkc->bqhc', p, c_sel)

    def to_blocks(a):
        return a.reshape(B, n_blk, Q_BLOCK, *a.shape[2:]).swapaxes(0, 1)

    o_lat = lax.map(block, (to_blocks(q_lat), to_blocks(qi), to_blocks(wi),
                            key_pos.reshape(n_blk, Q_BLOCK)))
    o_lat = o_lat.swapaxes(0, 1).reshape(B, S, H_ATT, D_LAT)
    o = jnp.einsum('bshc,hcv->bshv', o_lat, w_uv).reshape(B, S, H_ATT * D_V)
    return o @ w_o


def sq_relu_mlp(h, w_up, w_down):
    return jnp.square(jax.nn.relu(h @ w_up)) @ w_down


def setup_inputs(seed: int = 0) -> dict:
    key = jax.random.key(seed)
    ks = jax.random.split(key, 24)
    nr, na, L = N_REC_LAYERS, N_ATT_LAYERS, DEPTH
    f32 = jnp.float32

    def nrm(k, shape, scale):
        return jax.random.normal(k, shape, f32) * scale

    u = jax.random.uniform(ks[10], (nr, D_RNN), f32, 0.9, 0.999)
    return {
        "x": jax.random.normal(ks[0], (BATCH, SEQ, D_MODEL), f32),
        "norm_mix_g": 1.0 + nrm(ks[1], (L, D_MODEL), 0.02),
        "norm_mlp_g": 1.0 + nrm(ks[2], (L, D_MODEL), 0.02),
        "final_norm_g": 1.0 + nrm(ks[3], (D_MODEL,), 0.02),
        "rec_w_in": nrm(ks[4], (nr, D_MODEL, 2 * D_RNN), D_MODEL ** -0.5),
        "rec_conv_w": nrm(ks[5], (nr, CONV_W, D_RNN), CONV_W ** -0.5),
        "rec_conv_b": nrm(ks[6], (nr, D_RNN), 0.01),
        "rec_w_a": nrm(ks[7], (nr, RG_BLOCKS, RG_BW, RG_BW), RG_BW ** -0.5),
        "rec_b_a": nrm(ks[8], (nr, D_RNN), 0.01),
        "rec_w_x": nrm(ks[9], (nr, RG_BLOCKS, RG_BW, RG_BW), RG_BW ** -0.5),
        "rec_b_x": nrm(ks[11], (nr, D_RNN), 0.01),
        "rec_lambda": jnp.log(u) - jnp.log1p(-u),
        "rec_w_out": nrm(ks[12], (nr, D_RNN, D_MODEL), D_RNN ** -0.5),
        "att_w_in": nrm(ks[13], (na, D_MODEL, ATT_IN_DIM), D_MODEL ** -0.5),
        "att_kv_norm_g": 1.0 + nrm(ks[14], (na, D_LAT), 0.02),
        "att_w_uk": nrm(ks[15], (na, H_ATT, D_QK, D_LAT), D_LAT ** -0.5),
        "att_w_uv": nrm(ks[16], (na, H_ATT, D_LAT, D_V), D_LAT ** -0.5),
        "att_w_o": nrm(ks[17], (na, H_ATT * D_V, D_MODEL), (H_ATT * D_V) ** -0.5),
        "rel_bias": nrm(ks[18], (NUM_BUCKETS, H_ATT), 0.2),
        "mlp_w_up": nrm(ks[19], (L, D_MODEL, D_FF), D_MODEL ** -0.5),
        "mlp_w_down": nrm(ks[20], (L, D_FF, D_MODEL), D_FF ** -0.5),
    }


def reference(x, norm_mix_g, norm_mlp_g, final_norm_g,
              rec_w_in, rec_conv_w, rec_conv_b, rec_w_a, rec_b_a, rec_w_x, rec_b_x,
              rec_lambda, rec_w_out,
              att_w_in, att_kv_norm_g, att_w_uk, att_w_uv, att_w_o, rel_bias,
              mlp_w_up, mlp_w_down):
    for layer in range(DEPTH):
        j = layer // N_MIXERS
        hn = rms_norm(x, norm_mix_g[layer])
        if layer % N_MIXERS == 0:
            mix = rglru_mixer(hn, rec_w_in[j], rec_conv_w[j], rec_conv_b[j],
                              rec_w_a[j], rec_b_a[j], rec_w_x[j], rec_b_x[j],
                              rec_lambda[j], rec_w_out[j])
        else:
            mix = dsa_mixer(hn, att_w_in[j], att_kv_norm_g[j], att_w_uk[j],
                            att_w_uv[j], att_w_o[j], rel_bias)
        x = x + mix
        x = x + sq_relu_mlp(rms_norm(x, norm_mlp_g[layer]), mlp_w_up[layer], mlp_w_down[layer])
    return rms_norm(x, final_norm_g)
```

```python
import math
from contextlib import ExitStack
import numpy as np
import concourse.bass as bass
import concourse.mybir as mybir
from concourse.bass_utils import run_bass_kernel_spmd

F32 = mybir.dt.float32
BF16 = mybir.dt.bfloat16
AF = mybir.ActivationFunctionType
ALU = mybir.AluOpType

S_LEN = 2048
D = 1024
NCH = 8
DEPTH = 4
EPS = 1e-6
RG_C = 8.0
H_ATT = 16
H_IDX = 8
ATT_SCALE = 64 ** -0.5
IDX_W_SCALE = (8 ** -0.5) * (64 ** -0.5)
NEG_MASK = -1.0e30
NEG_REPL = -3.0e38

VG_MIX = 0
VG_MLP = 32
VG_FIN = 64
V_CONVW = 72
V_CONVB = 136
V_BA = 152
V_BX = 168
V_LAM = 184
V_KVG = 200
V_POW = 202
NIT = 18
NV = 202 + NIT + 1

def sid_mlp_up(l, g): return l * 16 + g
def sid_mlp_dn(l, d): return l * 16 + 8 + d
def sid_rec_in(j, g): return 64 + j * 7 + g
def sid_rec_gate(j): return 64 + j * 7 + 4
def sid_rec_out(j, g): return 64 + j * 7 + 5 + g
def sid_att_in(j, g): return 78 + j * 6 + g
def sid_att_o(j, g): return 78 + j * 6 + 4 + g
NSLOT = 90


class Tok:
    __slots__ = ("w", "r")

    def __init__(self):
        self.w = None
        self.r = []


class _Eng:
    def __init__(self, name, sem):
        self.name = name
        self.sem = sem
        self.count = 0
        self.instrs = []
        self.seen = {}
        self.pend_r = []
        self.pend_w = []
        self.bar = []


class Sched:
    NDMA = 8

    def __init__(self, nc, stack):
        self.nc = nc
        self.sems = {}
        self.eng = {}
        for name in ("pe", "act", "dve", "pool", "sp"):
            self.sems["s_" + name] = stack.enter_context(nc.semaphore("s_" + name))
            self.eng[name] = _Eng(name, "s_" + name)
        self.dma_sems = {}
        self.dma_n = {}
        for q in ("sp", "pool"):
            lst = []
            for i in range(self.NDMA):
                key = "d_%s%d" % (q, i)
                self.sems[key] = stack.enter_context(nc.semaphore(key))
                lst.append(key)
            self.dma_sems[q] = lst
            self.dma_n[q] = 0

    def _deps(self, E, reads, writes):
        need = {}

        def add(ev):
            if ev is None:
                return
            k, v = ev
            if need.get(k, 0) < v:
                need[k] = v
        for t in reads:
            add(t.w)
        for t in writes:
            add(t.w)
            for ev in t.r:
                add(ev)
        for ev in E.bar:
            add(ev)
        E.bar = []
        waits = []
        for k, v in need.items():
            if E.seen.get(k, 0) < v:
                E.seen[k] = v
                waits.append((k, v))
        return waits

    def op(self, eng, fn, reads=(), writes=(), inc=True):
        E = self.eng[eng]
        waits = self._deps(E, reads, writes)
        if inc:
            E.count += 1
            ev = (E.sem, E.count)
            E.instrs.append((waits, fn, E.sem, 1))
            for t in list(reads) + E.pend_r:
                t.r.append(ev)
            for t in list(writes) + E.pend_w:
                t.w = ev
                t.r = []
            E.pend_r = []
            E.pend_w = []
            return ev
        E.instrs.append((waits, fn, None, 0))
        E.pend_r.extend(reads)
        E.pend_w.extend(writes)
        return None

    def dma(self, q, fn, reads=(), writes=()):
        E = self.eng[q]
        n = self.dma_n[q]
        self.dma_n[q] = n + 1
        key = self.dma_sems[q][n % self.NDMA]
        val = 16 * (n // self.NDMA + 1)
        waits = self._deps(E, reads, writes)
        if val > 16 and E.seen.get(key, 0) < val - 16:
            E.seen[key] = val - 16
            waits.append((key, val - 16))
        E.instrs.append((waits, fn, key, 16))
        ev = (key, val)
        for t in reads:
            t.r.append(ev)
        for t in writes:
            t.w = ev
            t.r = []
        return ev

    def all_events(self, engines=("pe", "act", "dve", "pool", "sp"), queues=("sp", "pool")):
        evs = []
        for e in engines:
            E = self.eng[e]
            if E.count > 0:
                evs.append((E.sem, E.count))
        for q in queues:
            n = self.dma_n[q]
            for i in range(min(n, self.NDMA)):
                cnt = (n - 1 - i) // self.NDMA + 1
                evs.append((self.dma_sems[q][i], 16 * cnt))
        return evs

    def barrier(self, engines=("pe", "act", "dve", "sp")):
        evs = self.all_events(engines=engines, queues=("sp",))
        for e in engines:
            self.eng[e].bar.extend(evs)

    def emit(self, final_events=()):
        nc = self.nc
        sems = self.sems
        fw = [ev for ev in final_events if ev is not None]
        with nc.Block() as block:
            def run(E, h, tail=()):
                for waits, fn, isem, ival in E.instrs:
                    for k, v in waits:
                        h.wait_ge(sems[k], v)
                    ins = fn(h)
                    if isem is not None:
                        ins.then_inc(sems[isem], ival)
                for k, v in tail:
                    h.wait_ge(sems[k], v)

            @block.sync
            def _(h):
                run(self.eng["sp"], h, fw)

            @block.tensor
            def _(h):
                run(self.eng["pe"], h)

            @block.scalar
            def _(h):
                run(self.eng["act"], h)

            @block.vector
            def _(h):
                run(self.eng["dve"], h)

            @block.gpsimd
            def _(h):
                run(self.eng["pool"], h)


class Mem:
    def __init__(self, nc, base, top):
        self.nc = nc
        self.cur = (base + 31) // 32 * 32
        self.top = top
        self.n = 0

    def alloc(self, name, shape, dtype):
        nbytes = int(np.prod(shape[1:])) * (2 if dtype == BF16 else 4)
        nbytes = (nbytes + 31) // 32 * 32
        assert self.cur + nbytes <= self.top, "SBUF overflow %s: need %d have %d" % (name, nbytes, self.top - self.cur)
        self.n += 1
        t = self.nc.alloc_sbuf_tensor_at("%s_%d" % (name, self.n), list(shape), dtype, offset=self.cur)
        self.cur += nbytes
        return t


DBG = {}


def weight_plan(n_layers):
    plan = []
    for layer in range(n_layers):
        j = layer // 2
        if layer % 2 == 0:
            plan += [sid_rec_in(j, g) for g in range(4)]
            for T in range(4):
                if T < 3:
                    plan += [sid_rec_in(j, g) for g in range(4)]
                plan += [sid_rec_out(j, g) for g in range(2)]
        elif not DBG.get("skip_dsa"):
            for T in range(4):
                plan += [sid_att_in(j, g) for g in range(4)]
                plan += [sid_att_o(j, g) for g in range(2)]
        for T in range(2):
            if DBG.get("skip_mlp"):
                continue
            for _rep in range(DBG.get("mlp_rep", 1)):
                plan += [sid_mlp_up(layer, g) for g in range(8)]
                plan += [sid_mlp_dn(layer, d) for d in range(8)]
    return plan


def build_program(n_layers=DEPTH):
    nc = bass.Bass("TRN2", target_bir_lowering=False)
    x_d = nc.dram_tensor("x", [S_LEN, D], F32, kind="ExternalInput").ap()
    vec_d = nc.dram_tensor("vecs", [128, NV], F32, kind="ExternalInput").ap()
    w_d = nc.dram_tensor("wslots", [NSLOT, 128, 4096], F32, kind="ExternalInput").ap()
    attw_d = nc.dram_tensor("attw", [2, 128, 3072], F32, kind="ExternalInput").ap()
    btab_d = nc.dram_tensor("btab", [128, 4096], F32, kind="ExternalInput").ap()
    far_d = nc.dram_tensor("farb", [128, 16], F32, kind="ExternalInput").ap()
    out_d = nc.dram_tensor("out", [S_LEN, D], F32, kind="ExternalOutput").ap()

    with ExitStack() as st:
        S = Sched(nc, st)
        mem = Mem(nc, nc.sbuf_base, nc.sbuf_top)
        banks = [st.enter_context(nc.psum_tensor("bank%d" % i, [128, 512], F32)) for i in range(8)]
        bt = [Tok() for _ in range(8)]

        xT = mem.alloc("xT", [128, NCH, S_LEN], F32)
        xtok = [[Tok() for _ in range(4)] for _ in range(NCH)]
        ring = [mem.alloc("ring%d" % i, [128, 4096], BF16) for i in range(3)]
        rtok = [Tok() for _ in range(3)]
        vecs = mem.alloc("vecs", [128, NV], F32); t_vecs = Tok()
        ident = mem.alloc("ident", [128, 128], F32); t_ident = Tok()
        identb = mem.alloc("identb", [128, 128], BF16); t_identb = Tok()
        onesb = mem.alloc("onesb", [128, 128], BF16); t_ones = Tok()
        clam = mem.alloc("clam", [128, 16], F32); t_clam = Tok()
        cneg = mem.alloc("cneg", [128, 128], F32); t_cneg = Tok()
        arena_base = mem.cur

        plan = weight_plan(n_layers)
        wstate = {"i": 0, "issued": 0}

        def w_issue_upto(n):
            while wstate["issued"] < min(n, len(plan)):
                i = wstate["issued"]
                sid = plan[i]
                r = ring[i % 3]
                S.dma("pool", lambda h, r=r, sid=sid: h.dma_start(out=r[:, :], in_=w_d[sid]),
                      writes=[rtok[i % 3]])
                wstate["issued"] += 1

        def w_next(sid):
            i = wstate["i"]
            assert plan[i] == sid, (i, plan[i], sid)
            w_issue_upto(i + 1)
            wstate["i"] = i + 1
            return ring[i % 3], rtok[i % 3], i

        def w_prefetch():
            w_issue_upto(wstate["i"] + 2)

        S.dma("sp", lambda h: h.dma_start(out=vecs[:, :], in_=vec_d), writes=[t_vecs])
        S.op("pool", lambda h: h.memset(ident[:, :], 0.0), writes=[t_ident])
        S.op("pool", lambda h: h.affine_select(out=ident[:, :], in_=ident[:, :], pattern=[[-1, 128]],
                                                compare_op=ALU.not_equal, fill=1.0, base=0, channel_multiplier=1),
             reads=[t_ident], writes=[t_ident])
        S.op("pool", lambda h: h.tensor_copy(out=identb[:, :], in_=ident[:, :]), reads=[t_ident], writes=[t_identb])
        S.op("pool", lambda h: h.memset(onesb[:, :], 1.0), writes=[t_ones])
        S.op("pool", lambda h: h.memset(cneg[:, :], 0.0), writes=[t_cneg])
        S.op("pool", lambda h: h.affine_select(out=cneg[:, :], in_=cneg[:, :], pattern=[[-1, 128]],
                                                compare_op=ALU.is_ge, fill=NEG_MASK, base=0, channel_multiplier=1),
             reads=[t_cneg], writes=[t_cneg])
        S.op("act", lambda h: h.activation(out=clam[:, :], in_=vecs[:, V_LAM:V_LAM + 16], func=AF.Exp, scale=-1.0),
             reads=[t_vecs], writes=[t_clam])
        S.op("act", lambda h: h.activation(out=clam[:, :], in_=clam[:, :], func=AF.Ln, bias=1.0, scale=1.0),
             reads=[t_clam], writes=[t_clam])
        S.op("dve", lambda h: h.tensor_scalar(out=clam[:, :], in0=clam[:, :], scalar1=-RG_C, scalar2=None, op0=ALU.mult),
             reads=[t_clam], writes=[t_clam])
        w_issue_upto(2)

        class Rot:
            def __init__(self, items):
                self.items = items
                self.i = 0

            def next(self):
                it = self.items[self.i % len(self.items)]
                self.i += 1
                return it

        def arena(with_pool=False):
            S.barrier()
            if with_pool:
                S.eng["pool"].bar.extend(S.all_events(engines=("pe", "act", "dve", "sp"), queues=("sp",)))
            return Mem(nc, arena_base, nc.sbuf_top)

        evac = {"i": 0}

        def evac_copy(out_ap, in_ap, reads, writes, eng=None):
            if eng is None:
                eng = "act" if evac["i"] % 2 == 0 else "dve"
                evac["i"] += 1
            if eng == "act":
                return S.op("act", lambda h: h.copy(out=out_ap, in_=in_ap), reads=reads, writes=writes)
            return S.op("dve", lambda h: h.tensor_copy(out=out_ap, in_=in_ap), reads=reads, writes=writes)

        def phase_load():
            am = arena()
            stg = [am.alloc("stg", [128, D], F32) for _ in range(2)]
            stk = [Tok() for _ in range(2)]
            brot = Rot(list(range(8)))
            for t in range(16):
                sg, sk = stg[t % 2], stk[t % 2]
                S.dma("sp", lambda h, sg=sg, t=t: h.dma_start(out=sg[:, :], in_=x_d[t * 128:(t + 1) * 128, :]),
                      writes=[sk])
                for half in range(2):
                    b = brot.next()
                    for i in range(4):
                        c = half * 4 + i
                        S.op("pe", lambda h, b=b, i=i, c=c, sg=sg: h.transpose(banks[b][:, i * 128:(i + 1) * 128],
                                                                              sg[:, c * 128:(c + 1) * 128], ident[:, :]),
                             reads=[sk, t_ident], writes=[bt[b]], inc=(i == 3))
                    o_ap = xT[:, half * 4:half * 4 + 4, t * 128:(t + 1) * 128]
                    i_ap = banks[b][:, 0:512].rearrange("p (a b) -> p a b", a=4)
                    evac_copy(o_ap, i_ap, reads=[bt[b]], writes=[xtok[c][t // 4] for c in range(half * 4, half * 4 + 4)])

        def norm_tile(am_bufs, tok0, ntok, gcol, hn, hn_tok, ssbank, out_f32=False):
            sq_rot, sq_tok, rs, rs_tok = am_bufs
            for s0 in range(0, ntok, 512):
                t4 = (tok0 + s0) // 512
                for c in range(NCH):
                    k = sq_rot.next()
                    S.op("act", lambda h, k=k, c=c, s0=s0: h.activation(out=k[0][:, :], in_=xT[:, c, tok0 + s0:tok0 + s0 + 512],
                                                                         func=AF.Square),
                         reads=[xtok[c][t4]], writes=[k[1]])
                    S.op("pe", lambda h, k=k, c=c: h.matmul(banks[ssbank][:, :], lhsT=onesb[:, :], rhs=k[0][:, :],
                                                           start=(c == 0), stop=(c == NCH - 1)),
                         reads=[k[1], t_ones], writes=[bt[ssbank]], inc=True)
                S.op("act", lambda h: h.activation(out=rs[:, :], in_=banks[ssbank][:, :], func=AF.Sqrt,
                                                   bias=EPS, scale=1.0 / D),
                     reads=[bt[ssbank]], writes=[rs_tok])
                S.op("dve", lambda h: h.reciprocal(out=rs[:, :], in_=rs[:, :]), reads=[rs_tok], writes=[rs_tok])
                for c in range(NCH):
                    S.op("dve", lambda h, c=c, s0=s0: h.scalar_tensor_tensor(
                        out=hn[:, c, s0:s0 + 512], in0=xT[:, c, tok0 + s0:tok0 + s0 + 512],
                        scalar=vecs[:, gcol + c:gcol + c + 1], op0=ALU.mult, in1=rs[:, :], op1=ALU.mult),
                        reads=[xtok[c][t4], rs_tok, t_vecs], writes=[hn_tok])

        def norm_bufs(am):
            sq = [(am.alloc("sq", [128, 512], BF16), Tok()) for _ in range(2)]
            rs = am.alloc("rs", [128, 512], F32)
            return (Rot(sq), None, rs, Tok())

        def phase_mlp(layer):
            am = arena()
            nb = norm_bufs(am)
            hn = am.alloc("hn", [128, NCH, 1024], BF16); t_hn = Tok()
            aT = am.alloc("aT", [128, 32, 1024], BF16)
            a_tok = [[Tok() for _ in range(2)] for _ in range(32)]
            rl = [(am.alloc("rl", [128, 512], BF16), Tok()) for _ in range(3)]
            rl_rot = Rot(rl)
            up_rot = Rot([0, 1, 2])
            dn_rot = Rot([3, 4, 5])
            ssbank = 7
            for T in range(2):
                tok0 = T * 1024
                norm_tile(nb, tok0, 1024, VG_MLP + layer * 8, hn, t_hn, ssbank)
                for g in range(8):
                    slot, stok, _ = w_next(sid_mlp_up(layer, g))
                    sv = slot[:, :].rearrange("p (c n) -> p c n", c=8)
                    for fc in range(4):
                        f = g * 4 + fc
                        for half in range(2):
                            b = up_rot.next()
                            for c in range(NCH):
                                S.op("pe", lambda h, b=b, c=c, fc=fc, half=half, sv=sv: h.matmul(
                                    banks[b][:, :], lhsT=sv[:, c, fc * 128:(fc + 1) * 128],
                                    rhs=hn[:, c, half * 512:(half + 1) * 512], start=(c == 0), stop=(c == NCH - 1)),
                                    reads=[stok, t_hn], writes=[bt[b]], inc=(c == NCH - 1))
                            k = rl_rot.next()
                            S.op("act", lambda h, b=b, k=k: h.activation(out=k[0][:, :], in_=banks[b][:, :], func=AF.Relu),
                                 reads=[bt[b]], writes=[k[1]])
                            S.op("dve", lambda h, b=b, k=k, f=f, half=half: h.tensor_tensor(
                                out=aT[:, f, half * 512:(half + 1) * 512], in0=k[0][:, :], in1=banks[b][:, :], op=ALU.mult),
                                reads=[k[1], bt[b]], writes=[a_tok[f][half]])
                    w_prefetch()
                for d in range(8):
                    slot, stok, _ = w_next(sid_mlp_dn(layer, d))
                    sv = slot[:, :].rearrange("p (f n) -> p f n", f=32)
                    for half in range(2):
                        b = dn_rot.next()
                        for f in range(32):
                            S.op("pe", lambda h, b=b, f=f, half=half, sv=sv: h.matmul(
                                banks[b][:, :], lhsT=sv[:, f, :], rhs=aT[:, f, half * 512:(half + 1) * 512],
                                start=(f == 0), stop=(f == 31)),
                                reads=[stok, a_tok[f][half]], writes=[bt[b]], inc=(f == 31))
                        t4 = T * 2 + half
                        S.op("dve", lambda h, b=b, d=d, t4=t4: h.tensor_tensor(
                            out=xT[:, d, t4 * 512:(t4 + 1) * 512], in0=xT[:, d, t4 * 512:(t4 + 1) * 512],
                            in1=banks[b][:, :], op=ALU.add),
                            reads=[bt[b], xtok[d][t4]], writes=[xtok[d][t4]])
                    w_prefetch()

        def phase_rglru(layer):
            j = layer // 2
            am = arena(with_pool=True)
            nb = norm_bufs(am)
            hn = am.alloc("hn", [128, NCH, 512], BF16); t_hn = Tok()
            gates = [am.alloc("gate", [128, NCH, 512], BF16) for _ in range(2)]
            g_toks = [[Tok() for _ in range(NCH)] for _ in range(2)]
            xr = am.alloc("xr", [128, NCH, 515], F32); xr_tok = [Tok() for _ in range(NCH)]
            yT = am.alloc("yT", [128, NCH, 512], BF16); y_tok = [Tok() for _ in range(4)]
            hst = am.alloc("hst", [128, NCH], F32); hs_tok = [Tok() for _ in range(NCH)]
            gw = am.alloc("gw", [128, 4096], BF16); t_gw = Tok()
            sets = []
            for _ in range(2):
                sets.append(dict(
                    xc=am.alloc("xc", [128, 2, 512], F32), t_xc=Tok(),
                    xcb=am.alloc("xcb", [128, 2, 512], BF16), t_xcb=Tok(),
                    r=am.alloc("r", [128, 2, 512], F32), t_r=Tok(),
                    i=am.alloc("i", [128, 2, 512], F32), t_i=Tok(),
                    m=am.alloc("m", [128, 2, 512], F32), t_m=Tok(),
                    hs=am.alloc("hs", [128, 2, 512], F32), t_hs=Tok()))
            S.dma("pool", lambda h: h.dma_start(out=gw[:, :], in_=w_d[sid_rec_gate(j)]), writes=[t_gw])
            gv = gw[:, :].rearrange("p (w n j i) -> p w n j i", w=2, n=4, j=2)
            S.op("dve", lambda h: h.memset(xr[:, :, 0:3], 0.0), writes=xr_tok)
            S.op("dve", lambda h: h.memset(hst[:, :], 0.0), writes=hs_tok)
            prot = Rot([0, 1, 2, 3])
            grot = Rot([4, 5, 6])
            ssbank = 7
            cw = V_CONVW + j * 32
            prog = {"conv": 0}

            def gen_win(T):
                tok0 = T * 512
                gate = gates[T % 2]
                g_tok = g_toks[T % 2]
                norm_tile(nb, tok0, 512, VG_MIX + layer * 8, hn, t_hn, ssbank)
                yield
                for g in range(4):
                    slot, stok, _ = w_next(sid_rec_in(j, g))
                    sv = slot[:, :].rearrange("p (c n) -> p c n", c=8)
                    for oc in range(4):
                        o = g * 4 + oc
                        if o >= 8:
                            while prog["conv"] <= (o - 8) // 2 and prog["conv"] < 4 and T > 0:
                                yield
                        b = prot.next()
                        for c in range(NCH):
                            S.op("pe", lambda h, b=b, c=c, oc=oc, sv=sv: h.matmul(
                                banks[b][:, :], lhsT=sv[:, c, oc * 128:(oc + 1) * 128], rhs=hn[:, c, :],
                                start=(c == 0), stop=(c == NCH - 1)),
                                reads=[stok, t_hn], writes=[bt[b]], inc=(c == NCH - 1))
                        if o < 8:
                            S.op("act", lambda h, b=b, o=o, gate=gate: h.activation(out=gate[:, o, :], in_=banks[b][:, :],
                                                                                    func=AF.Gelu_apprx_tanh),
                                 reads=[bt[b]], writes=[g_tok[o]])
                        else:
                            c2 = o - 8
                            S.op("act", lambda h, b=b, c2=c2: h.copy(out=xr[:, c2, 3:515], in_=banks[b][:, :]),
                                 reads=[bt[b]], writes=[xr_tok[c2]])
                        yield
                    w_prefetch()

            def gen_ew(T):
                gate = gates[T % 2]
                g_tok = g_toks[T % 2]
                prog["conv"] = 0

                def stage_a(n):
                    B = sets[n % 2]
                    c0 = 2 * n
                    for cc in range(2):
                        c = c0 + cc
                        S.op("dve", lambda h, B=B, cc=cc, c=c: h.tensor_scalar(
                            out=B["xc"][:, cc, :], in0=xr[:, c, 0:512], scalar1=vecs[:, cw + c:cw + c + 1],
                            scalar2=vecs[:, V_CONVB + j * 8 + c:V_CONVB + j * 8 + c + 1], op0=ALU.mult, op1=ALU.add),
                            reads=[xr_tok[c], t_vecs], writes=[B["t_xc"]])
                        for k in range(1, 4):
                            S.op("dve", lambda h, B=B, cc=cc, c=c, k=k: h.scalar_tensor_tensor(
                                out=B["xc"][:, cc, :], in0=xr[:, c, k:k + 512],
                                scalar=vecs[:, cw + k * 8 + c:cw + k * 8 + c + 1], op0=ALU.mult,
                                in1=B["xc"][:, cc, :], op1=ALU.add),
                                reads=[xr_tok[c], t_vecs, B["t_xc"]], writes=[B["t_xc"]])
                        S.op("act", lambda h, c=c: h.copy(out=xr[:, c, 0:3], in_=xr[:, c, 512:515]),
                             reads=[xr_tok[c]], writes=[xr_tok[c]])
                    prog["conv"] = n + 1
                    S.op("act", lambda h, B=B: h.copy(out=B["xcb"][:, :, :], in_=B["xc"][:, :, :]),
                         reads=[B["t_xc"]], writes=[B["t_xcb"]])
                    yield
                    for cc in range(2):
                        c = c0 + cc
                        for wsel, dst, tk, bcol in ((0, "r", "t_r", V_BA), (1, "i", "t_i", V_BX)):
                            b = grot.next()
                            for jc in range(2):
                                S.op("pe", lambda h, b=b, wsel=wsel, n=n, jc=jc, cc=cc, B=B: h.matmul(
                                    banks[b][:, :], lhsT=gv[:, wsel, n, jc, cc * 128:(cc + 1) * 128],
                                    rhs=B["xcb"][:, jc, :], start=(jc == 0), stop=(jc == 1)),
                                    reads=[t_gw, B["t_xcb"]], writes=[bt[b]], inc=(jc == 1))
                            S.op("act", lambda h, b=b, B=B, dst=dst, cc=cc, bcol=bcol, c=c: h.activation(
                                out=B[dst][:, cc, :], in_=banks[b][:, :], func=AF.Sigmoid,
                                bias=vecs[:, bcol + j * 8 + c:bcol + j * 8 + c + 1], scale=1.0),
                                reads=[bt[b], t_vecs], writes=[B[tk]])
                        yield
                    for cc in range(2):
                        c = c0 + cc
                        S.op("act", lambda h, B=B, cc=cc, c=c: h.activation(
                            out=B["r"][:, cc, :], in_=B["r"][:, cc, :], func=AF.Exp,
                            scale=clam[:, j * 8 + c:j * 8 + c + 1]),
                            reads=[B["t_r"], t_clam], writes=[B["t_r"]])
                    S.op("act", lambda h, B=B: h.activation(out=B["m"][:, :, :], in_=B["r"][:, :, :], func=AF.Square),
                         reads=[B["t_r"]], writes=[B["t_m"]])
                    S.op("act", lambda h, B=B: h.activation(out=B["m"][:, :, :], in_=B["m"][:, :, :], func=AF.Relu,
                                                           bias=1.0, scale=-1.0),
                         reads=[B["t_m"]], writes=[B["t_m"]])
                    S.op("act", lambda h, B=B: h.activation(out=B["m"][:, :, :], in_=B["m"][:, :, :], func=AF.Sqrt),
                         reads=[B["t_m"]], writes=[B["t_m"]])
                    yield

                def stage_b(n):
                    B = sets[n % 2]
                    c0 = 2 * n
                    S.op("dve", lambda h, B=B: h.tensor_tensor(out=B["i"][:, :, :], in0=B["i"][:, :, :], in1=B["xc"][:, :, :],
                                                              op=ALU.mult),
                         reads=[B["t_i"], B["t_xc"]], writes=[B["t_i"]])
                    S.op("dve", lambda h, B=B: h.tensor_tensor(out=B["i"][:, :, :], in0=B["i"][:, :, :], in1=B["m"][:, :, :],
                                                              op=ALU.mult),
                         reads=[B["t_i"], B["t_m"]], writes=[B["t_i"]])
                    yield
                    for cc in range(2):
                        c = c0 + cc
                        S.op("dve", lambda h, B=B, cc=cc, c=c: h.tensor_tensor_scan(
                            out=B["hs"][:, cc, :], data0=B["r"][:, cc, :], data1=B["i"][:, cc, :],
                            initial=hst[:, c:c + 1], op0=ALU.mult, op1=ALU.add),
                            reads=[B["t_r"], B["t_i"], hs_tok[c]], writes=[B["t_hs"]])
                        S.op("act", lambda h, B=B, cc=cc, c=c: h.copy(out=hst[:, c:c + 1], in_=B["hs"][:, cc, 511:512]),
                             reads=[B["t_hs"]], writes=[hs_tok[c]])
                    S.op("dve", lambda h, B=B, c0=c0, gate=gate: h.tensor_tensor(out=yT[:, c0:c0 + 2, :], in0=B["hs"][:, :, :],
                                                                                in1=gate[:, c0:c0 + 2, :], op=ALU.mult),
                         reads=[B["t_hs"], g_tok[c0], g_tok[c0 + 1]], writes=[y_tok[n]])
                    yield


                yield from stage_a(0)
                for n in range(4):
                    if n < 3:
                        yield from stage_a(n + 1)
                    yield from stage_b(n)

            def wout(T):
                for g in range(2):
                    slot, stok, _ = w_next(sid_rec_out(j, g))
                    sv = slot[:, :].rearrange("p (c n) -> p c n", c=8)
                    for dc in range(4):
                        d = g * 4 + dc
                        b = prot.next()
                        for c in range(NCH):
                            S.op("pe", lambda h, b=b, c=c, dc=dc, sv=sv: h.matmul(
                                banks[b][:, :], lhsT=sv[:, c, dc * 128:(dc + 1) * 128], rhs=yT[:, c, :],
                                start=(c == 0), stop=(c == NCH - 1)),
                                reads=[stok] + y_tok, writes=[bt[b]], inc=(c == NCH - 1))
                        S.op("dve", lambda h, b=b, d=d, T=T: h.tensor_tensor(
                            out=xT[:, d, T * 512:(T + 1) * 512], in0=xT[:, d, T * 512:(T + 1) * 512],
                            in1=banks[b][:, :], op=ALU.add),
                            reads=[bt[b], xtok[d][T]], writes=[xtok[d][T]])
                    w_prefetch()

            def run_ilv(ga, gb):
                da = db = False
                while not (da and db):
                    if not da:
                        try:
                            next(ga)
                        except StopIteration:
                            da = True
                            prog["conv"] = 4
                    if not db:
                        try:
                            next(gb)
                        except StopIteration:
                            db = True

            for _ in gen_win(0):
                pass
            for T in range(4):
                if T < 3:
                    run_ilv(gen_ew(T), gen_win(T + 1))
                else:
                    for _ in gen_ew(T):
                        pass
                wout(T)

        def phase_dsa(layer):
            j = layer // 2
            am = arena(with_pool=True)
            nb = norm_bufs(am)
            kiA = am.alloc("kiA", [128, S_LEN], BF16); ki_tok = [Tok() for _ in range(4)]
            kiB = am.alloc("kiB", [128, S_LEN], BF16)
            cTc = am.alloc("cTc", [128, S_LEN], BF16); c_tok = [Tok() for _ in range(4)]
            V = am.alloc("V", [128, 16, 16 * 65], BF16); v_tok = [Tok() for _ in range(16)]
            attw = am.alloc("attw", [128, 3072], BF16); t_attw = Tok()
            btab = am.alloc("btab", [128, 4096], BF16); t_btab = Tok()
            btf = am.alloc("btf", [128, 16], F32); t_btf = Tok()
            hn = am.alloc("hn", [128, NCH, 512], BF16); t_hn = Tok()
            qT = am.alloc("qT", [128, NCH, 512], BF16); q_tok = [Tok() for _ in range(NCH)]
            qiT = am.alloc("qiT", [128, 4, 512], BF16); qi_tok = [Tok() for _ in range(4)]
            craw = am.alloc("craw", [128, 512], F32); t_craw = Tok()
            wabs = am.alloc("wabs", [128, 32], F32); t_wabs = Tok()
            wsgn = am.alloc("wsgn", [128, 32], F32); t_wsgn = Tok()
            score = am.alloc("score", [128, S_LEN], F32); sc_tok = [Tok() for _ in range(4)]
            mask = am.alloc("mask", [128, S_LEN], BF16); t_mask = Tok()
            MT = am.alloc("MT", [128, 16, 128], BF16); mt_tok = [Tok() for _ in range(4)]
            qlat = am.alloc("qlat", [128, 16, 128], BF16); ql_tok = [Tok() for _ in range(4)]
            rl_rot = Rot([(am.alloc("rl", [128, 512], F32), Tok()) for _ in range(2)])
            e_rot = Rot([(am.alloc("e", [128, 512], BF16), Tok()) for _ in range(3)])
            em_rot = Rot([(am.alloc("em", [128, 512], BF16), Tok()) for _ in range(3)])
            on = am.alloc("on", [128, 1024], BF16); t_on = Tok()
            bis = am.alloc("bis", [128, 8], F32); t_bis = Tok(); t_bis2 = Tok()
            wtab = am.alloc("wtab", [128, NIT + 1], F32); t_wtab = Tok()
            recip = am.alloc("recip", [128, 16], F32); t_recip = Tok()
            S.dma("pool", lambda h: h.dma_start(out=attw[:, :], in_=attw_d[j]), writes=[t_attw])
            S.dma("pool", lambda h: h.dma_start(out=btab[:, :], in_=btab_d), writes=[t_btab])
            S.dma("sp", lambda h: h.dma_start(out=btf[:, :], in_=far_d), writes=[t_btf])
            bt4 = btab[:, :].rearrange("p (o h q) -> p o h q", o=2, h=16)
            for o in range(2):
                S.op("dve", lambda h, o=o: h.tensor_tensor(
                    out=bt4[:, o, :, :], in0=bt4[:, o, :, :],
                    in1=btf[:, :].unsqueeze(2).to_broadcast([128, 16, 128]), op=ALU.subtract),
                    reads=[t_btab, t_btf], writes=[t_btab])
            V4 = V[:, :, :].rearrange("p k (h e) -> p k h e", e=65)
            S.op("dve", lambda h: h.memset(V4[:, :, :, 64:65], 1.0), writes=v_tok)
            S.op("dve", lambda h: h.memset(kiA[64:128, :], 0.0), writes=ki_tok)
            S.op("dve", lambda h: h.memset(kiB[0:64, :], 0.0), writes=ki_tok)
            IDXE = DBG.get("idx_eng", "dve")
            prot = Rot([0, 1, 5, 6])
            lrot = Rot([0, 1])
            lrot3 = Rot([0, 1, 7])
            drot = Rot([5, 6])
            OB = [2, 3, 4]
            XB = 7
            SUB = DBG.get("dsa_sub", 9)
            for T in range(4):
                tok0 = T * 512
                if SUB < 3:
                    if SUB >= 1:
                        norm_tile(nb, tok0, 512, VG_MIX + layer * 8, hn, t_hn, XB)
                    for g in range(6):
                        sid = sid_att_in(j, g) if g < 4 else sid_att_o(j, g - 4)
                        slot, stok, _ = w_next(sid)
                        sv = slot[:, :].rearrange("p (c n) -> p c n", c=8)
                        if SUB >= 2 and g < 2:
                            for oc in range(4):
                                o = g * 4 + oc
                                b = prot.next()
                                for c in range(NCH):
                                    S.op("pe", lambda h, b=b, c=c, oc=oc, sv=sv: h.matmul(
                                        banks[b][:, :], lhsT=sv[:, c, oc * 128:(oc + 1) * 128], rhs=hn[:, c, :],
                                        start=(c == 0), stop=(c == NCH - 1)),
                                        reads=[stok, t_hn], writes=[bt[b]], inc=(c == NCH - 1))
                                evac_copy(qT[:, o, :], banks[b][:, :], reads=[bt[b]], writes=[q_tok[o]])
                        else:
                            S.op("pe", lambda h, sv=sv: h.matmul(banks[0][:, :], lhsT=sv[:, 0, 0:128], rhs=sv[:, 0, 0:512],
                                                                 start=True, stop=True), reads=[stok], writes=[bt[0]])
                        w_prefetch()
                    continue
                norm_tile(nb, tok0, 512, VG_MIX + layer * 8, hn, t_hn, XB)
                for g in range(2):
                    slot, stok, _ = w_next(sid_att_in(j, g))
                    sv = slot[:, :].rearrange("p (c n) -> p c n", c=8)
                    for oc in range(4):
                        o = g * 4 + oc
                        b = prot.next()
                        for c in range(NCH):
                            S.op("pe", lambda h, b=b, c=c, oc=oc, sv=sv: h.matmul(
                                banks[b][:, :], lhsT=sv[:, c, oc * 128:(oc + 1) * 128], rhs=hn[:, c, :],
                                start=(c == 0), stop=(c == NCH - 1)),
                                reads=[stok, t_hn], writes=[bt[b]], inc=(c == NCH - 1))
                        evac_copy(qT[:, o, :], banks[b][:, :], reads=[bt[b]], writes=[q_tok[o]])
                    w_prefetch()
                slot, stok, _ = w_next(sid_att_in(j, 2))
                sv = slot[:, :].rearrange("p (c n) -> p c n", c=8)
                b = prot.next()
                for c in range(NCH):
                    S.op("pe", lambda h, b=b, c=c, sv=sv: h.matmul(banks[b][:, :], lhsT=sv[:, c, 0:128], rhs=hn[:, c, :],
                                                                  start=(c == 0), stop=(c == NCH - 1)),
                         reads=[stok, t_hn], writes=[bt[b]], inc=(c == NCH - 1))
                S.op("act", lambda h, b=b: h.copy(out=craw[:, :], in_=banks[b][:, :]), reads=[bt[b]], writes=[t_craw])
                b = prot.next()
                for c in range(NCH):
                    S.op("pe", lambda h, b=b, c=c, sv=sv: h.matmul(banks[b][:, :], lhsT=sv[:, c, 128:256], rhs=hn[:, c, :],
                                                                  start=(c == 0), stop=(c == NCH - 1)),
                         reads=[stok, t_hn], writes=[bt[b]], inc=(c == NCH - 1))
                evac_copy(kiA[0:64, tok0:tok0 + 512], banks[b][0:64, :], reads=[bt[b]], writes=[ki_tok[T]])
                evac_copy(kiB[64:128, tok0:tok0 + 512], banks[b][64:128, :], reads=[bt[b]], writes=[ki_tok[T]])
                SK = DBG.get("skip", ())
                for qb in range(4 if "wi" not in SK else 0):
                    for c in range(NCH):
                        S.op("pe", lambda h, qb=qb, c=c, sv=sv: h.matmul(
                            banks[XB][:, qb * 8:(qb + 1) * 8], lhsT=hn[:, c, qb * 128:(qb + 1) * 128],
                            rhs=sv[:, c, 256:264], start=(c == 0), stop=(c == NCH - 1), skip_group_check=True),
                            reads=[stok, t_hn], writes=[bt[XB]], inc=(c == NCH - 1))
                if IDXE == "dve2":
                    S.op("act", lambda h: h.mul(out=wabs[:, :], in_=banks[XB][:, 0:32], mul=IDX_W_SCALE),
                         reads=[bt[XB]], writes=[t_wabs])
                elif "abs" not in SK:
                    S.op("act", lambda h: h.activation(out=wabs[:, :], in_=banks[XB][:, 0:32], func=AF.Abs, scale=IDX_W_SCALE),
                         reads=[bt[XB]], writes=[t_wabs])
                    S.op("act", lambda h: h.activation(out=wsgn[:, :], in_=banks[XB][:, 0:32], func=AF.Sign),
                         reads=[bt[XB]], writes=[t_wsgn])
                w_prefetch()
                k = nb[0].next()
                if "kvn" in SK:
                    S.op("act", lambda h, tok0=tok0: h.copy(out=cTc[:, tok0:tok0 + 512], in_=craw[:, :]), reads=[t_craw], writes=[c_tok[T]])
                S.op("act", lambda h, k=k: h.activation(out=k[0][:, :], in_=craw[:, :], func=AF.Square),
                     reads=[t_craw], writes=[k[1]])
                S.op("pe", lambda h, k=k: h.matmul(banks[XB][:, :], lhsT=onesb[:, :], rhs=k[0][:, :], start=True, stop=True),
                     reads=[k[1], t_ones], writes=[bt[XB]])
                rs, rs_tok = nb[2], nb[3]
                S.op("act", lambda h: h.activation(out=rs[:, :], in_=banks[XB][:, :], func=AF.Sqrt, bias=EPS, scale=1.0 / 128),
                     reads=[bt[XB]], writes=[rs_tok])
                S.op("dve", lambda h: h.reciprocal(out=rs[:, :], in_=rs[:, :]), reads=[rs_tok], writes=[rs_tok])
                S.op("dve", lambda h, tok0=tok0: h.scalar_tensor_tensor(
                    out=cTc[:, tok0:tok0 + 512], in0=craw[:, :], scalar=vecs[:, V_KVG + j:V_KVG + j + 1], op0=ALU.mult,
                    in1=rs[:, :], op1=ALU.mult),
                    reads=[t_craw, rs_tok, t_vecs], writes=[c_tok[T]])
                slot, stok, _ = w_next(sid_att_in(j, 3))
                sv = slot[:, :].rearrange("p (c n) -> p c n", c=8)
                for oc in range(4):
                    b = prot.next()
                    for c in range(NCH):
                        S.op("pe", lambda h, b=b, c=c, oc=oc, sv=sv: h.matmul(
                            banks[b][:, :], lhsT=sv[:, c, oc * 128:(oc + 1) * 128], rhs=hn[:, c, :],
                            start=(c == 0), stop=(c == NCH - 1)),
                            reads=[stok, t_hn], writes=[bt[b]], inc=(c == NCH - 1))
                    evac_copy(qiT[:, oc, :], banks[b][:, :], reads=[bt[b]], writes=[qi_tok[oc]])
                w_prefetch()
                for kl in range(4 if "V" not in SK else 0):
                    kb = T * 4 + kl
                    for half in range(2):
                        b = prot.next()
                        S.op("pe", lambda h, b=b, kb=kb, half=half: h.matmul(
                            banks[b][:, :], lhsT=cTc[:, kb * 128:(kb + 1) * 128],
                            rhs=attw[:, 2048 + half * 512:2048 + (half + 1) * 512], start=True, stop=True),
                            reads=[c_tok[T], t_attw], writes=[bt[b]])
                        evac_copy(V4[:, kb, half * 8:(half + 1) * 8, 0:64],
                                  banks[b][:, :].rearrange("p (h v) -> p h v", v=64), reads=[bt[b]], writes=[v_tok[kb]])
                xb_bf = banks[XB][:, :].bitcast(BF16)

                def gen_index(qb, T=T):
                    G = T * 4 + qb
                    nk = (G + 1) * 128
                    q0 = qb * 128
                    nch = (nk + 511) // 512
                    for hi in range(H_IDX):
                        o, par = hi // 2, hi % 2
                        for kc in range(nch):
                            w = min(512, nk - kc * 512)
                            b = drot.next()
                            S.op("pe", lambda h, b=b, o=o, par=par, q0=q0, kc=kc, w=w: h.matmul(
                                banks[b][:, 0:w], lhsT=qiT[:, o, q0:q0 + 128],
                                rhs=(kiA if par == 0 else kiB)[:, kc * 512:kc * 512 + w], start=True, stop=True),
                                reads=[qi_tok[o], ki_tok[kc]], writes=[bt[b]])
                            r_ = rl_rot.next()
                            col = qb * 8 + hi
                            if IDXE == "dve2":
                                if hi == 0:
                                    S.op("dve", lambda h, b=b, w=w, kc=kc, col=col: h.tensor_scalar(
                                        out=score[:, kc * 512:kc * 512 + w], in0=banks[b][:, 0:w], scalar1=0.0,
                                        scalar2=wabs[:, col:col + 1], op0=ALU.max, op1=ALU.mult),
                                        reads=[bt[b], t_wabs], writes=[sc_tok[kc]])
                                else:
                                    S.op("dve", lambda h, b=b, r_=r_, w=w, col=col: h.tensor_scalar(
                                        out=r_[0][:, 0:w], in0=banks[b][:, 0:w], scalar1=0.0,
                                        scalar2=wabs[:, col:col + 1], op0=ALU.max, op1=ALU.mult),
                                        reads=[bt[b], t_wabs], writes=[r_[1]])
                                    S.op("dve", lambda h, r_=r_, w=w, kc=kc: h.tensor_tensor(
                                        out=score[:, kc * 512:kc * 512 + w], in0=score[:, kc * 512:kc * 512 + w],
                                        in1=r_[0][:, 0:w], op=ALU.add),
                                        reads=[r_[1], sc_tok[kc]], writes=[sc_tok[kc]])
                                yield
                                continue
                            S.op("act", lambda h, b=b, r_=r_, w=w, col=col: h.activation(
                                out=r_[0][:, 0:w], in_=banks[b][:, 0:w], func=AF.Relu, scale=wabs[:, col:col + 1]),
                                reads=[bt[b], t_wabs], writes=[r_[1]])
                            if hi == 0:
                                S.op(IDXE, lambda h, r_=r_, w=w, kc=kc, col=col: h.tensor_scalar(
                                    out=score[:, kc * 512:kc * 512 + w], in0=r_[0][:, 0:w], scalar1=wsgn[:, col:col + 1],
                                    scalar2=None, op0=ALU.mult),
                                    reads=[r_[1], t_wsgn], writes=[sc_tok[kc]])
                            elif IDXE == "dve":
                                S.op("dve", lambda h, r_=r_, w=w, kc=kc, col=col: h.scalar_tensor_tensor(
                                    out=score[:, kc * 512:kc * 512 + w], in0=r_[0][:, 0:w], scalar=wsgn[:, col:col + 1],
                                    op0=ALU.mult, in1=score[:, kc * 512:kc * 512 + w], op1=ALU.add),
                                    reads=[r_[1], t_wsgn, sc_tok[kc]], writes=[sc_tok[kc]])
                            else:
                                S.op("pool", lambda h, r_=r_, w=w, col=col: h.tensor_scalar(
                                    out=r_[0][:, 0:w], in0=r_[0][:, 0:w], scalar1=wsgn[:, col:col + 1],
                                    scalar2=None, op0=ALU.mult),
                                    reads=[r_[1], t_wsgn], writes=[r_[1]])
                                S.op("pool", lambda h, r_=r_, w=w, kc=kc: h.tensor_tensor(
                                    out=score[:, kc * 512:kc * 512 + w], in0=score[:, kc * 512:kc * 512 + w],
                                    in1=r_[0][:, 0:w], op=ALU.add),
                                    reads=[r_[1], sc_tok[kc]], writes=[sc_tok[kc]])
                            yield
                    sct = sc_tok[0:nch]
                    if G >= 2:
                        S.op("dve", lambda h, nk=nk: h.tensor_reduce(out=bis[:, 0:1], in_=score[:, 0:nk], axis=mybir.AxisListType.X,
                                                                     op=ALU.max, apply_absolute_value=True),
                             reads=sct, writes=[t_bis])
                        S.op("dve", lambda h: h.tensor_scalar(out=wtab[:, :], in0=vecs[:, V_POW:V_POW + NIT + 1],
                                                              scalar1=bis[:, 0:1], scalar2=None, op0=ALU.mult),
                             reads=[t_bis, t_vecs], writes=[t_wtab])
                        S.op("dve", lambda h: h.memset(bis[:, 1:2], 0.0), reads=[t_bis], writes=[t_bis])
                    S.op("dve", lambda h, G=G: h.tensor_tensor(out=score[:, G * 128:(G + 1) * 128],
                                                               in0=score[:, G * 128:(G + 1) * 128], in1=cneg[:, :], op=ALU.add),
                         reads=[sc_tok[G // 4], t_cneg], writes=[sc_tok[G // 4]])
                    yield

                def gen_select(qb, T=T):
                    G = T * 4 + qb
                    nk = (G + 1) * 128
                    nch = (nk + 511) // 512
                    sct = sc_tok[0:nch]
                    if G >= 2:
                        for it in range(NIT):
                            S.op("act", lambda h, nk=nk: h.activation(out=mask[:, 0:nk], in_=score[:, 0:nk], func=AF.Sign,
                                                                      bias=bis[:, 1:2], scale=1.0, accum_out=bis[:, 2:3]),
                                 reads=sct + [t_bis], writes=[t_mask, t_bis])
                            yield
                            S.op("dve", lambda h, nk=nk, it=it: h.scalar_tensor_tensor(
                                out=bis[:, 3:4], in0=bis[:, 2:3], scalar=511.5 - float(nk), op0=ALU.is_gt,
                                in1=wtab[:, it:it + 1], op1=ALU.mult),
                                reads=[t_bis, t_wtab], writes=[t_bis2])
                            S.op("dve", lambda h, it=it: h.scalar_tensor_tensor(
                                out=bis[:, 1:2], in0=bis[:, 3:4], scalar=wtab[:, it + 1:it + 2], op0=ALU.subtract,
                                in1=bis[:, 1:2], op1=ALU.add),
                                reads=[t_bis, t_bis2, t_wtab], writes=[t_bis])
                            yield
                        S.op("dve", lambda h: h.tensor_scalar(out=bis[:, 4:5], in0=bis[:, 1:2], scalar1=wtab[:, NIT:NIT + 1],
                                                              scalar2=-1.0, op0=ALU.subtract, op1=ALU.mult),
                             reads=[t_bis, t_wtab], writes=[t_bis])
                        S.op("dve", lambda h, nk=nk: h.tensor_scalar(out=mask[:, 0:nk], in0=score[:, 0:nk], scalar1=bis[:, 4:5],
                                                                     scalar2=None, op0=ALU.is_gt),
                             reads=sct + [t_bis], writes=[t_mask])
                    else:
                        S.op("dve", lambda h, nk=nk: h.tensor_scalar(out=mask[:, 0:nk], in0=score[:, 0:nk], scalar1=-1.0e29,
                                                                     scalar2=None, op0=ALU.is_gt),
                             reads=sct, writes=[t_mask])
                    yield

                def gen_attn(qb, T=T):
                    G = T * 4 + qb
                    q0 = qb * 128
                    for k0 in range(0, G + 1, 4):
                        n4 = min(4, G + 1 - k0)
                        for i in range(n4):
                            kb = k0 + i
                            S.op("pe", lambda h, i=i, kb=kb: h.transpose(xb_bf[:, i * 128:(i + 1) * 128],
                                                                        mask[:, kb * 128:(kb + 1) * 128], identb[:, :]),
                                 reads=[t_mask, t_identb], writes=[bt[XB]], inc=(i == n4 - 1))
                        evac_copy(MT[:, k0:k0 + n4, :], xb_bf[:, 0:n4 * 128].rearrange("p (a q) -> p a q", a=n4),
                                  reads=[bt[XB]], writes=[mt_tok[k0 // 4]])
                    yield
                    for hg in range(4):
                        for hl in range(4):
                            hh = hg * 4 + hl
                            o = hh // 2
                            S.op("pe", lambda h, hl=hl, o=o, hh=hh, q0=q0: h.matmul(
                                banks[XB][:, hl * 128:(hl + 1) * 128],
                                lhsT=attw[:, hh * 128:(hh + 1) * 128],
                                rhs=qT[:, o, q0:q0 + 128], start=True, stop=True, skip_group_check=True),
                                reads=[t_attw, q_tok[o]], writes=[bt[XB]], inc=(hl == 3))
                        S.op("act", lambda h, hg=hg: h.mul(out=qlat[:, hg * 4:(hg + 1) * 4, :],
                                                           in_=banks[XB][:, :].rearrange("p (a q) -> p a q", a=4), mul=ATT_SCALE),
                             reads=[bt[XB]], writes=[ql_tok[hg]])
                    yield
                    started = [False, False, False]
                    tiles = [(hg, kb) for hg in range(4) for kb in range(G + 1)]
                    pend = []
                    SKEW = 2

                    def issue_pv(hg, kb, m_):
                        for hl in range(4):
                            hh = hg * 4 + hl
                            ob = hh // 6
                            col = (hh % 6) * 65
                            st_flag = not started[ob]
                            started[ob] = True
                            S.op("pe", lambda h, ob=ob, col=col, m_=m_, hl=hl, kb=kb, hh=hh, st_flag=st_flag: h.matmul(
                                banks[OB[ob]][:, col:col + 65], lhsT=m_[0][:, hl * 128:(hl + 1) * 128],
                                rhs=V[:, kb, hh * 65:(hh + 1) * 65], start=st_flag, stop=False, skip_group_check=True),
                                reads=[m_[1], v_tok[kb]], writes=[bt[OB[ob]]], inc=(hl == 3))

                    for (hg, kb) in tiles:
                        b = lrot3.next()
                        S.op("pe", lambda h, b=b, kb=kb, hg=hg: h.matmul(
                            banks[b][:, :], lhsT=cTc[:, kb * 128:(kb + 1) * 128], rhs=qlat[:, hg * 4:(hg + 1) * 4, :],
                            start=True, stop=True),
                            reads=[c_tok[kb // 4], ql_tok[hg]], writes=[bt[b]])
                        e_ = e_rot.next()
                        off = G - kb
                        if off < 2:
                            S.op("dve", lambda h, b=b, off=off, hg=hg: h.tensor_tensor(
                                out=banks[b][:, :].rearrange("p (a q) -> p a q", a=4),
                                in0=banks[b][:, :].rearrange("p (a q) -> p a q", a=4),
                                in1=bt4[:, off, hg * 4:(hg + 1) * 4, :], op=ALU.add),
                                reads=[bt[b], t_btab], writes=[bt[b]])
                        S.op("act", lambda h, b=b, e_=e_: h.activation(out=e_[0][:, :], in_=banks[b][:, :], func=AF.Exp),
                             reads=[bt[b]], writes=[e_[1]])
                        m_ = em_rot.next()
                        S.op("dve", lambda h, e_=e_, m_=m_, kb=kb: h.tensor_tensor(
                            out=m_[0][:, :].rearrange("p (a q) -> p a q", a=4),
                            in0=e_[0][:, :].rearrange("p (a q) -> p a q", a=4),
                            in1=MT[:, kb:kb + 1, :].to_broadcast([128, 4, 128]), op=ALU.mult),
                            reads=[e_[1], mt_tok[kb // 4]], writes=[m_[1]])
                        pend.append((hg, kb, m_))
                        if len(pend) > SKEW:
                            issue_pv(*pend.pop(0))
                        yield
                    while pend:
                        issue_pv(*pend.pop(0))
                    yield
                    for ob in range(3):
                        nh = 6 if ob < 2 else 4
                        o3 = banks[OB[ob]][:, 0:nh * 65].rearrange("p (h e) -> p h e", e=65)
                        S.op("dve", lambda h, ob=ob, nh=nh, o3=o3: h.reciprocal(
                            out=recip[:, ob * 6:ob * 6 + nh].unsqueeze(2), in_=o3[:, :, 64:65]),
                            reads=[bt[OB[ob]]], writes=[t_recip])
                        S.op("dve", lambda h, ob=ob, nh=nh, o3=o3: h.tensor_tensor(
                            out=on[:, ob * 384:ob * 384 + nh * 64].rearrange("p (h v) -> p h v", v=64),
                            in0=o3[:, :, 0:64],
                            in1=recip[:, ob * 6:ob * 6 + nh].unsqueeze(2).to_broadcast([128, nh, 64]), op=ALU.mult),
                            reads=[bt[OB[ob]], t_recip], writes=[t_on])
                    yield
                    for half in range(2):
                        for i in range(4):
                            o = half * 4 + i
                            S.op("pe", lambda h, i=i, o=o: h.transpose(xb_bf[:, i * 128:(i + 1) * 128],
                                                                      on[:, o * 128:(o + 1) * 128], identb[:, :]),
                                 reads=[t_on, t_identb], writes=[bt[XB]], inc=(i == 3))
                        evac_copy(hn[:, half * 4:half * 4 + 4, q0:q0 + 128],
                                  xb_bf[:, 0:512].rearrange("p (a q) -> p a q", a=4), reads=[bt[XB]], writes=[t_hn])
                    yield

                def chain(*gens):
                    for g in gens:
                        yield from g

                def interleave(ga, gb, ra=1, rb=1):
                    done_a = done_b = False
                    while not (done_a and done_b):
                        for _ in range(ra):
                            if not done_a:
                                try:
                                    next(ga)
                                except StopIteration:
                                    done_a = True
                        for _ in range(rb):
                            if not done_b:
                                try:
                                    next(gb)
                                except StopIteration:
                                    done_b = True

                for _ in chain(gen_index(0), gen_select(0)):
                    pass
                for qb in range(4):
                    if qb < 3 and DBG.get("pipeline", True):
                        interleave(gen_attn(qb), chain(gen_index(qb + 1), gen_select(qb + 1)), 1, 1)
                    else:
                        for _ in gen_attn(qb):
                            pass
                        if qb < 3:
                            for _ in chain(gen_index(qb + 1), gen_select(qb + 1)):
                                pass
                for g in range(2):
                    slot, stok, _ = w_next(sid_att_o(j, g))
                    sv = slot[:, :].rearrange("p (c n) -> p c n", c=8)
                    for dc in range(4):
                        d = g * 4 + dc
                        b = prot.next()
                        for c in range(NCH):
                            S.op("pe", lambda h, b=b, c=c, dc=dc, sv=sv: h.matmul(
                                banks[b][:, :], lhsT=sv[:, c, dc * 128:(dc + 1) * 128], rhs=hn[:, c, :],
                                start=(c == 0), stop=(c == NCH - 1)),
                                reads=[stok, t_hn], writes=[bt[b]], inc=(c == NCH - 1))
                        S.op("dve", lambda h, b=b, d=d, T=T: h.tensor_tensor(
                            out=xT[:, d, T * 512:(T + 1) * 512], in0=xT[:, d, T * 512:(T + 1) * 512],
                            in1=banks[b][:, :], op=ALU.add),
                            reads=[bt[b], xtok[d][T]], writes=[xtok[d][T]])
                    w_prefetch()

        def phase_final():
            am = arena()
            nb = norm_bufs(am)
            yf = am.alloc("yf", [128, NCH, 512], F32); t_yf = Tok()
            ost = [(am.alloc("ost", [128, D], F32), Tok()) for _ in range(2)]
            brot = Rot([0, 1, 2, 3, 4, 5])
            n_out = 0
            for T in range(4):
                norm_tile(nb, T * 512, 512, VG_FIN, yf, t_yf, 7)
                for tb in range(4):
                    og, ok = ost[n_out % 2]
                    n_out += 1
                    for half in range(2):
                        b = brot.next()
                        for i in range(4):
                            c = half * 4 + i
                            S.op("pe", lambda h, b=b, i=i, c=c, tb=tb: h.transpose(
                                banks[b][:, i * 128:(i + 1) * 128], yf[:, c, tb * 128:(tb + 1) * 128], ident[:, :]),
                                reads=[t_yf, t_ident], writes=[bt[b]], inc=(i == 3))
                        evac_copy(og[:, half * 512:(half + 1) * 512], banks[b][:, :], reads=[bt[b]], writes=[ok])
                    r0 = T * 512 + tb * 128
                    S.dma("sp", lambda h, og=og, r0=r0: h.dma_start(out=out_d[r0:r0 + 128, :], in_=og[:, :]), reads=[ok])

        phase_load()
        for layer in range(n_layers):
            if layer % 2 == 0:
                phase_rglru(layer)
            elif not DBG.get("skip_dsa"):
                phase_dsa(layer)
            if not DBG.get("skip_mlp"):
                for _rep in range(DBG.get("mlp_rep", 1)):
                    phase_mlp(layer)
        phase_final()
        S.emit(S.all_events(engines=(), queues=("sp",)))
    return nc


def _fm(v):
    v = np.asarray(v, np.float32)
    lead = v.shape[:-1]
    return v.reshape(lead + (8, 128))


def _slot_kn(W, ncol=512):
    K, N = W.shape
    G = N // ncol
    return np.ascontiguousarray(W.reshape(8, 128, G, ncol).transpose(2, 1, 0, 3)).reshape(G, 128, 8 * ncol)


def prepare_shared(inp):
    f = lambda k: np.asarray(inp[k], np.float32)
    vec = np.zeros((128, NV), np.float32)
    vec[:, VG_MIX:VG_MIX + 32] = f("norm_mix_g").reshape(4, 8, 128).transpose(2, 0, 1).reshape(128, 32)
    vec[:, VG_MLP:VG_MLP + 32] = f("norm_mlp_g").reshape(4, 8, 128).transpose(2, 0, 1).reshape(128, 32)
    vec[:, VG_FIN:VG_FIN + 8] = f("final_norm_g").reshape(8, 128).T
    vec[:, V_CONVW:V_CONVW + 64] = f("rec_conv_w").reshape(2, 4, 8, 128).transpose(3, 0, 1, 2).reshape(128, 64)
    for key, col in (("rec_conv_b", V_CONVB), ("rec_b_a", V_BA), ("rec_b_x", V_BX), ("rec_lambda", V_LAM)):
        vec[:, col:col + 16] = f(key).reshape(2, 8, 128).transpose(2, 0, 1).reshape(128, 16)
    vec[:, V_KVG:V_KVG + 2] = f("att_kv_norm_g").T
    vec[:, V_POW:V_POW + NIT + 1] = -(2.0 ** -np.arange(NIT + 1, dtype=np.float64)).astype(np.float32)[None, :]
    ws = np.zeros((NSLOT, 128, 4096), np.float32)
    up, dn = f("mlp_w_up"), f("mlp_w_down")
    for l in range(4):
        ws[sid_mlp_up(l, 0):sid_mlp_up(l, 0) + 8] = _slot_kn(up[l])
        ws[sid_mlp_dn(l, 0):sid_mlp_dn(l, 0) + 8] = dn[l].reshape(32, 128, 8, 128).transpose(2, 1, 0, 3).reshape(8, 128, 4096)
    rin, rout, wa, wx = f("rec_w_in"), f("rec_w_out"), f("rec_w_a"), f("rec_w_x")
    for j in range(2):
        ws[sid_rec_in(j, 0):sid_rec_in(j, 0) + 4] = _slot_kn(rin[j])
        ga = wa[j].reshape(4, 2, 128, 256).transpose(2, 0, 1, 3).reshape(128, 2048)
        gx = wx[j].reshape(4, 2, 128, 256).transpose(2, 0, 1, 3).reshape(128, 2048)
        ws[sid_rec_gate(j)] = np.concatenate([ga, gx], axis=1)
        ws[sid_rec_out(j, 0):sid_rec_out(j, 0) + 2] = _slot_kn(rout[j])
    ain, ao = f("att_w_in"), f("att_w_o")
    for j in range(2):
        W = ain[j]
        ws[sid_att_in(j, 0):sid_att_in(j, 0) + 2] = _slot_kn(W[:, 0:1024])
        Wc = np.concatenate([W[:, 1024:1152], W[:, 1664:1728], W[:, 1664:1728], W[:, 1728:1736],
                             np.zeros((1024, 248), np.float32)], axis=1)
        ws[sid_att_in(j, 2)] = _slot_kn(Wc)[0]
        ws[sid_att_in(j, 3)] = _slot_kn(W[:, 1152:1664])[0]
        ws[sid_att_o(j, 0):sid_att_o(j, 0) + 2] = _slot_kn(ao[j])
    wuk, wuv = f("att_w_uk"), f("att_w_uv")
    attw = np.zeros((2, 128, 3072), np.float32)
    for j in range(2):
        for hh in range(16):
            par = hh % 2
            attw[j, par * 64:(par + 1) * 64, hh * 128:(hh + 1) * 128] = wuk[j, hh]
        attw[j, :, 2048:3072] = wuv[j].transpose(1, 0, 2).reshape(128, 1024)
    rb = f("rel_bias")
    kk = np.arange(128)[:, None, None]
    off = np.arange(2)[None, :, None]
    qq = np.arange(128)[None, None, :]
    dist = off * 128 + qq - kk
    n = np.maximum(dist, 0)
    nf = np.maximum(n, 1).astype(np.float32)
    large = 16 + (np.log(nf / 16) / math.log(128 / 16) * 16).astype(np.int32)
    large = np.minimum(large, 31)
    bucket = np.where(n < 16, n, large)
    btab = rb[bucket]
    btab = np.ascontiguousarray(btab.transpose(0, 1, 3, 2)).reshape(128, 4096)
    farb = np.ascontiguousarray(np.broadcast_to(rb[31][None, :], (128, 16)))
    return dict(vecs=vec, wslots=ws, attw=attw, btab=btab, farb=farb)


_CACHE = {}


def kernel(**inputs):
    x = np.asarray(inputs["x"], np.float32)
    shared = prepare_shared(inputs)
    n_layers = int(_CACHE.get("n_layers", DEPTH))
    nc = build_program(n_layers)
    in_maps = []
    for b in range(8):
        m = dict(shared)
        m["x"] = np.ascontiguousarray(x[b])
        in_maps.append(m)
    res = run_bass_kernel_spmd(nc, in_maps, core_ids=list(range(8)))
    out = np.stack([np.asarray(r["out"], np.float32) for r in res.results], axis=0)
    return out
```

```python
import math
from contextlib import ExitStack
import numpy as np
import concourse.bass as bass
import concourse.mybir as mybir
from concourse.bass_utils import run_bass_kernel_spmd

F32 = mybir.dt.float32
BF16 = mybir.dt.bfloat16
AF = mybir.ActivationFunctionType
ALU = mybir.AluOpType

S_LEN = 2048
D = 1024
NCH = 8
DEPTH = 4
EPS = 1e-6
RG_C = 8.0
H_ATT = 16
H_IDX = 8
ATT_SCALE = 64 ** -0.5
IDX_W_SCALE = (8 ** -0.5) * (64 ** -0.5)
NEG_MASK = -1.0e30
NEG_REPL = -3.0e38

VG_MIX = 0
VG_MLP = 32
VG_FIN = 64
V_CONVW = 72
V_CONVB = 136
V_BA = 152
V_BX = 168
V_LAM = 184
V_KVG = 200
V_POW = 202
NIT = 18
NV = 202 + NIT + 1

def sid_mlp_up(l, g): return l * 16 + g
def sid_mlp_dn(l, d): return l * 16 + 8 + d
def sid_rec_in(j, g): return 64 + j * 7 + g
def sid_rec_gate(j): return 64 + j * 7 + 4
def sid_rec_out(j, g): return 64 + j * 7 + 5 + g
def sid_att_in(j, g): return 78 + j * 6 + g
def sid_att_o(j, g): return 78 + j * 6 + 4 + g
NSLOT = 90


class Tok:
    __slots__ = ("w", "r")

    def __init__(self):
        self.w = None
        self.r = []


class _Eng:
    def __init__(self, name, sem):
        self.name = name
        self.sem = sem
        self.count = 0
        self.instrs = []
        self.seen = {}
        self.pend_r = []
        self.pend_w = []
        self.bar = []


class Sched:
    NDMA = 8

    def __init__(self, nc, stack):
        self.nc = nc
        self.sems = {}
        self.eng = {}
        for name in ("pe", "act", "dve", "pool", "sp"):
            self.sems["s_" + name] = stack.enter_context(nc.semaphore("s_" + name))
            self.eng[name] = _Eng(name, "s_" + name)
        self.dma_sems = {}
        self.dma_n = {}
        for q in ("sp", "pool"):
            lst = []
            for i in range(self.NDMA):
                key = "d_%s%d" % (q, i)
                self.sems[key] = stack.enter_context(nc.semaphore(key))
                lst.append(key)
            self.dma_sems[q] = lst
            self.dma_n[q] = 0

    def _deps(self, E, reads, writes):
        need = {}

        def add(ev):
            if ev is None:
                return
            k, v = ev
            if need.get(k, 0) < v:
                need[k] = v
        for t in reads:
            add(t.w)
        for t in writes:
            add(t.w)
            for ev in t.r:
                add(ev)
        for ev in E.bar:
            add(ev)
        E.bar = []
        waits = []
        for k, v in need.items():
            if E.seen.get(k, 0) < v:
                E.seen[k] = v
                waits.append((k, v))
        return waits

    def op(self, eng, fn, reads=(), writes=(), inc=True):
        E = self.eng[eng]
        waits = self._deps(E, reads, writes)
        if inc:
            E.count += 1
            ev = (E.sem, E.count)
            E.instrs.append((waits, fn, E.sem, 1))
            for t in list(reads) + E.pend_r:
                t.r.append(ev)
            for t in list(writes) + E.pend_w:
                t.w = ev
                t.r = []
            E.pend_r = []
            E.pend_w = []
            return ev
        E.instrs.append((waits, fn, None, 0))
        E.pend_r.extend(reads)
        E.pend_w.extend(writes)
        return None

    def dma(self, q, fn, reads=(), writes=()):
        E = self.eng[q]
        n = self.dma_n[q]
        self.dma_n[q] = n + 1
        key = self.dma_sems[q][n % self.NDMA]
        val = 16 * (n // self.NDMA + 1)
        waits = self._deps(E, reads, writes)
        if val > 16 and E.seen.get(key, 0) < val - 16:
            E.seen[key] = val - 16
            waits.append((key, val - 16))
        E.instrs.append((waits, fn, key, 16))
        ev = (key, val)
        for t in reads:
            t.r.append(ev)
        for t in writes:
            t.w = ev
            t.r = []
        return ev

    def all_events(self, engines=("pe", "act", "dve", "pool", "sp"), queues=("sp", "pool")):
        evs = []
        for e in engines:
            E = self.eng[e]
            if E.count > 0:
                evs.append((E.sem, E.count))
        for q in queues:
            n = self.dma_n[q]
            for i in range(min(n, self.NDMA)):
                cnt = (n - 1 - i) // self.NDMA + 1
                evs.append((self.dma_sems[q][i], 16 * cnt))
        return evs

    def barrier(self, engines=("pe", "act", "dve", "sp")):
        evs = self.all_events(engines=engines, queues=("sp",))
        for e in engines:
            self.eng[e].bar.extend(evs)

    def emit(self, final_events=()):
        nc = self.nc
        sems = self.sems
        fw = [ev for ev in final_events if ev is not None]
        with nc.Block() as block:
            def run(E, h, tail=()):
                for waits, fn, isem, ival in E.instrs:
                    for k, v in waits:
                        h.wait_ge(sems[k], v)
                    ins = fn(h)
                    if isem is not None:
                        ins.then_inc(sems[isem], ival)
                for k, v in tail:
                    h.wait_ge(sems[k], v)

            @block.sync
            def _(h):
                run(self.eng["sp"], h, fw)

            @block.tensor
            def _(h):
                run(self.eng["pe"], h)

            @block.scalar
            def _(h):
                run(self.eng["act"], h)

            @block.vector
            def _(h):
                run(self.eng["dve"], h)

            @block.gpsimd
            def _(h):
                run(self.eng["pool"], h)


class Mem:
    def __init__(self, nc, base, top):
        self.nc = nc
        self.cur = (base + 31) // 32 * 32
        self.top = top
        self.n = 0

    def alloc(self, name, shape, dtype):
        nbytes = int(np.prod(shape[1:])) * (2 if dtype == BF16 else 4)
        nbytes = (nbytes + 31) // 32 * 32
        assert self.cur + nbytes <= self.top, "SBUF overflow %s: need %d have %d" % (name, nbytes, self.top - self.cur)
        self.n += 1
        t = self.nc.alloc_sbuf_tensor_at("%s_%d" % (name, self.n), list(shape), dtype, offset=self.cur)
        self.cur += nbytes
        return t


DBG = {}


def weight_plan(n_layers):
    plan = []
    for layer in range(n_layers):
        j = layer // 2
        if layer % 2 == 0:
            plan += [sid_rec_in(j, g) for g in range(4)]
            for T in range(4):
                if T < 3:
                    plan += [sid_rec_in(j, g) for g in range(4)]
                plan += [sid_rec_out(j, g) for g in range(2)]
        elif not DBG.get("skip_dsa"):
            for T in range(4):
                plan += [sid_att_in(j, g) for g in range(4)]
                plan += [sid_att_o(j, g) for g in range(2)]
        for T in range(2):
            if DBG.get("skip_mlp"):
                continue
            for _rep in range(DBG.get("mlp_rep", 1)):
                plan += [sid_mlp_up(layer, g) for g in range(8)]
                plan += [sid_mlp_dn(layer, d) for d in range(8)]
    return plan


def build_program(n_layers=DEPTH):
    nc = bass.Bass("TRN2", target_bir_lowering=False)
    x_d = nc.dram_tensor("x", [S_LEN, D], F32, kind="ExternalInput").ap()
    vec_d = nc.dram_tensor("vecs", [128, NV], F32, kind="ExternalInput").ap()
    w_d = nc.dram_tensor("wslots", [NSLOT, 128, 4096], F32, kind="ExternalInput").ap()
    attw_d = nc.dram_tensor("attw", [2, 128, 3072], F32, kind="ExternalInput").ap()
    btab_d = nc.dram_tensor("btab", [128, 4096], F32, kind="ExternalInput").ap()
    far_d = nc.dram_tensor("farb", [128, 16], F32, kind="ExternalInput").ap()
    out_d = nc.dram_tensor("out", [S_LEN, D], F32, kind="ExternalOutput").ap()

    with ExitStack() as st:
        S = Sched(nc, st)
        mem = Mem(nc, nc.sbuf_base, nc.sbuf_top)
        banks = [st.enter_context(nc.psum_tensor("bank%d" % i, [128, 512], F32)) for i in range(8)]
        bt = [Tok() for _ in range(8)]

        xT = mem.alloc("xT", [128, NCH, S_LEN], F32)
        xtok = [[Tok() for _ in range(4)] for _ in range(NCH)]
        ring = [mem.alloc("ring%d" % i, [128, 4096], BF16) for i in range(3)]
        rtok = [Tok() for _ in range(3)]
        vecs = mem.alloc("vecs", [128, NV], F32); t_vecs = Tok()
        ident = mem.alloc("ident", [128, 128], F32); t_ident = Tok()
        identb = mem.alloc("identb", [128, 128], BF16); t_identb = Tok()
        onesb = mem.alloc("onesb", [128, 128], BF16); t_ones = Tok()
        clam = mem.alloc("clam", [128, 16], F32); t_clam = Tok()
        cneg = mem.alloc("cneg", [128, 128], F32); t_cneg = Tok()
        arena_base = mem.cur

        plan = weight_plan(n_layers)
        wstate = {"i": 0, "issued": 0}

        def w_issue_upto(n):
            while wstate["issued"] < min(n, len(plan)):
                i = wstate["issued"]
                sid = plan[i]
                r = ring[i % 3]
                S.dma("pool", lambda h, r=r, sid=sid: h.dma_start(out=r[:, :], in_=w_d[sid]),
                      writes=[rtok[i % 3]])
                wstate["issued"] += 1

        def w_next(sid):
            i = wstate["i"]
            assert plan[i] == sid, (i, plan[i], sid)
            w_issue_upto(i + 1)
            wstate["i"] = i + 1
            return ring[i % 3], rtok[i % 3], i

        def w_prefetch():
            w_issue_upto(wstate["i"] + 2)

        S.dma("sp", lambda h: h.dma_start(out=vecs[:, :], in_=vec_d), writes=[t_vecs])
        S.op("pool", lambda h: h.memset(ident[:, :], 0.0), writes=[t_ident])
        S.op("pool", lambda h: h.affine_select(out=ident[:, :], in_=ident[:, :], pattern=[[-1, 128]],
                                                compare_op=ALU.not_equal, fill=1.0, base=0, channel_multiplier=1),
             reads=[t_ident], writes=[t_ident])
        S.op("pool", lambda h: h.tensor_copy(out=identb[:, :], in_=ident[:, :]), reads=[t_ident], writes=[t_identb])
        S.op("pool", lambda h: h.memset(onesb[:, :], 1.0), writes=[t_ones])
        S.op("pool", lambda h: h.memset(cneg[:, :], 0.0), writes=[t_cneg])
        S.op("pool", lambda h: h.affine_select(out=cneg[:, :], in_=cneg[:, :], pattern=[[-1, 128]],
                                                compare_op=ALU.is_ge, fill=NEG_MASK, base=0, channel_multiplier=1),
             reads=[t_cneg], writes=[t_cneg])
        S.op("act", lambda h: h.activation(out=clam[:, :], in_=vecs[:, V_LAM:V_LAM + 16], func=AF.Exp, scale=-1.0),
             reads=[t_vecs], writes=[t_clam])
        S.op("act", lambda h: h.activation(out=clam[:, :], in_=clam[:, :], func=AF.Ln, bias=1.0, scale=1.0),
             reads=[t_clam], writes=[t_clam])
        S.op("dve", lambda h: h.tensor_scalar(out=clam[:, :], in0=clam[:, :], scalar1=-RG_C, scalar2=None, op0=ALU.mult),
             reads=[t_clam], writes=[t_clam])
        w_issue_upto(2)

        class Rot:
            def __init__(self, items):
                self.items = items
                self.i = 0

            def next(self):
                it = self.items[self.i % len(self.items)]
                self.i += 1
                return it

        def arena(with_pool=False, no_barrier=False):
            if no_barrier:
                return Mem(nc, arena_base, nc.sbuf_top)
            S.barrier()
            if with_pool:
                S.eng["pool"].bar.extend(S.all_events(engines=("pe", "act", "dve", "sp"), queues=("sp",)))
            return Mem(nc, arena_base, nc.sbuf_top)

        evac = {"i": 0}

        def evac_copy(out_ap, in_ap, reads, writes, eng=None):
            if eng is None:
                eng = "act" if evac["i"] % 2 == 0 else "dve"
                evac["i"] += 1
            if eng == "act":
                return S.op("act", lambda h: h.copy(out=out_ap, in_=in_ap), reads=reads, writes=writes)
            return S.op("dve", lambda h: h.tensor_copy(out=out_ap, in_=in_ap), reads=reads, writes=writes)

        def phase_load():
            am = Mem(nc, nc.sbuf_top - 2 * 4096 - 64, nc.sbuf_top)
            stg = [am.alloc("stg", [128, D], F32) for _ in range(2)]
            stk = [Tok() for _ in range(2)]
            brot = Rot(list(range(8)))
            for t in range(16):
                sg, sk = stg[t % 2], stk[t % 2]
                S.dma("sp", lambda h, sg=sg, t=t: h.dma_start(out=sg[:, :], in_=x_d[t * 128:(t + 1) * 128, :]),
                      writes=[sk])
                for half in range(2):
                    b = brot.next()
                    for i in range(4):
                        c = half * 4 + i
                        S.op("pe", lambda h, b=b, i=i, c=c, sg=sg: h.transpose(banks[b][:, i * 128:(i + 1) * 128],
                                                                              sg[:, c * 128:(c + 1) * 128], ident[:, :]),
                             reads=[sk, t_ident], writes=[bt[b]], inc=(i == 3))
                    o_ap = xT[:, half * 4:half * 4 + 4, t * 128:(t + 1) * 128]
                    i_ap = banks[b][:, 0:512].rearrange("p (a b) -> p a b", a=4)
                    evac_copy(o_ap, i_ap, reads=[bt[b]], writes=[xtok[c][t // 4] for c in range(half * 4, half * 4 + 4)])

        def norm_tile(am_bufs, tok0, ntok, gcol, hn, hn_tok, ssbank, out_f32=False):
            sq_rot, sq_tok, rs, rs_tok = am_bufs
            for s0 in range(0, ntok, 512):
                t4 = (tok0 + s0) // 512
                for c in range(NCH):
                    k = sq_rot.next()
                    S.op("act", lambda h, k=k, c=c, s0=s0: h.activation(out=k[0][:, :], in_=xT[:, c, tok0 + s0:tok0 + s0 + 512],
                                                                         func=AF.Square),
                         reads=[xtok[c][t4]], writes=[k[1]])
                    S.op("pe", lambda h, k=k, c=c: h.matmul(banks[ssbank][:, :], lhsT=onesb[:, :], rhs=k[0][:, :],
                                                           start=(c == 0), stop=(c == NCH - 1)),
                         reads=[k[1], t_ones], writes=[bt[ssbank]], inc=True)
                S.op("act", lambda h: h.activation(out=rs[:, :], in_=banks[ssbank][:, :], func=AF.Sqrt,
                                                   bias=EPS, scale=1.0 / D),
                     reads=[bt[ssbank]], writes=[rs_tok])
                S.op("dve", lambda h: h.reciprocal(out=rs[:, :], in_=rs[:, :]), reads=[rs_tok], writes=[rs_tok])
                for c in range(NCH):
                    S.op("dve", lambda h, c=c, s0=s0: h.scalar_tensor_tensor(
                        out=hn[:, c, s0:s0 + 512], in0=xT[:, c, tok0 + s0:tok0 + s0 + 512],
                        scalar=vecs[:, gcol + c:gcol + c + 1], op0=ALU.mult, in1=rs[:, :], op1=ALU.mult),
                        reads=[xtok[c][t4], rs_tok, t_vecs], writes=[hn_tok])

        def norm_bufs(am):
            sq = [(am.alloc("sq", [128, 512], BF16), Tok()) for _ in range(2)]
            rs = am.alloc("rs", [128, 512], F32)
            return (Rot(sq), None, rs, Tok())

        def phase_mlp(layer):
            am = arena()
            nb = norm_bufs(am)
            hn = am.alloc("hn", [128, NCH, 1024], BF16); t_hn = Tok()
            aT = am.alloc("aT", [128, 32, 1024], BF16)
            a_tok = [[Tok() for _ in range(2)] for _ in range(32)]
            rl = [(am.alloc("rl", [128, 512], BF16), Tok()) for _ in range(3)]
            rl_rot = Rot(rl)
            up_rot = Rot([0, 1, 2])
            dn_rot = Rot([3, 4, 5])
            ssbank = 7
            for T in range(2):
                tok0 = T * 1024
                norm_tile(nb, tok0, 1024, VG_MLP + layer * 8, hn, t_hn, ssbank)
                for g in range(8):
                    slot, stok, _ = w_next(sid_mlp_up(layer, g))
                    sv = slot[:, :].rearrange("p (c n) -> p c n", c=8)
                    for fc in range(4):
                        f = g * 4 + fc
                        for half in range(2):
                            b = up_rot.next()
                            for c in range(NCH):
                                S.op("pe", lambda h, b=b, c=c, fc=fc, half=half, sv=sv: h.matmul(
                                    banks[b][:, :], lhsT=sv[:, c, fc * 128:(fc + 1) * 128],
                                    rhs=hn[:, c, half * 512:(half + 1) * 512], start=(c == 0), stop=(c == NCH - 1)),
                                    reads=[stok, t_hn], writes=[bt[b]], inc=(c == NCH - 1))
                            k = rl_rot.next()
                            S.op("act", lambda h, b=b, k=k: h.activation(out=k[0][:, :], in_=banks[b][:, :], func=AF.Relu),
                                 reads=[bt[b]], writes=[k[1]])
                            S.op("dve", lambda h, b=b, k=k, f=f, half=half: h.tensor_tensor(
                                out=aT[:, f, half * 512:(half + 1) * 512], in0=k[0][:, :], in1=banks[b][:, :], op=ALU.mult),
                                reads=[k[1], bt[b]], writes=[a_tok[f][half]])
                    w_prefetch()
                for d in range(8):
                    slot, stok, _ = w_next(sid_mlp_dn(layer, d))
                    sv = slot[:, :].rearrange("p (f n) -> p f n", f=32)
                    for half in range(2):
                        b = dn_rot.next()
                        for f in range(32):
                            S.op("pe", lambda h, b=b, f=f, half=half, sv=sv: h.matmul(
                                banks[b][:, :], lhsT=sv[:, f, :], rhs=aT[:, f, half * 512:(half + 1) * 512],
                                start=(f == 0), stop=(f == 31)),
                                reads=[stok, a_tok[f][half]], writes=[bt[b]], inc=(f == 31))
                        t4 = T * 2 + half
                        S.op("dve", lambda h, b=b, d=d, t4=t4: h.tensor_tensor(
                            out=xT[:, d, t4 * 512:(t4 + 1) * 512], in0=xT[:, d, t4 * 512:(t4 + 1) * 512],
                            in1=banks[b][:, :], op=ALU.add),
                            reads=[bt[b], xtok[d][t4]], writes=[xtok[d][t4]])
                    w_prefetch()

        def phase_rglru(layer):
            j = layer // 2
            am = arena(with_pool=True, no_barrier=(layer == 0))
            am.top = nc.sbuf_top - 2 * 4096 - 64 if layer == 0 else am.top
            nb = norm_bufs(am)
            hn = am.alloc("hn", [128, NCH, 512], BF16); t_hn = Tok()
            gates = [am.alloc("gate", [128, NCH, 512], BF16) for _ in range(2)]
            g_toks = [[Tok() for _ in range(NCH)] for _ in range(2)]
            xr = am.alloc("xr", [128, NCH, 515], F32); xr_tok = [Tok() for _ in range(NCH)]
            yT = am.alloc("yT", [128, NCH, 512], BF16); y_tok = [Tok() for _ in range(4)]
            hst = am.alloc("hst", [128, NCH], F32); hs_tok = [Tok() for _ in range(NCH)]
            gw = am.alloc("gw", [128, 4096], BF16); t_gw = Tok()
            sets = []
            for _ in range(2):
                sets.append(dict(
                    xc=am.alloc("xc", [128, 2, 512], F32), t_xc=Tok(),
                    xcb=am.alloc("xcb", [128, 2, 512], BF16), t_xcb=Tok(),
                    r=am.alloc("r", [128, 2, 512], F32), t_r=Tok(),
                    i=am.alloc("i", [128, 2, 512], F32), t_i=Tok(),
                    m=am.alloc("m", [128, 2, 512], F32), t_m=Tok(),
                    hs=am.alloc("hs", [128, 2, 512], F32), t_hs=Tok()))
            S.dma("pool", lambda h: h.dma_start(out=gw[:, :], in_=w_d[sid_rec_gate(j)]), writes=[t_gw])
            gv = gw[:, :].rearrange("p (w n j i) -> p w n j i", w=2, n=4, j=2)
            S.op("dve", lambda h: h.memset(xr[:, :, 0:3], 0.0), writes=xr_tok)
            S.op("dve", lambda h: h.memset(hst[:, :], 0.0), writes=hs_tok)
            prot = Rot([0, 1, 2, 3])
            grot = Rot([4, 5, 6])
            ssbank = 7
            cw = V_CONVW + j * 32
            prog = {"conv": 0}

            def gen_win(T):
                tok0 = T * 512
                gate = gates[T % 2]
                g_tok = g_toks[T % 2]
                norm_tile(nb, tok0, 512, VG_MIX + layer * 8, hn, t_hn, ssbank)
                yield
                for g in range(4):
                    slot, stok, _ = w_next(sid_rec_in(j, g))
                    sv = slot[:, :].rearrange("p (c n) -> p c n", c=8)
                    for oc in range(4):
                        o = g * 4 + oc
                        if o >= 8:
                            while prog["conv"] <= (o - 8) // 2 and prog["conv"] < 4 and T > 0:
                                yield
                        b = prot.next()
                        for c in range(NCH):
                            S.op("pe", lambda h, b=b, c=c, oc=oc, sv=sv: h.matmul(
                                banks[b][:, :], lhsT=sv[:, c, oc * 128:(oc + 1) * 128], rhs=hn[:, c, :],
                                start=(c == 0), stop=(c == NCH - 1)),
                                reads=[stok, t_hn], writes=[bt[b]], inc=(c == NCH - 1))
                        if o < 8:
                            S.op("act", lambda h, b=b, o=o, gate=gate: h.activation(out=gate[:, o, :], in_=banks[b][:, :],
                                                                                    func=AF.Gelu_apprx_tanh),
                                 reads=[bt[b]], writes=[g_tok[o]])
                        else:
                            c2 = o - 8
                            S.op("act", lambda h, b=b, c2=c2: h.copy(out=xr[:, c2, 3:515], in_=banks[b][:, :]),
                                 reads=[bt[b]], writes=[xr_tok[c2]])
                        yield
                    w_prefetch()

            def gen_ew(T):
                gate = gates[T % 2]
                g_tok = g_toks[T % 2]
                prog["conv"] = 0

                def stage_a(n):
                    B = sets[n % 2]
                    c0 = 2 * n
                    for cc in range(2):
                        c = c0 + cc
                        S.op("dve", lambda h, B=B, cc=cc, c=c: h.tensor_scalar(
                            out=B["xc"][:, cc, :], in0=xr[:, c, 0:512], scalar1=vecs[:, cw + c:cw + c + 1],
                            scalar2=vecs[:, V_CONVB + j * 8 + c:V_CONVB + j * 8 + c + 1], op0=ALU.mult, op1=ALU.add),
                            reads=[xr_tok[c], t_vecs], writes=[B["t_xc"]])
                        for k in range(1, 4):
                            S.op("dve", lambda h, B=B, cc=cc, c=c, k=k: h.scalar_tensor_tensor(
                                out=B["xc"][:, cc, :], in0=xr[:, c, k:k + 512],
                                scalar=vecs[:, cw + k * 8 + c:cw + k * 8 + c + 1], op0=ALU.mult,
                                in1=B["xc"][:, cc, :], op1=ALU.add),
                                reads=[xr_tok[c], t_vecs, B["t_xc"]], writes=[B["t_xc"]])
                        S.op("act", lambda h, c=c: h.copy(out=xr[:, c, 0:3], in_=xr[:, c, 512:515]),
                             reads=[xr_tok[c]], writes=[xr_tok[c]])
                    prog["conv"] = n + 1
                    S.op("act", lambda h, B=B: h.copy(out=B["xcb"][:, :, :], in_=B["xc"][:, :, :]),
                         reads=[B["t_xc"]], writes=[B["t_xcb"]])
                    yield
                    for cc in range(2):
                        c = c0 + cc
                        for wsel, dst, tk, bcol in ((0, "r", "t_r", V_BA), (1, "i", "t_i", V_BX)):
                            b = grot.next()
                            for jc in range(2):
                                S.op("pe", lambda h, b=b, wsel=wsel, n=n, jc=jc, cc=cc, B=B: h.matmul(
                                    banks[b][:, :], lhsT=gv[:, wsel, n, jc, cc * 128:(cc + 1) * 128],
                                    rhs=B["xcb"][:, jc, :], start=(jc == 0), stop=(jc == 1)),
                                    reads=[t_gw, B["t_xcb"]], writes=[bt[b]], inc=(jc == 1))
                            S.op("act", lambda h, b=b, B=B, dst=dst, cc=cc, bcol=bcol, c=c: h.activation(
                                out=B[dst][:, cc, :], in_=banks[b][:, :], func=AF.Sigmoid,
                                bias=vecs[:, bcol + j * 8 + c:bcol + j * 8 + c + 1], scale=1.0),
                                reads=[bt[b], t_vecs], writes=[B[tk]])
                        yield
                    for cc in range(2):
                        c = c0 + cc
                        S.op("act", lambda h, B=B, cc=cc, c=c: h.activation(
                            out=B["r"][:, cc, :], in_=B["r"][:, cc, :], func=AF.Exp,
                            scale=clam[:, j * 8 + c:j * 8 + c + 1]),
                            reads=[B["t_r"], t_clam], writes=[B["t_r"]])
                    S.op("act", lambda h, B=B: h.activation(out=B["m"][:, :, :], in_=B["r"][:, :, :], func=AF.Square),
                         reads=[B["t_r"]], writes=[B["t_m"]])
                    S.op("act", lambda h, B=B: h.activation(out=B["m"][:, :, :], in_=B["m"][:, :, :], func=AF.Relu,
                                                           bias=1.0, scale=-1.0),
                         reads=[B["t_m"]], writes=[B["t_m"]])
                    S.op("act", lambda h, B=B: h.activation(out=B["m"][:, :, :], in_=B["m"][:, :, :], func=AF.Sqrt),
                         reads=[B["t_m"]], writes=[B["t_m"]])
                    yield

                def stage_b(n):
                    B = sets[n % 2]
                    c0 = 2 * n
                    S.op("dve", lambda h, B=B: h.tensor_tensor(out=B["i"][:, :, :], in0=B["i"][:, :, :], in1=B["xc"][:, :, :],
                                                              op=ALU.mult),
                         reads=[B["t_i"], B["t_xc"]], writes=[B["t_i"]])
                    S.op("dve", lambda h, B=B: h.tensor_tensor(out=B["i"][:, :, :], in0=B["i"][:, :, :], in1=B["m"][:, :, :],
                                                              op=ALU.mult),
                         reads=[B["t_i"], B["t_m"]], writes=[B["t_i"]])
                    yield
                    for cc in range(2):
                        c = c0 + cc
                        S.op("dve", lambda h, B=B, cc=cc, c=c: h.tensor_tensor_scan(
                            out=B["hs"][:, cc, :], data0=B["r"][:, cc, :], data1=B["i"][:, cc, :],
                            initial=hst[:, c:c + 1], op0=ALU.mult, op1=ALU.add),
                            reads=[B["t_r"], B["t_i"], hs_tok[c]], writes=[B["t_hs"]])
                        S.op("act", lambda h, B=B, cc=cc, c=c: h.copy(out=hst[:, c:c + 1], in_=B["hs"][:, cc, 511:512]),
                             reads=[B["t_hs"]], writes=[hs_tok[c]])
                    S.op("dve", lambda h, B=B, c0=c0, gate=gate: h.tensor_tensor(out=yT[:, c0:c0 + 2, :], in0=B["hs"][:, :, :],
                                                                                in1=gate[:, c0:c0 + 2, :], op=ALU.mult),
                         reads=[B["t_hs"], g_tok[c0], g_tok[c0 + 1]], writes=[y_tok[n]])
                    yield


                yield from stage_a(0)
                for n in range(4):
                    if n < 3:
                        yield from stage_a(n + 1)
                    yield from stage_b(n)

            def wout(T):
                for g in range(2):
                    slot, stok, _ = w_next(sid_rec_out(j, g))
                    sv = slot[:, :].rearrange("p (c n) -> p c n", c=8)
                    for dc in range(4):
                        d = g * 4 + dc
                        b = prot.next()
                        for c in range(NCH):
                            S.op("pe", lambda h, b=b, c=c, dc=dc, sv=sv: h.matmul(
                                banks[b][:, :], lhsT=sv[:, c, dc * 128:(dc + 1) * 128], rhs=yT[:, c, :],
                                start=(c == 0), stop=(c == NCH - 1)),
                                reads=[stok] + y_tok, writes=[bt[b]], inc=(c == NCH - 1))
                        S.op("dve", lambda h, b=b, d=d, T=T: h.tensor_tensor(
                            out=xT[:, d, T * 512:(T + 1) * 512], in0=xT[:, d, T * 512:(T + 1) * 512],
                            in1=banks[b][:, :], op=ALU.add),
                            reads=[bt[b], xtok[d][T]], writes=[xtok[d][T]])
                    w_prefetch()

            def run_ilv(ga, gb):
                da = db = False
                while not (da and db):
                    if not da:
                        try:
                            next(ga)
                        except StopIteration:
                            da = True
                            prog["conv"] = 4
                    if not db:
                        try:
                            next(gb)
                        except StopIteration:
                            db = True

            for _ in gen_win(0):
                pass
            for T in range(4):
                if T < 3:
                    run_ilv(gen_ew(T), gen_win(T + 1))
                else:
                    for _ in gen_ew(T):
                        pass
                wout(T)

        def phase_dsa(layer):
            j = layer // 2
            am = arena(with_pool=True)
            nb = norm_bufs(am)
            kiA = am.alloc("kiA", [128, S_LEN], BF16); ki_tok = [Tok() for _ in range(4)]
            kiB = am.alloc("kiB", [128, S_LEN], BF16)
            cTc = am.alloc("cTc", [128, S_LEN], BF16); c_tok = [Tok() for _ in range(4)]
            V = am.alloc("V", [128, 16, 16 * 65], BF16); v_tok = [Tok() for _ in range(16)]
            attw = am.alloc("attw", [128, 3072], BF16); t_attw = Tok()
            btab = am.alloc("btab", [128, 4096], BF16); t_btab = Tok()
            btf = am.alloc("btf", [128, 16], F32); t_btf = Tok()
            hn = am.alloc("hn", [128, NCH, 512], BF16); t_hn = Tok()
            qT = am.alloc("qT", [128, NCH, 512], BF16); q_tok = [Tok() for _ in range(NCH)]
            qiT = am.alloc("qiT", [128, 4, 512], BF16); qi_tok = [Tok() for _ in range(4)]
            craw = am.alloc("craw", [128, 512], F32); t_craw = Tok()
            wabs = am.alloc("wabs", [128, 32], F32); t_wabs = Tok()
            wsgn = am.alloc("wsgn", [128, 32], F32); t_wsgn = Tok()
            score = am.alloc("score", [128, S_LEN], F32); sc_tok = [Tok() for _ in range(4)]
            mask = am.alloc("mask", [128, S_LEN], BF16); t_mask = Tok()
            MT = am.alloc("MT", [128, 16, 128], BF16); mt_tok = [Tok() for _ in range(4)]
            qlat = am.alloc("qlat", [128, 16, 128], BF16); ql_tok = [Tok() for _ in range(4)]
            rl_rot = Rot([(am.alloc("rl", [128, 512], F32), Tok()) for _ in range(2)])
            e_rot = Rot([(am.alloc("e", [128, 512], BF16), Tok()) for _ in range(3)])
            em_rot = Rot([(am.alloc("em", [128, 512], BF16), Tok()) for _ in range(3)])
            on = am.alloc("on", [128, 1024], BF16); t_on = Tok()
            bis = am.alloc("bis", [128, 8], F32); t_bis = Tok(); t_bis2 = Tok()
            wtab = am.alloc("wtab", [128, NIT + 1], F32); t_wtab = Tok()
            recip = am.alloc("recip", [128, 16], F32); t_recip = Tok()
            S.dma("pool", lambda h: h.dma_start(out=attw[:, :], in_=attw_d[j]), writes=[t_attw])
            S.dma("pool", lambda h: h.dma_start(out=btab[:, :], in_=btab_d), writes=[t_btab])
            S.dma("sp", lambda h: h.dma_start(out=btf[:, :], in_=far_d), writes=[t_btf])
            bt4 = btab[:, :].rearrange("p (o h q) -> p o h q", o=2, h=16)
            for o in range(2):
                S.op("dve", lambda h, o=o: h.tensor_tensor(
                    out=bt4[:, o, :, :], in0=bt4[:, o, :, :],
                    in1=btf[:, :].unsqueeze(2).to_broadcast([128, 16, 128]), op=ALU.subtract),
                    reads=[t_btab, t_btf], writes=[t_btab])
            V4 = V[:, :, :].rearrange("p k (h e) -> p k h e", e=65)
            S.op("dve", lambda h: h.memset(V4[:, :, :, 64:65], 1.0), writes=v_tok)
            S.op("dve", lambda h: h.memset(kiA[64:128, :], 0.0), writes=ki_tok)
            S.op("dve", lambda h: h.memset(kiB[0:64, :], 0.0), writes=ki_tok)
            IDXE = DBG.get("idx_eng", "dve")
            prot = Rot([0, 1, 5, 6])
            lrot = Rot([0, 1])
            lrot3 = Rot([0, 1, 7])
            drot = Rot([5, 6])
            OB = [2, 3, 4]
            XB = 7
            SUB = DBG.get("dsa_sub", 9)
            for T in range(4):
                tok0 = T * 512
                if SUB < 3:
                    if SUB >= 1:
                        norm_tile(nb, tok0, 512, VG_MIX + layer * 8, hn, t_hn, XB)
                    for g in range(6):
                        sid = sid_att_in(j, g) if g < 4 else sid_att_o(j, g - 4)
                        slot, stok, _ = w_next(sid)
                        sv = slot[:, :].rearrange("p (c n) -> p c n", c=8)
                        if SUB >= 2 and g < 2:
                            for oc in range(4):
                                o = g * 4 + oc
                                b = prot.next()
                                for c in range(NCH):
                                    S.op("pe", lambda h, b=b, c=c, oc=oc, sv=sv: h.matmul(
                                        banks[b][:, :], lhsT=sv[:, c, oc * 128:(oc + 1) * 128], rhs=hn[:, c, :],
                                        start=(c == 0), stop=(c == NCH - 1)),
                                        reads=[stok, t_hn], writes=[bt[b]], inc=(c == NCH - 1))
                                evac_copy(qT[:, o, :], banks[b][:, :], reads=[bt[b]], writes=[q_tok[o]])
                        else:
                            S.op("pe", lambda h, sv=sv: h.matmul(banks[0][:, :], lhsT=sv[:, 0, 0:128], rhs=sv[:, 0, 0:512],
                                                                 start=True, stop=True), reads=[stok], writes=[bt[0]])
                        w_prefetch()
                    continue
                norm_tile(nb, tok0, 512, VG_MIX + layer * 8, hn, t_hn, XB)
                for g in range(2):
                    slot, stok, _ = w_next(sid_att_in(j, g))
                    sv = slot[:, :].rearrange("p (c n) -> p c n", c=8)
                    for oc in range(4):
                        o = g * 4 + oc
                        b = prot.next()
                        for c in range(NCH):
                            S.op("pe", lambda h, b=b, c=c, oc=oc, sv=sv: h.matmul(
                                banks[b][:, :], lhsT=sv[:, c, oc * 128:(oc + 1) * 128], rhs=hn[:, c, :],
                                start=(c == 0), stop=(c == NCH - 1)),
                                reads=[stok, t_hn], writes=[bt[b]], inc=(c == NCH - 1))
                        evac_copy(qT[:, o, :], banks[b][:, :], reads=[bt[b]], writes=[q_tok[o]])
                    w_prefetch()
                slot, stok, _ = w_next(sid_att_in(j, 2))
                sv = slot[:, :].rearrange("p (c n) -> p c n", c=8)
                b = prot.next()
                for c in range(NCH):
                    S.op("pe", lambda h, b=b, c=c, sv=sv: h.matmul(banks[b][:, :], lhsT=sv[:, c, 0:128], rhs=hn[:, c, :],
                                                                  start=(c == 0), stop=(c == NCH - 1)),
                         reads=[stok, t_hn], writes=[bt[b]], inc=(c == NCH - 1))
                S.op("act", lambda h, b=b: h.copy(out=craw[:, :], in_=banks[b][:, :]), reads=[bt[b]], writes=[t_craw])
                b = prot.next()
                for c in range(NCH):
                    S.op("pe", lambda h, b=b, c=c, sv=sv: h.matmul(banks[b][:, :], lhsT=sv[:, c, 128:256], rhs=hn[:, c, :],
                                                                  start=(c == 0), stop=(c == NCH - 1)),
                         reads=[stok, t_hn], writes=[bt[b]], inc=(c == NCH - 1))
                evac_copy(kiA[0:64, tok0:tok0 + 512], banks[b][0:64, :], reads=[bt[b]], writes=[ki_tok[T]])
                evac_copy(kiB[64:128, tok0:tok0 + 512], banks[b][64:128, :], reads=[bt[b]], writes=[ki_tok[T]])
                SK = DBG.get("skip", ())
                for qb in range(4 if "wi" not in SK else 0):
                    for c in range(NCH):
                        S.op("pe", lambda h, qb=qb, c=c, sv=sv: h.matmul(
                            banks[XB][:, qb * 8:(qb + 1) * 8], lhsT=hn[:, c, qb * 128:(qb + 1) * 128],
                            rhs=sv[:, c, 256:264], start=(c == 0), stop=(c == NCH - 1), skip_group_check=True),
                            reads=[stok, t_hn], writes=[bt[XB]], inc=(c == NCH - 1))
                if IDXE == "dve2":
                    S.op("act", lambda h: h.mul(out=wabs[:, :], in_=banks[XB][:, 0:32], mul=IDX_W_SCALE),
                         reads=[bt[XB]], writes=[t_wabs])
                elif "abs" not in SK:
                    S.op("act", lambda h: h.activation(out=wabs[:, :], in_=banks[XB][:, 0:32], func=AF.Abs, scale=IDX_W_SCALE),
                         reads=[bt[XB]], writes=[t_wabs])
                    S.op("act", lambda h: h.activation(out=wsgn[:, :], in_=banks[XB][:, 0:32], func=AF.Sign),
                         reads=[bt[XB]], writes=[t_wsgn])
                w_prefetch()
                k = nb[0].next()
                if "kvn" in SK:
                    S.op("act", lambda h, tok0=tok0: h.copy(out=cTc[:, tok0:tok0 + 512], in_=craw[:, :]), reads=[t_craw], writes=[c_tok[T]])
                S.op("act", lambda h, k=k: h.activation(out=k[0][:, :], in_=craw[:, :], func=AF.Square),
                     reads=[t_craw], writes=[k[1]])
                S.op("pe", lambda h, k=k: h.matmul(banks[XB][:, :], lhsT=onesb[:, :], rhs=k[0][:, :], start=True, stop=True),
                     reads=[k[1], t_ones], writes=[bt[XB]])
                rs, rs_tok = nb[2], nb[3]
                S.op("act", lambda h: h.activation(out=rs[:, :], in_=banks[XB][:, :], func=AF.Sqrt, bias=EPS, scale=1.0 / 128),
                     reads=[bt[XB]], writes=[rs_tok])
                S.op("dve", lambda h: h.reciprocal(out=rs[:, :], in_=rs[:, :]), reads=[rs_tok], writes=[rs_tok])
                S.op("dve", lambda h, tok0=tok0: h.scalar_tensor_tensor(
                    out=cTc[:, tok0:tok0 + 512], in0=craw[:, :], scalar=vecs[:, V_KVG + j:V_KVG + j + 1], op0=ALU.mult,
                    in1=rs[:, :], op1=ALU.mult),
                    reads=[t_craw, rs_tok, t_vecs], writes=[c_tok[T]])
                slot, stok, _ = w_next(sid_att_in(j, 3))
                sv = slot[:, :].rearrange("p (c n) -> p c n", c=8)
                for oc in range(4):
                    b = prot.next()
                    for c in range(NCH):
                        S.op("pe", lambda h, b=b, c=c, oc=oc, sv=sv: h.matmul(
                            banks[b][:, :], lhsT=sv[:, c, oc * 128:(oc + 1) * 128], rhs=hn[:, c, :],
                            start=(c == 0), stop=(c == NCH - 1)),
                            reads=[stok, t_hn], writes=[bt[b]], inc=(c == NCH - 1))
                    evac_copy(qiT[:, oc, :], banks[b][:, :], reads=[bt[b]], writes=[qi_tok[oc]])
                w_prefetch()
                for kl in range(4 if "V" not in SK else 0):
                    kb = T * 4 + kl
                    for half in range(2):
                        b = prot.next()
                        S.op("pe", lambda h, b=b, kb=kb, half=half: h.matmul(
                            banks[b][:, :], lhsT=cTc[:, kb * 128:(kb + 1) * 128],
                            rhs=attw[:, 2048 + half * 512:2048 + (half + 1) * 512], start=True, stop=True),
                            reads=[c_tok[T], t_attw], writes=[bt[b]])
                        evac_copy(V4[:, kb, half * 8:(half + 1) * 8, 0:64],
                                  banks[b][:, :].rearrange("p (h v) -> p h v", v=64), reads=[bt[b]], writes=[v_tok[kb]])
                xb_bf = banks[XB][:, :].bitcast(BF16)

                def gen_index(qb, T=T):
                    G = T * 4 + qb
                    nk = (G + 1) * 128
                    q0 = qb * 128
                    nch = (nk + 511) // 512
                    for hi in range(H_IDX):
                        o, par = hi // 2, hi % 2
                        for kc in range(nch):
                            w = min(512, nk - kc * 512)
                            b = drot.next()
                            S.op("pe", lambda h, b=b, o=o, par=par, q0=q0, kc=kc, w=w: h.matmul(
                                banks[b][:, 0:w], lhsT=qiT[:, o, q0:q0 + 128],
                                rhs=(kiA if par == 0 else kiB)[:, kc * 512:kc * 512 + w], start=True, stop=True),
                                reads=[qi_tok[o], ki_tok[kc]], writes=[bt[b]])
                            r_ = rl_rot.next()
                            col = qb * 8 + hi
                            if IDXE == "dve2":
                                if hi == 0:
                                    S.op("dve", lambda h, b=b, w=w, kc=kc, col=col: h.tensor_scalar(
                                        out=score[:, kc * 512:kc * 512 + w], in0=banks[b][:, 0:w], scalar1=0.0,
                                        scalar2=wabs[:, col:col + 1], op0=ALU.max, op1=ALU.mult),
                                        reads=[bt[b], t_wabs], writes=[sc_tok[kc]])
                                else:
                                    S.op("dve", lambda h, b=b, r_=r_, w=w, col=col: h.tensor_scalar(
                                        out=r_[0][:, 0:w], in0=banks[b][:, 0:w], scalar1=0.0,
                                        scalar2=wabs[:, col:col + 1], op0=ALU.max, op1=ALU.mult),
                                        reads=[bt[b], t_wabs], writes=[r_[1]])
                                    S.op("dve", lambda h, r_=r_, w=w, kc=kc: h.tensor_tensor(
                                        out=score[:, kc * 512:kc * 512 + w], in0=score[:, kc * 512:kc * 512 + w],
                                        in1=r_[0][:, 0:w], op=ALU.add),
                                        reads=[r_[1], sc_tok[kc]], writes=[sc_tok[kc]])
                                yield
                                continue
                            S.op("act", lambda h, b=b, r_=r_, w=w, col=col: h.activation(
                                out=r_[0][:, 0:w], in_=banks[b][:, 0:w], func=AF.Relu, scale=wabs[:, col:col + 1]),
                                reads=[bt[b], t_wabs], writes=[r_[1]])
                            if hi == 0:
                                S.op(IDXE, lambda h, r_=r_, w=w, kc=kc, col=col: h.tensor_scalar(
                                    out=score[:, kc * 512:kc * 512 + w], in0=r_[0][:, 0:w], scalar1=wsgn[:, col:col + 1],
                                    scalar2=None, op0=ALU.mult),
                                    reads=[r_[1], t_wsgn], writes=[sc_tok[kc]])
                            elif IDXE == "dve":
                                S.op("dve", lambda h, r_=r_, w=w, kc=kc, col=col: h.scalar_tensor_tensor(
                                    out=score[:, kc * 512:kc * 512 + w], in0=r_[0][:, 0:w], scalar=wsgn[:, col:col + 1],
                                    op0=ALU.mult, in1=score[:, kc * 512:kc * 512 + w], op1=ALU.add),
                                    reads=[r_[1], t_wsgn, sc_tok[kc]], writes=[sc_tok[kc]])
                            else:
                                S.op("pool", lambda h, r_=r_, w=w, col=col: h.tensor_scalar(
                                    out=r_[0][:, 0:w], in0=r_[0][:, 0:w], scalar1=wsgn[:, col:col + 1],
                                    scalar2=None, op0=ALU.mult),
                                    reads=[r_[1], t_wsgn], writes=[r_[1]])
                                S.op("pool", lambda h, r_=r_, w=w, kc=kc: h.tensor_tensor(
                                    out=score[:, kc * 512:kc * 512 + w], in0=score[:, kc * 512:kc * 512 + w],
                                    in1=r_[0][:, 0:w], op=ALU.add),
                                    reads=[r_[1], sc_tok[kc]], writes=[sc_tok[kc]])
                            yield
                    sct = sc_tok[0:nch]
                    if G >= 2:
                        S.op("dve", lambda h, nk=nk: h.tensor_reduce(out=bis[:, 0:1], in_=score[:, 0:nk], axis=mybir.AxisListType.X,
                                                                     op=ALU.max, apply_absolute_value=True),
                             reads=sct, writes=[t_bis])
                        S.op("dve", lambda h: h.tensor_scalar(out=wtab[:, :], in0=vecs[:, V_POW:V_POW + NIT + 1],
                                                              scalar1=bis[:, 0:1], scalar2=None, op0=ALU.mult),
                             reads=[t_bis, t_vecs], writes=[t_wtab])
                        S.op("dve", lambda h: h.memset(bis[:, 1:2], 0.0), reads=[t_bis], writes=[t_bis])
                    S.op("dve", lambda h, G=G: h.tensor_tensor(out=score[:, G * 128:(G + 1) * 128],
                                                               in0=score[:, G * 128:(G + 1) * 128], in1=cneg[:, :], op=ALU.add),
                         reads=[sc_tok[G // 4], t_cneg], writes=[sc_tok[G // 4]])
                    yield

                def gen_select(qb, T=T):
                    G = T * 4 + qb
                    nk = (G + 1) * 128
                    nch = (nk + 511) // 512
                    sct = sc_tok[0:nch]
                    if G >= 2:
                        for it in range(NIT):
                            S.op("act", lambda h, nk=nk: h.activation(out=mask[:, 0:nk], in_=score[:, 0:nk], func=AF.Sign,
                                                                      bias=bis[:, 1:2], scale=1.0, accum_out=bis[:, 2:3]),
                                 reads=sct + [t_bis], writes=[t_mask, t_bis])
                            yield
                            S.op("dve", lambda h, nk=nk, it=it: h.scalar_tensor_tensor(
                                out=bis[:, 3:4], in0=bis[:, 2:3], scalar=511.5 - float(nk), op0=ALU.is_gt,
                                in1=wtab[:, it:it + 1], op1=ALU.mult),
                                reads=[t_bis, t_wtab], writes=[t_bis2])
                            S.op("dve", lambda h, it=it: h.scalar_tensor_tensor(
                                out=bis[:, 1:2], in0=bis[:, 3:4], scalar=wtab[:, it + 1:it + 2], op0=ALU.subtract,
                                in1=bis[:, 1:2], op1=ALU.add),
                                reads=[t_bis, t_bis2, t_wtab], writes=[t_bis])
                            yield
                        S.op("dve", lambda h: h.tensor_scalar(out=bis[:, 4:5], in0=bis[:, 1:2], scalar1=wtab[:, NIT:NIT + 1],
                                                              scalar2=-1.0, op0=ALU.subtract, op1=ALU.mult),
                             reads=[t_bis, t_wtab], writes=[t_bis])
                        S.op("dve", lambda h, nk=nk: h.tensor_scalar(out=mask[:, 0:nk], in0=score[:, 0:nk], scalar1=bis[:, 4:5],
                                                                     scalar2=None, op0=ALU.is_gt),
                             reads=sct + [t_bis], writes=[t_mask])
                    else:
                        S.op("dve", lambda h, nk=nk: h.tensor_scalar(out=mask[:, 0:nk], in0=score[:, 0:nk], scalar1=-1.0e29,
                                                                     scalar2=None, op0=ALU.is_gt),
                             reads=sct, writes=[t_mask])
                    yield

                def gen_attn(qb, T=T):
                    G = T * 4 + qb
                    q0 = qb * 128
                    for k0 in range(0, G + 1, 4):
                        n4 = min(4, G + 1 - k0)
                        for i in range(n4):
                            kb = k0 + i
                            S.op("pe", lambda h, i=i, kb=kb: h.transpose(xb_bf[:, i * 128:(i + 1) * 128],
                                                                        mask[:, kb * 128:(kb + 1) * 128], identb[:, :]),
                                 reads=[t_mask, t_identb], writes=[bt[XB]], inc=(i == n4 - 1))
                        evac_copy(MT[:, k0:k0 + n4, :], xb_bf[:, 0:n4 * 128].rearrange("p (a q) -> p a q", a=n4),
                                  reads=[bt[XB]], writes=[mt_tok[k0 // 4]])
                    yield
                    for hg in range(4):
                        for hl in range(4):
                            hh = hg * 4 + hl
                            o = hh // 2
                            S.op("pe", lambda h, hl=hl, o=o, hh=hh, q0=q0: h.matmul(
                                banks[XB][:, hl * 128:(hl + 1) * 128],
                                lhsT=attw[:, hh * 128:(hh + 1) * 128],
                                rhs=qT[:, o, q0:q0 + 128], start=True, stop=True, skip_group_check=True),
                                reads=[t_attw, q_tok[o]], writes=[bt[XB]], inc=(hl == 3))
                        S.op("act", lambda h, hg=hg: h.mul(out=qlat[:, hg * 4:(hg + 1) * 4, :],
                                                           in_=banks[XB][:, :].rearrange("p (a q) -> p a q", a=4), mul=ATT_SCALE),
                             reads=[bt[XB]], writes=[ql_tok[hg]])
                    yield
                    started = [False, False, False]
                    tiles = [(hg, kb) for hg in range(4) for kb in range(G + 1)]
                    pend = []
                    SKEW = 2

                    def issue_pv(hg, kb, m_):
                        for hl in range(4):
                            hh = hg * 4 + hl
                            ob = hh // 6
                            col = (hh % 6) * 65
                            st_flag = not started[ob]
                            started[ob] = True
                            S.op("pe", lambda h, ob=ob, col=col, m_=m_, hl=hl, kb=kb, hh=hh, st_flag=st_flag: h.matmul(
                                banks[OB[ob]][:, col:col + 65], lhsT=m_[0][:, hl * 128:(hl + 1) * 128],
                                rhs=V[:, kb, hh * 65:(hh + 1) * 65], start=st_flag, stop=False, skip_group_check=True),
                                reads=[m_[1], v_tok[kb]], writes=[bt[OB[ob]]], inc=(hl == 3))

                    for (hg, kb) in tiles:
                        b = lrot3.next()
                        S.op("pe", lambda h, b=b, kb=kb, hg=hg: h.matmul(
                            banks[b][:, :], lhsT=cTc[:, kb * 128:(kb + 1) * 128], rhs=qlat[:, hg * 4:(hg + 1) * 4, :],
                            start=True, stop=True),
                            reads=[c_tok[kb // 4], ql_tok[hg]], writes=[bt[b]])
                        e_ = e_rot.next()
                        off = G - kb
                        if off < 2:
                            S.op("dve", lambda h, b=b, off=off, hg=hg: h.tensor_tensor(
                                out=banks[b][:, :].rearrange("p (a q) -> p a q", a=4),
                                in0=banks[b][:, :].rearrange("p (a q) -> p a q", a=4),
                                in1=bt4[:, off, hg * 4:(hg + 1) * 4, :], op=ALU.add),
                                reads=[bt[b], t_btab], writes=[bt[b]])
                        S.op("act", lambda h, b=b, e_=e_: h.activation(out=e_[0][:, :], in_=banks[b][:, :], func=AF.Exp),
                             reads=[bt[b]], writes=[e_[1]])
                        m_ = em_rot.next()
                        S.op("dve", lambda h, e_=e_, m_=m_, kb=kb: h.tensor_tensor(
                            out=m_[0][:, :].rearrange("p (a q) -> p a q", a=4),
                            in0=e_[0][:, :].rearrange("p (a q) -> p a q", a=4),
                            in1=MT[:, kb:kb + 1, :].to_broadcast([128, 4, 128]), op=ALU.mult),
                            reads=[e_[1], mt_tok[kb // 4]], writes=[m_[1]])
                        pend.append((hg, kb, m_))
                        if len(pend) > SKEW:
                            issue_pv(*pend.pop(0))
                        yield
                    while pend:
                        issue_pv(*pend.pop(0))
                    yield
                    for ob in range(3):
                        nh = 6 if ob < 2 else 4
                        o3 = banks[OB[ob]][:, 0:nh * 65].rearrange("p (h e) -> p h e", e=65)
                        S.op("dve", lambda h, ob=ob, nh=nh, o3=o3: h.reciprocal(
                            out=recip[:, ob * 6:ob * 6 + nh].unsqueeze(2), in_=o3[:, :, 64:65]),
                            reads=[bt[OB[ob]]], writes=[t_recip])
                        S.op("dve", lambda h, ob=ob, nh=nh, o3=o3: h.tensor_tensor(
                            out=on[:, ob * 384:ob * 384 + nh * 64].rearrange("p (h v) -> p h v", v=64),
                            in0=o3[:, :, 0:64],
                            in1=recip[:, ob * 6:ob * 6 + nh].unsqueeze(2).to_broadcast([128, nh, 64]), op=ALU.mult),
                            reads=[bt[OB[ob]], t_recip], writes=[t_on])
                    yield
                    for half in range(2):
                        for i in range(4):
                            o = half * 4 + i
                            S.op("pe", lambda h, i=i, o=o: h.transpose(xb_bf[:, i * 128:(i + 1) * 128],
                                                                      on[:, o * 128:(o + 1) * 128], identb[:, :]),
                                 reads=[t_on, t_identb], writes=[bt[XB]], inc=(i == 3))
                        evac_copy(hn[:, half * 4:half * 4 + 4, q0:q0 + 128],
                                  xb_bf[:, 0:512].rearrange("p (a q) -> p a q", a=4), reads=[bt[XB]], writes=[t_hn])
                    yield

                def chain(*gens):
                    for g in gens:
                        yield from g

                def interleave(ga, gb, ra=1, rb=1):
                    done_a = done_b = False
                    while not (done_a and done_b):
                        for _ in range(ra):
                            if not done_a:
                                try:
                                    next(ga)
                                except StopIteration:
                                    done_a = True
                        for _ in range(rb):
                            if not done_b:
                                try:
                                    next(gb)
                                except StopIteration:
                                    done_b = True

                for _ in chain(gen_index(0), gen_select(0)):
                    pass
                for qb in range(4):
                    if qb < 3 and DBG.get("pipeline", True):
                        interleave(gen_attn(qb), chain(gen_index(qb + 1), gen_select(qb + 1)), 1, 1)
                    else:
                        for _ in gen_attn(qb):
                            pass
                        if qb < 3:
                            for _ in chain(gen_index(qb + 1), gen_select(qb + 1)):
                                pass
                for g in range(2):
                    slot, stok, _ = w_next(sid_att_o(j, g))
                    sv = slot[:, :].rearrange("p (c n) -> p c n", c=8)
                    for dc in range(4):
                        d = g * 4 + dc
                        b = prot.next()
                        for c in range(NCH):
                            S.op("pe", lambda h, b=b, c=c, dc=dc, sv=sv: h.matmul(
                                banks[b][:, :], lhsT=sv[:, c, dc * 128:(dc + 1) * 128], rhs=hn[:, c, :],
                                start=(c == 0), stop=(c == NCH - 1)),
                                reads=[stok, t_hn], writes=[bt[b]], inc=(c == NCH - 1))
                        S.op("dve", lambda h, b=b, d=d, T=T: h.tensor_tensor(
                            out=xT[:, d, T * 512:(T + 1) * 512], in0=xT[:, d, T * 512:(T + 1) * 512],
                            in1=banks[b][:, :], op=ALU.add),
                            reads=[bt[b], xtok[d][T]], writes=[xtok[d][T]])
                    w_prefetch()

        def phase_final():
            am = arena()
            nb = norm_bufs(am)
            yf = am.alloc("yf", [128, NCH, 512], F32); t_yf = Tok()
            ost = [(am.alloc("ost", [128, D], F32), Tok()) for _ in range(2)]
            brot = Rot([0, 1, 2, 3, 4, 5])
            n_out = 0
            for T in range(4):
                norm_tile(nb, T * 512, 512, VG_FIN, yf, t_yf, 7)
                for tb in range(4):
                    og, ok = ost[n_out % 2]
                    n_out += 1
                    for half in range(2):
                        b = brot.next()
                        for i in range(4):
                            c = half * 4 + i
                            S.op("pe", lambda h, b=b, i=i, c=c, tb=tb: h.transpose(
                                banks[b][:, i * 128:(i + 1) * 128], yf[:, c, tb * 128:(tb + 1) * 128], ident[:, :]),
                                reads=[t_yf, t_ident], writes=[bt[b]], inc=(i == 3))
                        evac_copy(og[:, half * 512:(half + 1) * 512], banks[b][:, :], reads=[bt[b]], writes=[ok])
                    r0 = T * 512 + tb * 128
                    S.dma("sp", lambda h, og=og, r0=r0: h.dma_start(out=out_d[r0:r0 + 128, :], in_=og[:, :]), reads=[ok])

        phase_load()
        for layer in range(n_layers):
            if layer % 2 == 0:
                phase_rglru(layer)
            elif not DBG.get("skip_dsa"):
                phase_dsa(layer)
            if not DBG.get("skip_mlp"):
                for _rep in range(DBG.get("mlp_rep", 1)):
                    phase_mlp(layer)
        phase_final()
        S.emit(S.all_events(engines=(), queues=("sp",)))
    return nc


def _fm(v):
    v = np.asarray(v, np.float32)
    lead = v.shape[:-1]
    return v.reshape(lead + (8, 128))


def _slot_kn(W, ncol=512):
    K, N = W.shape
    G = N // ncol
    return np.ascontiguousarray(W.reshape(8, 128, G, ncol).transpose(2, 1, 0, 3)).reshape(G, 128, 8 * ncol)


def prepare_shared(inp):
    f = lambda k: np.asarray(inp[k], np.float32)
    vec = np.zeros((128, NV), np.float32)
    vec[:, VG_MIX:VG_MIX + 32] = f("norm_mix_g").reshape(4, 8, 128).transpose(2, 0, 1).reshape(128, 32)
    vec[:, VG_MLP:VG_MLP + 32] = f("norm_mlp_g").reshape(4, 8, 128).transpose(2, 0, 1).reshape(128, 32)
    vec[:, VG_FIN:VG_FIN + 8] = f("final_norm_g").reshape(8, 128).T
    vec[:, V_CONVW:V_CONVW + 64] = f("rec_conv_w").reshape(2, 4, 8, 128).transpose(3, 0, 1, 2).reshape(128, 64)
    for key, col in (("rec_conv_b", V_CONVB), ("rec_b_a", V_BA), ("rec_b_x", V_BX), ("rec_lambda", V_LAM)):
        vec[:, col:col + 16] = f(key).reshape(2, 8, 128).transpose(2, 0, 1).reshape(128, 16)
    vec[:, V_KVG:V_KVG + 2] = f("att_kv_norm_g").T
    vec[:, V_POW:V_POW + NIT + 1] = -(2.0 ** -np.arange(NIT + 1, dtype=np.float64)).astype(np.float32)[None, :]
    ws = np.zeros((NSLOT, 128, 4096), np.float32)
    up, dn = f("mlp_w_up"), f("mlp_w_down")
    for l in range(4):
        ws[sid_mlp_up(l, 0):sid_mlp_up(l, 0) + 8] = _slot_kn(up[l])
        ws[sid_mlp_dn(l, 0):sid_mlp_dn(l, 0) + 8] = dn[l].reshape(32, 128, 8, 128).transpose(2, 1, 0, 3).reshape(8, 128, 4096)
    rin, rout, wa, wx = f("rec_w_in"), f("rec_w_out"), f("rec_w_a"), f("rec_w_x")
    for j in range(2):
        ws[sid_rec_in(j, 0):sid_rec_in(j, 0) + 4] = _slot_kn(rin[j])
        ga = wa[j].reshape(4, 2, 128, 256).transpose(2, 0, 1, 3).reshape(128, 2048)
        gx = wx[j].reshape(4, 2, 128, 256).transpose(2, 0, 1, 3).reshape(128, 2048)
        ws[sid_rec_gate(j)] = np.concatenate([ga, gx], axis=1)
        ws[sid_rec_out(j, 0):sid_rec_out(j, 0) + 2] = _slot_kn(rout[j])
    ain, ao = f("att_w_in"), f("att_w_o")
    for j in range(2):
        W = ain[j]
        ws[sid_att_in(j, 0):sid_att_in(j, 0) + 2] = _slot_kn(W[:, 0:1024])
        Wc = np.concatenate([W[:, 1024:1152], W[:, 1664:1728], W[:, 1664:1728], W[:, 1728:1736],
                             np.zeros((1024, 248), np.float32)], axis=1)
        ws[sid_att_in(j, 2)] = _slot_kn(Wc)[0]
        ws[sid_att_in(j, 3)] = _slot_kn(W[:, 1152:1664])[0]
        ws[sid_att_o(j, 0):sid_att_o(j, 0) + 2] = _slot_kn(ao[j])
    wuk, wuv = f("att_w_uk"), f("att_w_uv")
    attw = np.zeros((2, 128, 3072), np.float32)
    for j in range(2):
        for hh in range(16):
            par = hh % 2
            attw[j, par * 64:(par + 1) * 64, hh * 128:(hh + 1) * 128] = wuk[j, hh]
        attw[j, :, 2048:3072] = wuv[j].transpose(1, 0, 2).reshape(128, 1024)
    rb = f("rel_bias")
    kk = np.arange(128)[:, None, None]
    off = np.arange(2)[None, :, None]
    qq = np.arange(128)[None, None, :]
    dist = off * 128 + qq - kk
    n = np.maximum(dist, 0)
    nf = np.maximum(n, 1).astype(np.float32)
    large = 16 + (np.log(nf / 16) / math.log(128 / 16) * 16).astype(np.int32)
    large = np.minimum(large, 31)
    bucket = np.where(n < 16, n, large)
    btab = rb[bucket]
    btab = np.ascontiguousarray(btab.transpose(0, 1, 3, 2)).reshape(128, 4096)
    farb = np.ascontiguousarray(np.broadcast_to(rb[31][None, :], (128, 16)))
    return dict(vecs=vec, wslots=ws, attw=attw, btab=btab, farb=farb)


_CACHE = {}


def kernel(**inputs):
    x = np.asarray(inputs["x"], np.float32)
    shared = prepare_shared(inputs)
    n_layers = int(_CACHE.get("n_layers", DEPTH))
    nc = build_program(n_layers)
    in_maps = []
    for b in range(8):
        m = dict(shared)
        m["x"] = np.ascontiguousarray(x[b])
        in_maps.append(m)
    res = run_bass_kernel_spmd(nc, in_maps, core_ids=list(range(8)))
    out = np.stack([np.asarray(r["out"], np.float32) for r in res.results], axis=0)
    return out
```

```python
import math
from contextlib import ExitStack
import numpy as np
import concourse.bass as bass
import concourse.mybir as mybir
from concourse.bass_utils import run_bass_kernel_spmd

F32 = mybir.dt.float32
BF16 = mybir.dt.bfloat16
AF = mybir.ActivationFunctionType
ALU = mybir.AluOpType

S_LEN = 2048
D = 1024
NCH = 8
DEPTH = 4
EPS = 1e-6
RG_C = 8.0
H_ATT = 16
H_IDX = 8
ATT_SCALE = 64 ** -0.5
IDX_W_SCALE = (8 ** -0.5) * (64 ** -0.5)
NEG_MASK = -1.0e30
NEG_REPL = -3.0e38

VG_MIX = 0
VG_MLP = 32
VG_FIN = 64
V_CONVW = 72
V_CONVB = 136
V_BA = 152
V_BX = 168
V_LAM = 184
V_KVG = 200
V_POW = 202
NIT = 18
NV = 202 + NIT + 1

def sid_mlp_up(l, g): return l * 16 + g
def sid_mlp_dn(l, d): return l * 16 + 8 + d
def sid_rec_in(j, g): return 64 + j * 7 + g
def sid_rec_gate(j): return 64 + j * 7 + 4
def sid_rec_out(j, g): return 64 + j * 7 + 5 + g
def sid_att_in(j, g): return 78 + j * 6 + g
def sid_att_o(j, g): return 78 + j * 6 + 4 + g
NSLOT = 90


class Tok:
    __slots__ = ("w", "r")

    def __init__(self):
        self.w = None
        self.r = []


class _Eng:
    def __init__(self, name, sem):
        self.name = name
        self.sem = sem
        self.count = 0
        self.instrs = []
        self.seen = {}
        self.pend_r = []
        self.pend_w = []
        self.bar = []


class Sched:
    NDMA = 8

    def __init__(self, nc, stack):
        self.nc = nc
        self.sems = {}
        self.eng = {}
        for name in ("pe", "act", "dve", "pool", "sp"):
            self.sems["s_" + name] = stack.enter_context(nc.semaphore("s_" + name))
            self.eng[name] = _Eng(name, "s_" + name)
        self.dma_sems = {}
        self.dma_n = {}
        for q in ("sp", "pool"):
            lst = []
            for i in range(self.NDMA):
                key = "d_%s%d" % (q, i)
                self.sems[key] = stack.enter_context(nc.semaphore(key))
                lst.append(key)
            self.dma_sems[q] = lst
            self.dma_n[q] = 0

    def _deps(self, E, reads, writes):
        need = {}

        def add(ev):
            if ev is None:
                return
            k, v = ev
            if need.get(k, 0) < v:
                need[k] = v
        for t in reads:
            add(t.w)
        for t in writes:
            add(t.w)
            for ev in t.r:
                add(ev)
        for ev in E.bar:
            add(ev)
        E.bar = []
        waits = []
        for k, v in need.items():
            if E.seen.get(k, 0) < v:
                E.seen[k] = v
                waits.append((k, v))
        return waits

    def op(self, eng, fn, reads=(), writes=(), inc=True):
        E = self.eng[eng]
        waits = self._deps(E, reads, writes)
        if inc:
            E.count += 1
            ev = (E.sem, E.count)
            E.instrs.append((waits, fn, E.sem, 1))
            for t in list(reads) + E.pend_r:
                t.r.append(ev)
            for t in list(writes) + E.pend_w:
                t.w = ev
                t.r = []
            E.pend_r = []
            E.pend_w = []
            return ev
        E.instrs.append((waits, fn, None, 0))
        E.pend_r.extend(reads)
        E.pend_w.extend(writes)
        return None

    def dma(self, q, fn, reads=(), writes=()):
        E = self.eng[q]
        n = self.dma_n[q]
        self.dma_n[q] = n + 1
        key = self.dma_sems[q][n % self.NDMA]
        val = 16 * (n // self.NDMA + 1)
        waits = self._deps(E, reads, writes)
        if val > 16 and E.seen.get(key, 0) < val - 16:
            E.seen[key] = val - 16
            waits.append((key, val - 16))
        E.instrs.append((waits, fn, key, 16))
        ev = (key, val)
        for t in reads:
            t.r.append(ev)
        for t in writes:
            t.w = ev
            t.r = []
        return ev

    def all_events(self, engines=("pe", "act", "dve", "pool", "sp"), queues=("sp", "pool")):
        evs = []
        for e in engines:
            E = self.eng[e]
            if E.count > 0:
                evs.append((E.sem, E.count))
        for q in queues:
            n = self.dma_n[q]
            for i in range(min(n, self.NDMA)):
                cnt = (n - 1 - i) // self.NDMA + 1
                evs.append((self.dma_sems[q][i], 16 * cnt))
        return evs

    def barrier(self, engines=("pe", "act", "dve", "sp")):
        evs = self.all_events(engines=engines, queues=("sp",))
        for e in engines:
            self.eng[e].bar.extend(evs)

    def emit(self, final_events=()):
        nc = self.nc
        sems = self.sems
        fw = [ev for ev in final_events if ev is not None]
        with nc.Block() as block:
            def run(E, h, tail=()):
                for waits, fn, isem, ival in E.instrs:
                    for k, v in waits:
                        h.wait_ge(sems[k], v)
                    ins = fn(h)
                    if isem is not None:
                        ins.then_inc(sems[isem], ival)
                for k, v in tail:
                    h.wait_ge(sems[k], v)

            @block.sync
            def _(h):
                run(self.eng["sp"], h, fw)

            @block.tensor
            def _(h):
                run(self.eng["pe"], h)

            @block.scalar
            def _(h):
                run(self.eng["act"], h)

            @block.vector
            def _(h):
                run(self.eng["dve"], h)

            @block.gpsimd
            def _(h):
                run(self.eng["pool"], h)


class Mem:
    def __init__(self, nc, base, top):
        self.nc = nc
        self.cur = (base + 31) // 32 * 32
        self.top = top
        self.n = 0

    def alloc(self, name, shape, dtype):
        nbytes = int(np.prod(shape[1:])) * (2 if dtype == BF16 else 4)
        nbytes = (nbytes + 31) // 32 * 32
        assert self.cur + nbytes <= self.top, "SBUF overflow %s: need %d have %d" % (name, nbytes, self.top - self.cur)
        self.n += 1
        t = self.nc.alloc_sbuf_tensor_at("%s_%d" % (name, self.n), list(shape), dtype, offset=self.cur)
        self.cur += nbytes
        return t


DBG = {}


def weight_plan(n_layers):
    plan = []
    for layer in range(n_layers):
        j = layer // 2
        if layer % 2 == 0:
            plan += [sid_rec_in(j, g) for g in range(4)]
            for T in range(4):
                if T < 3:
                    plan += [sid_rec_in(j, g) for g in range(4)]
                plan += [sid_rec_out(j, g) for g in range(2)]
        elif not DBG.get("skip_dsa"):
            for T in range(4):
                plan += [sid_att_in(j, g) for g in (2, 3, 0, 1)]
                plan += [sid_att_o(j, g) for g in range(2)]
        for T in range(2):
            if DBG.get("skip_mlp"):
                continue
            for _rep in range(DBG.get("mlp_rep", 1)):
                plan += [sid_mlp_up(layer, g) for g in range(8)]
                plan += [sid_mlp_dn(layer, d) for d in range(8)]
    return plan


def build_program(n_layers=DEPTH):
    nc = bass.Bass("TRN2", target_bir_lowering=False)
    x_d = nc.dram_tensor("x", [S_LEN, D], F32, kind="ExternalInput").ap()
    vec_d = nc.dram_tensor("vecs", [128, NV], F32, kind="ExternalInput").ap()
    w_d = nc.dram_tensor("wslots", [NSLOT, 128, 4096], F32, kind="ExternalInput").ap()
    attw_d = nc.dram_tensor("attw", [2, 128, 3072], F32, kind="ExternalInput").ap()
    btab_d = nc.dram_tensor("btab", [128, 4096], F32, kind="ExternalInput").ap()
    far_d = nc.dram_tensor("farb", [128, 16], F32, kind="ExternalInput").ap()
    out_d = nc.dram_tensor("out", [S_LEN, D], F32, kind="ExternalOutput").ap()

    with ExitStack() as st:
        S = Sched(nc, st)
        mem = Mem(nc, nc.sbuf_base, nc.sbuf_top)
        banks = [st.enter_context(nc.psum_tensor("bank%d" % i, [128, 512], F32)) for i in range(8)]
        bt = [Tok() for _ in range(8)]

        xT = mem.alloc("xT", [128, NCH, S_LEN], F32)
        xtok = [[Tok() for _ in range(4)] for _ in range(NCH)]
        ring = [mem.alloc("ring%d" % i, [128, 4096], BF16) for i in range(3)]
        rtok = [Tok() for _ in range(3)]
        vecs = mem.alloc("vecs", [128, NV], F32); t_vecs = Tok()
        ident = mem.alloc("ident", [128, 128], F32); t_ident = Tok()
        identb = mem.alloc("identb", [128, 128], BF16); t_identb = Tok()
        onesb = mem.alloc("onesb", [128, 128], BF16); t_ones = Tok()
        clam = mem.alloc("clam", [128, 16], F32); t_clam = Tok()
        cneg = mem.alloc("cneg", [128, 128], F32); t_cneg = Tok()
        arena_base = mem.cur

        plan = weight_plan(n_layers)
        wstate = {"i": 0, "issued": 0}

        def w_issue_upto(n):
            while wstate["issued"] < min(n, len(plan)):
                i = wstate["issued"]
                sid = plan[i]
                r = ring[i % 3]
                S.dma("pool", lambda h, r=r, sid=sid: h.dma_start(out=r[:, :], in_=w_d[sid]),
                      writes=[rtok[i % 3]])
                wstate["issued"] += 1

        def w_next(sid):
            i = wstate["i"]
            assert plan[i] == sid, (i, plan[i], sid)
            w_issue_upto(i + 1)
            wstate["i"] = i + 1
            return ring[i % 3], rtok[i % 3], i

        def w_prefetch():
            w_issue_upto(wstate["i"] + 2)

        S.dma("sp", lambda h: h.dma_start(out=vecs[:, :], in_=vec_d), writes=[t_vecs])
        S.op("pool", lambda h: h.memset(ident[:, :], 0.0), writes=[t_ident])
        S.op("pool", lambda h: h.affine_select(out=ident[:, :], in_=ident[:, :], pattern=[[-1, 128]],
                                                compare_op=ALU.not_equal, fill=1.0, base=0, channel_multiplier=1),
             reads=[t_ident], writes=[t_ident])
        S.op("pool", lambda h: h.tensor_copy(out=identb[:, :], in_=ident[:, :]), reads=[t_ident], writes=[t_identb])
        S.op("pool", lambda h: h.memset(onesb[:, :], 1.0), writes=[t_ones])
        S.op("pool", lambda h: h.memset(cneg[:, :], 0.0), writes=[t_cneg])
        S.op("pool", lambda h: h.affine_select(out=cneg[:, :], in_=cneg[:, :], pattern=[[-1, 128]],
                                                compare_op=ALU.is_ge, fill=NEG_MASK, base=0, channel_multiplier=1),
             reads=[t_cneg], writes=[t_cneg])
        S.op("act", lambda h: h.activation(out=clam[:, :], in_=vecs[:, V_LAM:V_LAM + 16], func=AF.Exp, scale=-1.0),
             reads=[t_vecs], writes=[t_clam])
        S.op("act", lambda h: h.activation(out=clam[:, :], in_=clam[:, :], func=AF.Ln, bias=1.0, scale=1.0),
             reads=[t_clam], writes=[t_clam])
        S.op("dve", lambda h: h.tensor_scalar(out=clam[:, :], in0=clam[:, :], scalar1=-RG_C, scalar2=None, op0=ALU.mult),
             reads=[t_clam], writes=[t_clam])
        w_issue_upto(2)

        class Rot:
            def __init__(self, items):
                self.items = items
                self.i = 0

            def next(self):
                it = self.items[self.i % len(self.items)]
                self.i += 1
                return it

        def arena(with_pool=False, no_barrier=False):
            if no_barrier:
                return Mem(nc, arena_base, nc.sbuf_top)
            S.barrier()
            if with_pool:
                S.eng["pool"].bar.extend(S.all_events(engines=("pe", "act", "dve", "sp"), queues=("sp",)))
            return Mem(nc, arena_base, nc.sbuf_top)

        evac = {"i": 0}

        def evac_copy(out_ap, in_ap, reads, writes, eng=None):
            if eng is None:
                eng = "act" if evac["i"] % 2 == 0 else "dve"
                evac["i"] += 1
            if eng == "act":
                return S.op("act", lambda h: h.copy(out=out_ap, in_=in_ap), reads=reads, writes=writes)
            return S.op("dve", lambda h: h.tensor_copy(out=out_ap, in_=in_ap), reads=reads, writes=writes)

        def phase_load():
            am = Mem(nc, nc.sbuf_top - 2 * 4096 - 64, nc.sbuf_top)
            stg = [am.alloc("stg", [128, D], F32) for _ in range(2)]
            stk = [Tok() for _ in range(2)]
            brot = Rot(list(range(8)))
            for t in range(16):
                sg, sk = stg[t % 2], stk[t % 2]
                S.dma("sp", lambda h, sg=sg, t=t: h.dma_start(out=sg[:, :], in_=x_d[t * 128:(t + 1) * 128, :]),
                      writes=[sk])
                for half in range(2):
                    b = brot.next()
                    for i in range(4):
                        c = half * 4 + i
                        S.op("pe", lambda h, b=b, i=i, c=c, sg=sg: h.transpose(banks[b][:, i * 128:(i + 1) * 128],
                                                                              sg[:, c * 128:(c + 1) * 128], ident[:, :]),
                             reads=[sk, t_ident], writes=[bt[b]], inc=(i == 3))
                    o_ap = xT[:, half * 4:half * 4 + 4, t * 128:(t + 1) * 128]
                    i_ap = banks[b][:, 0:512].rearrange("p (a b) -> p a b", a=4)
                    evac_copy(o_ap, i_ap, reads=[bt[b]], writes=[xtok[c][t // 4] for c in range(half * 4, half * 4 + 4)])

        def norm_tile(am_bufs, tok0, ntok, gcol, hn, hn_tok, ssbank, out_f32=False):
            sq_rot, sq_tok, rs, rs_tok = am_bufs
            for s0 in range(0, ntok, 512):
                t4 = (tok0 + s0) // 512
                for c in range(NCH):
                    k = sq_rot.next()
                    S.op("act", lambda h, k=k, c=c, s0=s0: h.activation(out=k[0][:, :], in_=xT[:, c, tok0 + s0:tok0 + s0 + 512],
                                                                         func=AF.Square),
                         reads=[xtok[c][t4]], writes=[k[1]])
                    S.op("pe", lambda h, k=k, c=c: h.matmul(banks[ssbank][:, :], lhsT=onesb[:, :], rhs=k[0][:, :],
                                                           start=(c == 0), stop=(c == NCH - 1)),
                         reads=[k[1], t_ones], writes=[bt[ssbank]], inc=True)
                S.op("act", lambda h: h.activation(out=rs[:, :], in_=banks[ssbank][:, :], func=AF.Sqrt,
                                                   bias=EPS, scale=1.0 / D),
                     reads=[bt[ssbank]], writes=[rs_tok])
                S.op("dve", lambda h: h.reciprocal(out=rs[:, :], in_=rs[:, :]), reads=[rs_tok], writes=[rs_tok])
                for c in range(NCH):
                    S.op("dve", lambda h, c=c, s0=s0: h.scalar_tensor_tensor(
                        out=hn[:, c, s0:s0 + 512], in0=xT[:, c, tok0 + s0:tok0 + s0 + 512],
                        scalar=vecs[:, gcol + c:gcol + c + 1], op0=ALU.mult, in1=rs[:, :], op1=ALU.mult),
                        reads=[xtok[c][t4], rs_tok, t_vecs], writes=[hn_tok])

        def norm_bufs(am):
            sq = [(am.alloc("sq", [128, 512], BF16), Tok()) for _ in range(2)]
            rs = am.alloc("rs", [128, 512], F32)
            return (Rot(sq), None, rs, Tok())

        def phase_mlp(layer):
            am = arena()
            nb = norm_bufs(am)
            hn = am.alloc("hn", [128, NCH, 1024], BF16); t_hn = Tok()
            aT = am.alloc("aT", [128, 32, 1024], BF16)
            a_tok = [[Tok() for _ in range(2)] for _ in range(32)]
            rl = [(am.alloc("rl", [128, 512], BF16), Tok()) for _ in range(3)]
            rl_rot = Rot(rl)
            up_rot = Rot([0, 1, 2])
            dn_rot = Rot([3, 4, 5])
            ssbank = 7
            for T in range(2):
                tok0 = T * 1024
                norm_tile(nb, tok0, 1024, VG_MLP + layer * 8, hn, t_hn, ssbank)
                for g in range(8):
                    slot, stok, _ = w_next(sid_mlp_up(layer, g))
                    sv = slot[:, :].rearrange("p (c n) -> p c n", c=8)
                    for fc in range(4):
                        f = g * 4 + fc
                        for half in range(2):
                            b = up_rot.next()
                            for c in range(NCH):
                                S.op("pe", lambda h, b=b, c=c, fc=fc, half=half, sv=sv: h.matmul(
                                    banks[b][:, :], lhsT=sv[:, c, fc * 128:(fc + 1) * 128],
                                    rhs=hn[:, c, half * 512:(half + 1) * 512], start=(c == 0), stop=(c == NCH - 1)),
                                    reads=[stok, t_hn], writes=[bt[b]], inc=(c == NCH - 1))
                            k = rl_rot.next()
                            S.op("act", lambda h, b=b, k=k: h.activation(out=k[0][:, :], in_=banks[b][:, :], func=AF.Relu),
                                 reads=[bt[b]], writes=[k[1]])
                            S.op("dve", lambda h, b=b, k=k, f=f, half=half: h.tensor_tensor(
                                out=aT[:, f, half * 512:(half + 1) * 512], in0=k[0][:, :], in1=banks[b][:, :], op=ALU.mult),
                                reads=[k[1], bt[b]], writes=[a_tok[f][half]])
                    w_prefetch()
                for d in range(8):
                    slot, stok, _ = w_next(sid_mlp_dn(layer, d))
                    sv = slot[:, :].rearrange("p (f n) -> p f n", f=32)
                    for half in range(2):
                        b = dn_rot.next()
                        for f in range(32):
                            S.op("pe", lambda h, b=b, f=f, half=half, sv=sv: h.matmul(
                                banks[b][:, :], lhsT=sv[:, f, :], rhs=aT[:, f, half * 512:(half + 1) * 512],
                                start=(f == 0), stop=(f == 31)),
                                reads=[stok, a_tok[f][half]], writes=[bt[b]], inc=(f == 31))
                        t4 = T * 2 + half
                        S.op("dve", lambda h, b=b, d=d, t4=t4: h.tensor_tensor(
                            out=xT[:, d, t4 * 512:(t4 + 1) * 512], in0=xT[:, d, t4 * 512:(t4 + 1) * 512],
                            in1=banks[b][:, :], op=ALU.add),
                            reads=[bt[b], xtok[d][t4]], writes=[xtok[d][t4]])
                    w_prefetch()

        def phase_rglru(layer):
            j = layer // 2
            am = arena(with_pool=True, no_barrier=(layer == 0))
            am.top = nc.sbuf_top - 2 * 4096 - 64 if layer == 0 else am.top
            nb = norm_bufs(am)
            hn = am.alloc("hn", [128, NCH, 512], BF16); t_hn = Tok()
            gates = [am.alloc("gate", [128, NCH, 512], BF16) for _ in range(2)]
            g_toks = [[Tok() for _ in range(NCH)] for _ in range(2)]
            xr = am.alloc("xr", [128, NCH, 515], F32); xr_tok = [Tok() for _ in range(NCH)]
            yT = am.alloc("yT", [128, NCH, 512], BF16); y_tok = [Tok() for _ in range(4)]
            hst = am.alloc("hst", [128, NCH], F32); hs_tok = [Tok() for _ in range(NCH)]
            gw = am.alloc("gw", [128, 4096], BF16); t_gw = Tok()
            sets = []
            for _ in range(2):
                sets.append(dict(
                    xc=am.alloc("xc", [128, 2, 512], F32), t_xc=Tok(),
                    xcb=am.alloc("xcb", [128, 2, 512], BF16), t_xcb=Tok(),
                    r=am.alloc("r", [128, 2, 512], F32), t_r=Tok(),
                    i=am.alloc("i", [128, 2, 512], F32), t_i=Tok(),
                    m=am.alloc("m", [128, 2, 512], F32), t_m=Tok(),
                    hs=am.alloc("hs", [128, 2, 512], F32), t_hs=Tok()))
            S.dma("pool", lambda h: h.dma_start(out=gw[:, :], in_=w_d[sid_rec_gate(j)]), writes=[t_gw])
            gv = gw[:, :].rearrange("p (w n j i) -> p w n j i", w=2, n=4, j=2)
            S.op("dve", lambda h: h.memset(xr[:, :, 0:3], 0.0), writes=xr_tok)
            S.op("dve", lambda h: h.memset(hst[:, :], 0.0), writes=hs_tok)
            prot = Rot([0, 1, 2, 3])
            grot = Rot([4, 5, 6])
            ssbank = 7
            cw = V_CONVW + j * 32
            prog = {"conv": 0}

            def gen_win(T):
                tok0 = T * 512
                gate = gates[T % 2]
                g_tok = g_toks[T % 2]
                norm_tile(nb, tok0, 512, VG_MIX + layer * 8, hn, t_hn, ssbank)
                yield
                for g in range(4):
                    slot, stok, _ = w_next(sid_rec_in(j, g))
                    sv = slot[:, :].rearrange("p (c n) -> p c n", c=8)
                    for oc in range(4):
                        o = g * 4 + oc
                        if o >= 8:
                            while prog["conv"] <= (o - 8) // 2 and prog["conv"] < 4 and T > 0:
                                yield
                        b = prot.next()
                        for c in range(NCH):
                            S.op("pe", lambda h, b=b, c=c, oc=oc, sv=sv: h.matmul(
                                banks[b][:, :], lhsT=sv[:, c, oc * 128:(oc + 1) * 128], rhs=hn[:, c, :],
                                start=(c == 0), stop=(c == NCH - 1)),
                                reads=[stok, t_hn], writes=[bt[b]], inc=(c == NCH - 1))
                        if o < 8:
                            S.op("act", lambda h, b=b, o=o, gate=gate: h.activation(out=gate[:, o, :], in_=banks[b][:, :],
                                                                                    func=AF.Gelu_apprx_tanh),
                                 reads=[bt[b]], writes=[g_tok[o]])
                        else:
                            c2 = o - 8
                            S.op("act", lambda h, b=b, c2=c2: h.copy(out=xr[:, c2, 3:515], in_=banks[b][:, :]),
                                 reads=[bt[b]], writes=[xr_tok[c2]])
                        yield
                    w_prefetch()

            def gen_ew(T):
                gate = gates[T % 2]
                g_tok = g_toks[T % 2]
                prog["conv"] = 0

                def stage_a(n):
                    B = sets[n % 2]
                    c0 = 2 * n
                    for cc in range(2):
                        c = c0 + cc
                        S.op("dve", lambda h, B=B, cc=cc, c=c: h.tensor_scalar(
                            out=B["xc"][:, cc, :], in0=xr[:, c, 0:512], scalar1=vecs[:, cw + c:cw + c + 1],
                            scalar2=vecs[:, V_CONVB + j * 8 + c:V_CONVB + j * 8 + c + 1], op0=ALU.mult, op1=ALU.add),
                            reads=[xr_tok[c], t_vecs], writes=[B["t_xc"]])
                        for k in range(1, 4):
                            S.op("dve", lambda h, B=B, cc=cc, c=c, k=k: h.scalar_tensor_tensor(
                                out=B["xc"][:, cc, :], in0=xr[:, c, k:k + 512],
                                scalar=vecs[:, cw + k * 8 + c:cw + k * 8 + c + 1], op0=ALU.mult,
                                in1=B["xc"][:, cc, :], op1=ALU.add),
                                reads=[xr_tok[c], t_vecs, B["t_xc"]], writes=[B["t_xc"]])
                        S.op("act", lambda h, c=c: h.copy(out=xr[:, c, 0:3], in_=xr[:, c, 512:515]),
                             reads=[xr_tok[c]], writes=[xr_tok[c]])
                    prog["conv"] = n + 1
                    S.op("act", lambda h, B=B: h.copy(out=B["xcb"][:, :, :], in_=B["xc"][:, :, :]),
                         reads=[B["t_xc"]], writes=[B["t_xcb"]])
                    yield
                    for cc in range(2):
                        c = c0 + cc
                        for wsel, dst, tk, bcol in ((0, "r", "t_r", V_BA), (1, "i", "t_i", V_BX)):
                            b = grot.next()
                            for jc in range(2):
                                S.op("pe", lambda h, b=b, wsel=wsel, n=n, jc=jc, cc=cc, B=B: h.matmul(
                                    banks[b][:, :], lhsT=gv[:, wsel, n, jc, cc * 128:(cc + 1) * 128],
                                    rhs=B["xcb"][:, jc, :], start=(jc == 0), stop=(jc == 1)),
                                    reads=[t_gw, B["t_xcb"]], writes=[bt[b]], inc=(jc == 1))
                            S.op("act", lambda h, b=b, B=B, dst=dst, cc=cc, bcol=bcol, c=c: h.activation(
                                out=B[dst][:, cc, :], in_=banks[b][:, :], func=AF.Sigmoid,
                                bias=vecs[:, bcol + j * 8 + c:bcol + j * 8 + c + 1], scale=1.0),
                                reads=[bt[b], t_vecs], writes=[B[tk]])
                        yield
                    for cc in range(2):
                        c = c0 + cc
                        S.op("act", lambda h, B=B, cc=cc, c=c: h.activation(
                            out=B["r"][:, cc, :], in_=B["r"][:, cc, :], func=AF.Exp,
                            scale=clam[:, j * 8 + c:j * 8 + c + 1]),
                            reads=[B["t_r"], t_clam], writes=[B["t_r"]])
                    S.op("act", lambda h, B=B: h.activation(out=B["m"][:, :, :], in_=B["r"][:, :, :], func=AF.Square),
                         reads=[B["t_r"]], writes=[B["t_m"]])
                    S.op("act", lambda h, B=B: h.activation(out=B["m"][:, :, :], in_=B["m"][:, :, :], func=AF.Relu,
                                                           bias=1.0, scale=-1.0),
                         reads=[B["t_m"]], writes=[B["t_m"]])
                    S.op("act", lambda h, B=B: h.activation(out=B["m"][:, :, :], in_=B["m"][:, :, :], func=AF.Sqrt),
                         reads=[B["t_m"]], writes=[B["t_m"]])
                    yield

                def stage_b(n):
                    B = sets[n % 2]
                    c0 = 2 * n
                    S.op("dve", lambda h, B=B: h.tensor_tensor(out=B["i"][:, :, :], in0=B["i"][:, :, :], in1=B["xc"][:, :, :],
                                                              op=ALU.mult),
                         reads=[B["t_i"], B["t_xc"]], writes=[B["t_i"]])
                    S.op("dve", lambda h, B=B: h.tensor_tensor(out=B["i"][:, :, :], in0=B["i"][:, :, :], in1=B["m"][:, :, :],
                                                              op=ALU.mult),
                         reads=[B["t_i"], B["t_m"]], writes=[B["t_i"]])
                    yield
                    for cc in range(2):
                        c = c0 + cc
                        S.op("dve", lambda h, B=B, cc=cc, c=c: h.tensor_tensor_scan(
                            out=B["hs"][:, cc, :], data0=B["r"][:, cc, :], data1=B["i"][:, cc, :],
                            initial=hst[:, c:c + 1], op0=ALU.mult, op1=ALU.add),
                            reads=[B["t_r"], B["t_i"], hs_tok[c]], writes=[B["t_hs"]])
                        S.op("act", lambda h, B=B, cc=cc, c=c: h.copy(out=hst[:, c:c + 1], in_=B["hs"][:, cc, 511:512]),
                             reads=[B["t_hs"]], writes=[hs_tok[c]])
                    S.op("dve", lambda h, B=B, c0=c0, gate=gate: h.tensor_tensor(out=yT[:, c0:c0 + 2, :], in0=B["hs"][:, :, :],
                                                                                in1=gate[:, c0:c0 + 2, :], op=ALU.mult),
                         reads=[B["t_hs"], g_tok[c0], g_tok[c0 + 1]], writes=[y_tok[n]])
                    yield


                yield from stage_a(0)
                for n in range(4):
                    if n < 3:
                        yield from stage_a(n + 1)
                    yield from stage_b(n)

            def wout(T):
                for g in range(2):
                    slot, stok, _ = w_next(sid_rec_out(j, g))
                    sv = slot[:, :].rearrange("p (c n) -> p c n", c=8)
                    for dc in range(4):
                        d = g * 4 + dc
                        b = prot.next()
                        for c in range(NCH):
                            S.op("pe", lambda h, b=b, c=c, dc=dc, sv=sv: h.matmul(
                                banks[b][:, :], lhsT=sv[:, c, dc * 128:(dc + 1) * 128], rhs=yT[:, c, :],
                                start=(c == 0), stop=(c == NCH - 1)),
                                reads=[stok] + y_tok, writes=[bt[b]], inc=(c == NCH - 1))
                        S.op("dve", lambda h, b=b, d=d, T=T: h.tensor_tensor(
                            out=xT[:, d, T * 512:(T + 1) * 512], in0=xT[:, d, T * 512:(T + 1) * 512],
                            in1=banks[b][:, :], op=ALU.add),
                            reads=[bt[b], xtok[d][T]], writes=[xtok[d][T]])
                    w_prefetch()

            def run_ilv(ga, gb):
                da = db = False
                while not (da and db):
                    if not da:
                        try:
                            next(ga)
                        except StopIteration:
                            da = True
                            prog["conv"] = 4
                    if not db:
                        try:
                            next(gb)
                        except StopIteration:
                            db = True

            for _ in gen_win(0):
                pass
            for T in range(4):
                if T < 3:
                    run_ilv(gen_ew(T), gen_win(T + 1))
                else:
                    for _ in gen_ew(T):
                        pass
                wout(T)

        def phase_dsa(layer):
            j = layer // 2
            am = arena(with_pool=True)
            nb = norm_bufs(am)
            kiA = am.alloc("kiA", [128, S_LEN], BF16); ki_tok = [Tok() for _ in range(4)]
            kiB = am.alloc("kiB", [128, S_LEN], BF16)
            cTc = am.alloc("cTc", [128, S_LEN], BF16); c_tok = [Tok() for _ in range(4)]
            V = am.alloc("V", [128, 16, 16 * 65], BF16); v_tok = [Tok() for _ in range(16)]
            attw = am.alloc("attw", [128, 3072], BF16); t_attw = Tok()
            btab = am.alloc("btab", [128, 4096], BF16); t_btab = Tok()
            btf = am.alloc("btf", [128, 16], F32); t_btf = Tok()
            hn = am.alloc("hn", [128, NCH, 512], BF16); t_hn = Tok()
            qT = am.alloc("qT", [128, NCH, 512], BF16); q_tok = [Tok() for _ in range(NCH)]
            qiT = am.alloc("qiT", [128, 4, 512], BF16); qi_tok = [Tok() for _ in range(4)]
            craw = am.alloc("craw", [128, 512], F32); t_craw = Tok()
            wabs = am.alloc("wabs", [128, 32], F32); t_wabs = Tok()
            wsgn = am.alloc("wsgn", [128, 32], F32); t_wsgn = Tok()
            score = am.alloc("score", [128, S_LEN], F32); sc_tok = [Tok() for _ in range(4)]
            mask = am.alloc("mask", [128, S_LEN], BF16); t_mask = Tok()
            MT = am.alloc("MT", [128, 16, 128], BF16); mt_tok = [Tok() for _ in range(4)]
            qlat = am.alloc("qlat", [128, 16, 128], BF16); ql_tok = [Tok() for _ in range(4)]
            rl_rot = Rot([(am.alloc("rl", [128, 512], F32), Tok()) for _ in range(2)])
            e_rot = Rot([(am.alloc("e", [128, 512], BF16), Tok()) for _ in range(3)])
            em_rot = Rot([(am.alloc("em", [128, 512], BF16), Tok()) for _ in range(3)])
            on = am.alloc("on", [128, 1024], BF16); t_on = Tok()
            bis = am.alloc("bis", [128, 8], F32); t_bis = Tok(); t_bis2 = Tok()
            wtab = am.alloc("wtab", [128, NIT + 1], F32); t_wtab = Tok()
            recip = am.alloc("recip", [128, 16], F32); t_recip = Tok()
            S.dma("pool", lambda h: h.dma_start(out=attw[:, :], in_=attw_d[j]), writes=[t_attw])
            S.dma("pool", lambda h: h.dma_start(out=btab[:, :], in_=btab_d), writes=[t_btab])
            S.dma("sp", lambda h: h.dma_start(out=btf[:, :], in_=far_d), writes=[t_btf])
            bt4 = btab[:, :].rearrange("p (o h q) -> p o h q", o=2, h=16)
            for o in range(2):
                S.op("dve", lambda h, o=o: h.tensor_tensor(
                    out=bt4[:, o, :, :], in0=bt4[:, o, :, :],
                    in1=btf[:, :].unsqueeze(2).to_broadcast([128, 16, 128]), op=ALU.subtract),
                    reads=[t_btab, t_btf], writes=[t_btab])
            V4 = V[:, :, :].rearrange("p k (h e) -> p k h e", e=65)
            S.op("dve", lambda h: h.memset(V4[:, :, :, 64:65], 1.0), writes=v_tok)
            S.op("dve", lambda h: h.memset(kiA[64:128, :], 0.0), writes=ki_tok)
            S.op("dve", lambda h: h.memset(kiB[0:64, :], 0.0), writes=ki_tok)
            IDXE = DBG.get("idx_eng", "dve")
            prot = Rot([0, 1, 5, 6])
            lrot = Rot([0, 1])
            lrot3 = Rot([0, 1, 7])
            drot = Rot([5, 6])
            OB = [2, 3, 4]
            XB = 7
            SUB = DBG.get("dsa_sub", 9)
            for T in range(4):
                tok0 = T * 512
                if SUB < 3:
                    if SUB >= 1:
                        norm_tile(nb, tok0, 512, VG_MIX + layer * 8, hn, t_hn, XB)
                    for g in range(6):
                        sid = sid_att_in(j, g) if g < 4 else sid_att_o(j, g - 4)
                        slot, stok, _ = w_next(sid)
                        sv = slot[:, :].rearrange("p (c n) -> p c n", c=8)
                        if SUB >= 2 and g < 2:
                            for oc in range(4):
                                o = g * 4 + oc
                                b = prot.next()
                                for c in range(NCH):
                                    S.op("pe", lambda h, b=b, c=c, oc=oc, sv=sv: h.matmul(
                                        banks[b][:, :], lhsT=sv[:, c, oc * 128:(oc + 1) * 128], rhs=hn[:, c, :],
                                        start=(c == 0), stop=(c == NCH - 1)),
                                        reads=[stok, t_hn], writes=[bt[b]], inc=(c == NCH - 1))
                                evac_copy(qT[:, o, :], banks[b][:, :], reads=[bt[b]], writes=[q_tok[o]])
                        else:
                            S.op("pe", lambda h, sv=sv: h.matmul(banks[0][:, :], lhsT=sv[:, 0, 0:128], rhs=sv[:, 0, 0:512],
                                                                 start=True, stop=True), reads=[stok], writes=[bt[0]])
                        w_prefetch()
                    continue
                norm_tile(nb, tok0, 512, VG_MIX + layer * 8, hn, t_hn, XB)
                slot, stok, _ = w_next(sid_att_in(j, 2))
                sv = slot[:, :].rearrange("p (c n) -> p c n", c=8)
                b = prot.next()
                for c in range(NCH):
                    S.op("pe", lambda h, b=b, c=c, sv=sv: h.matmul(banks[b][:, :], lhsT=sv[:, c, 0:128], rhs=hn[:, c, :],
                                                                  start=(c == 0), stop=(c == NCH - 1)),
                         reads=[stok, t_hn], writes=[bt[b]], inc=(c == NCH - 1))
                S.op("act", lambda h, b=b: h.copy(out=craw[:, :], in_=banks[b][:, :]), reads=[bt[b]], writes=[t_craw])
                b = prot.next()
                for c in range(NCH):
                    S.op("pe", lambda h, b=b, c=c, sv=sv: h.matmul(banks[b][:, :], lhsT=sv[:, c, 128:256], rhs=hn[:, c, :],
                                                                  start=(c == 0), stop=(c == NCH - 1)),
                         reads=[stok, t_hn], writes=[bt[b]], inc=(c == NCH - 1))
                evac_copy(kiA[0:64, tok0:tok0 + 512], banks[b][0:64, :], reads=[bt[b]], writes=[ki_tok[T]])
                evac_copy(kiB[64:128, tok0:tok0 + 512], banks[b][64:128, :], reads=[bt[b]], writes=[ki_tok[T]])
                SK = DBG.get("skip", ())
                for qb in range(4 if "wi" not in SK else 0):
                    for c in range(NCH):
                        S.op("pe", lambda h, qb=qb, c=c, sv=sv: h.matmul(
                            banks[XB][:, qb * 8:(qb + 1) * 8], lhsT=hn[:, c, qb * 128:(qb + 1) * 128],
                            rhs=sv[:, c, 256:264], start=(c == 0), stop=(c == NCH - 1), skip_group_check=True),
                            reads=[stok, t_hn], writes=[bt[XB]], inc=(c == NCH - 1))
                if IDXE == "dve2":
                    S.op("act", lambda h: h.mul(out=wabs[:, :], in_=banks[XB][:, 0:32], mul=IDX_W_SCALE),
                         reads=[bt[XB]], writes=[t_wabs])
                elif "abs" not in SK:
                    S.op("act", lambda h: h.activation(out=wabs[:, :], in_=banks[XB][:, 0:32], func=AF.Abs, scale=IDX_W_SCALE),
                         reads=[bt[XB]], writes=[t_wabs])
                    S.op("act", lambda h: h.activation(out=wsgn[:, :], in_=banks[XB][:, 0:32], func=AF.Sign),
                         reads=[bt[XB]], writes=[t_wsgn])
                w_prefetch()
                k = nb[0].next()
                if "kvn" in SK:
                    S.op("act", lambda h, tok0=tok0: h.copy(out=cTc[:, tok0:tok0 + 512], in_=craw[:, :]), reads=[t_craw], writes=[c_tok[T]])
                S.op("act", lambda h, k=k: h.activation(out=k[0][:, :], in_=craw[:, :], func=AF.Square),
                     reads=[t_craw], writes=[k[1]])
                S.op("pe", lambda h, k=k: h.matmul(banks[XB][:, :], lhsT=onesb[:, :], rhs=k[0][:, :], start=True, stop=True),
                     reads=[k[1], t_ones], writes=[bt[XB]])
                rs, rs_tok = nb[2], nb[3]
                S.op("act", lambda h: h.activation(out=rs[:, :], in_=banks[XB][:, :], func=AF.Sqrt, bias=EPS, scale=1.0 / 128),
                     reads=[bt[XB]], writes=[rs_tok])
                S.op("dve", lambda h: h.reciprocal(out=rs[:, :], in_=rs[:, :]), reads=[rs_tok], writes=[rs_tok])
                S.op("dve", lambda h, tok0=tok0: h.scalar_tensor_tensor(
                    out=cTc[:, tok0:tok0 + 512], in0=craw[:, :], scalar=vecs[:, V_KVG + j:V_KVG + j + 1], op0=ALU.mult,
                    in1=rs[:, :], op1=ALU.mult),
                    reads=[t_craw, rs_tok, t_vecs], writes=[c_tok[T]])
                slot, stok, _ = w_next(sid_att_in(j, 3))
                sv = slot[:, :].rearrange("p (c n) -> p c n", c=8)
                for oc in range(4):
                    b = prot.next()
                    for c in range(NCH):
                        S.op("pe", lambda h, b=b, c=c, oc=oc, sv=sv: h.matmul(
                            banks[b][:, :], lhsT=sv[:, c, oc * 128:(oc + 1) * 128], rhs=hn[:, c, :],
                            start=(c == 0), stop=(c == NCH - 1)),
                            reads=[stok, t_hn], writes=[bt[b]], inc=(c == NCH - 1))
                    evac_copy(qiT[:, oc, :], banks[b][:, :], reads=[bt[b]], writes=[qi_tok[oc]])
                w_prefetch()
                xb_bf = banks[XB][:, :].bitcast(BF16)

                def gen_index(qb, T=T):
                    G = T * 4 + qb
                    nk = (G + 1) * 128
                    q0 = qb * 128
                    nch = (nk + 511) // 512
                    for hi in range(H_IDX):
                        o, par = hi // 2, hi % 2
                        for kc in range(nch):
                            w = min(512, nk - kc * 512)
                            b = drot.next()
                            S.op("pe", lambda h, b=b, o=o, par=par, q0=q0, kc=kc, w=w: h.matmul(
                                banks[b][:, 0:w], lhsT=qiT[:, o, q0:q0 + 128],
                                rhs=(kiA if par == 0 else kiB)[:, kc * 512:kc * 512 + w], start=True, stop=True),
                                reads=[qi_tok[o], ki_tok[kc]], writes=[bt[b]])
                            r_ = rl_rot.next()
                            col = qb * 8 + hi
                            if IDXE == "dve2":
                                if hi == 0:
                                    S.op("dve", lambda h, b=b, w=w, kc=kc, col=col: h.tensor_scalar(
                                        out=score[:, kc * 512:kc * 512 + w], in0=banks[b][:, 0:w], scalar1=0.0,
                                        scalar2=wabs[:, col:col + 1], op0=ALU.max, op1=ALU.mult),
                                        reads=[bt[b], t_wabs], writes=[sc_tok[kc]])
                                else:
                                    S.op("dve", lambda h, b=b, r_=r_, w=w, col=col: h.tensor_scalar(
                                        out=r_[0][:, 0:w], in0=banks[b][:, 0:w], scalar1=0.0,
                                        scalar2=wabs[:, col:col + 1], op0=ALU.max, op1=ALU.mult),
                                        reads=[bt[b], t_wabs], writes=[r_[1]])
                                    S.op("dve", lambda h, r_=r_, w=w, kc=kc: h.tensor_tensor(
                                        out=score[:, kc * 512:kc * 512 + w], in0=score[:, kc * 512:kc * 512 + w],
                                        in1=r_[0][:, 0:w], op=ALU.add),
                                        reads=[r_[1], sc_tok[kc]], writes=[sc_tok[kc]])
                                yield
                                continue
                            S.op("act", lambda h, b=b, r_=r_, w=w, col=col: h.activation(
                                out=r_[0][:, 0:w], in_=banks[b][:, 0:w], func=AF.Relu, scale=wabs[:, col:col + 1]),
                                reads=[bt[b], t_wabs], writes=[r_[1]])
                            if hi == 0:
                                S.op(IDXE, lambda h, r_=r_, w=w, kc=kc, col=col: h.tensor_scalar(
                                    out=score[:, kc * 512:kc * 512 + w], in0=r_[0][:, 0:w], scalar1=wsgn[:, col:col + 1],
                                    scalar2=None, op0=ALU.mult),
                                    reads=[r_[1], t_wsgn], writes=[sc_tok[kc]])
                            elif IDXE == "dve":
                                S.op("dve", lambda h, r_=r_, w=w, kc=kc, col=col: h.scalar_tensor_tensor(
                                    out=score[:, kc * 512:kc * 512 + w], in0=r_[0][:, 0:w], scalar=wsgn[:, col:col + 1],
                                    op0=ALU.mult, in1=score[:, kc * 512:kc * 512 + w], op1=ALU.add),
                                    reads=[r_[1], t_wsgn, sc_tok[kc]], writes=[sc_tok[kc]])
                            else:
                                S.op("pool", lambda h, r_=r_, w=w, col=col: h.tensor_scalar(
                                    out=r_[0][:, 0:w], in0=r_[0][:, 0:w], scalar1=wsgn[:, col:col + 1],
                                    scalar2=None, op0=ALU.mult),
                                    reads=[r_[1], t_wsgn], writes=[r_[1]])
                                S.op("pool", lambda h, r_=r_, w=w, kc=kc: h.tensor_tensor(
                                    out=score[:, kc * 512:kc * 512 + w], in0=score[:, kc * 512:kc * 512 + w],
                                    in1=r_[0][:, 0:w], op=ALU.add),
                                    reads=[r_[1], sc_tok[kc]], writes=[sc_tok[kc]])
                            yield
                    sct = sc_tok[0:nch]
                    if G >= 2:
                        S.op("dve", lambda h, nk=nk: h.tensor_reduce(out=bis[:, 0:1], in_=score[:, 0:nk], axis=mybir.AxisListType.X,
                                                                     op=ALU.max, apply_absolute_value=True),
                             reads=sct, writes=[t_bis])
                        S.op("dve", lambda h: h.tensor_scalar(out=wtab[:, :], in0=vecs[:, V_POW:V_POW + NIT + 1],
                                                              scalar1=bis[:, 0:1], scalar2=None, op0=ALU.mult),
                             reads=[t_bis, t_vecs], writes=[t_wtab])
                        S.op("dve", lambda h: h.memset(bis[:, 1:2], 0.0), reads=[t_bis], writes=[t_bis])
                    S.op("dve", lambda h, G=G: h.tensor_tensor(out=score[:, G * 128:(G + 1) * 128],
                                                               in0=score[:, G * 128:(G + 1) * 128], in1=cneg[:, :], op=ALU.add),
                         reads=[sc_tok[G // 4], t_cneg], writes=[sc_tok[G // 4]])
                    yield

                def gen_select(qb, T=T, local=False):
                    G = T * 4 + qb
                    nk = (G + 1) * 128
                    nch = (nk + 511) // 512
                    sct = sc_tok[0:nch]
                    if G >= 2:
                        for it in range(NIT):
                            S.op("act", lambda h, nk=nk: h.activation(out=mask[:, 0:nk], in_=score[:, 0:nk], func=AF.Sign,
                                                                      bias=bis[:, 1:2], scale=1.0, accum_out=bis[:, 2:3]),
                                 reads=sct + [t_bis], writes=[t_mask, t_bis])
                            yield
                            if local:
                                S.op("act", lambda h, nk=nk: h.activation(out=bis[:, 3:4], in_=bis[:, 2:3], func=AF.Sign,
                                                                          bias=float(nk) - 511.5, scale=1.0),
                                     reads=[t_bis], writes=[t_bis])
                                S.op("act", lambda h, it=it: h.activation(out=bis[:, 1:2], in_=bis[:, 3:4], func=AF.Identity,
                                                                          bias=bis[:, 1:2], scale=wtab[:, it + 1:it + 2]),
                                     reads=[t_bis, t_wtab], writes=[t_bis])
                                yield
                                continue
                            S.op("dve", lambda h, nk=nk, it=it: h.scalar_tensor_tensor(
                                out=bis[:, 3:4], in0=bis[:, 2:3], scalar=511.5 - float(nk), op0=ALU.is_gt,
                                in1=wtab[:, it:it + 1], op1=ALU.mult),
                                reads=[t_bis, t_wtab], writes=[t_bis2])
                            S.op("dve", lambda h, it=it: h.scalar_tensor_tensor(
                                out=bis[:, 1:2], in0=bis[:, 3:4], scalar=wtab[:, it + 1:it + 2], op0=ALU.subtract,
                                in1=bis[:, 1:2], op1=ALU.add),
                                reads=[t_bis, t_bis2, t_wtab], writes=[t_bis])
                            yield
                        S.op("dve", lambda h: h.tensor_scalar(out=bis[:, 4:5], in0=bis[:, 1:2], scalar1=wtab[:, NIT:NIT + 1],
                                                              scalar2=-1.0, op0=ALU.subtract, op1=ALU.mult),
                             reads=[t_bis, t_wtab], writes=[t_bis])
                        S.op("dve", lambda h, nk=nk: h.tensor_scalar(out=mask[:, 0:nk], in0=score[:, 0:nk], scalar1=bis[:, 4:5],
                                                                     scalar2=None, op0=ALU.is_gt),
                             reads=sct + [t_bis], writes=[t_mask])
                    else:
                        S.op("dve", lambda h, nk=nk: h.tensor_scalar(out=mask[:, 0:nk], in0=score[:, 0:nk], scalar1=-1.0e29,
                                                                     scalar2=None, op0=ALU.is_gt),
                             reads=sct, writes=[t_mask])
                    yield

                def gen_attn(qb, T=T):
                    G = T * 4 + qb
                    q0 = qb * 128
                    for k0 in range(0, G + 1, 4):
                        n4 = min(4, G + 1 - k0)
                        for i in range(n4):
                            kb = k0 + i
                            S.op("pe", lambda h, i=i, kb=kb: h.transpose(xb_bf[:, i * 128:(i + 1) * 128],
                                                                        mask[:, kb * 128:(kb + 1) * 128], identb[:, :]),
                                 reads=[t_mask, t_identb], writes=[bt[XB]], inc=(i == n4 - 1))
                        evac_copy(MT[:, k0:k0 + n4, :], xb_bf[:, 0:n4 * 128].rearrange("p (a q) -> p a q", a=n4),
                                  reads=[bt[XB]], writes=[mt_tok[k0 // 4]])
                    yield
                    for hg in range(4):
                        for hl in range(4):
                            hh = hg * 4 + hl
                            o = hh // 2
                            S.op("pe", lambda h, hl=hl, o=o, hh=hh, q0=q0: h.matmul(
                                banks[XB][:, hl * 128:(hl + 1) * 128],
                                lhsT=attw[:, hh * 128:(hh + 1) * 128],
                                rhs=qT[:, o, q0:q0 + 128], start=True, stop=True, skip_group_check=True),
                                reads=[t_attw, q_tok[o]], writes=[bt[XB]], inc=(hl == 3))
                        S.op("act", lambda h, hg=hg: h.mul(out=qlat[:, hg * 4:(hg + 1) * 4, :],
                                                           in_=banks[XB][:, :].rearrange("p (a q) -> p a q", a=4), mul=ATT_SCALE),
                             reads=[bt[XB]], writes=[ql_tok[hg]])
                    yield
                    started = [False, False, False]
                    tiles = [(hg, kb) for hg in range(4) for kb in range(G + 1)]
                    pend = []
                    SKEW = 2

                    def issue_pv(hg, kb, m_):
                        for hl in range(4):
                            hh = hg * 4 + hl
                            ob = hh // 6
                            col = (hh % 6) * 65
                            st_flag = not started[ob]
                            started[ob] = True
                            S.op("pe", lambda h, ob=ob, col=col, m_=m_, hl=hl, kb=kb, hh=hh, st_flag=st_flag: h.matmul(
                                banks[OB[ob]][:, col:col + 65], lhsT=m_[0][:, hl * 128:(hl + 1) * 128],
                                rhs=V[:, kb, hh * 65:(hh + 1) * 65], start=st_flag, stop=False, skip_group_check=True),
                                reads=[m_[1], v_tok[kb]], writes=[bt[OB[ob]]], inc=(hl == 3))

                    for (hg, kb) in tiles:
                        b = lrot3.next()
                        S.op("pe", lambda h, b=b, kb=kb, hg=hg: h.matmul(
                            banks[b][:, :], lhsT=cTc[:, kb * 128:(kb + 1) * 128], rhs=qlat[:, hg * 4:(hg + 1) * 4, :],
                            start=True, stop=True),
                            reads=[c_tok[kb // 4], ql_tok[hg]], writes=[bt[b]])
                        e_ = e_rot.next()
                        off = G - kb
                        if off < 2:
                            S.op("dve", lambda h, b=b, off=off, hg=hg: h.tensor_tensor(
                                out=banks[b][:, :].rearrange("p (a q) -> p a q", a=4),
                                in0=banks[b][:, :].rearrange("p (a q) -> p a q", a=4),
                                in1=bt4[:, off, hg * 4:(hg + 1) * 4, :], op=ALU.add),
                                reads=[bt[b], t_btab], writes=[bt[b]])
                        S.op("act", lambda h, b=b, e_=e_: h.activation(out=e_[0][:, :], in_=banks[b][:, :], func=AF.Exp),
                             reads=[bt[b]], writes=[e_[1]])
                        m_ = em_rot.next()
                        S.op("dve", lambda h, e_=e_, m_=m_, kb=kb: h.tensor_tensor(
                            out=m_[0][:, :].rearrange("p (a q) -> p a q", a=4),
                            in0=e_[0][:, :].rearrange("p (a q) -> p a q", a=4),
                            in1=MT[:, kb:kb + 1, :].to_broadcast([128, 4, 128]), op=ALU.mult),
                            reads=[e_[1], mt_tok[kb // 4]], writes=[m_[1]])
                        pend.append((hg, kb, m_))
                        if len(pend) > SKEW:
                            issue_pv(*pend.pop(0))
                        yield
                    while pend:
                        issue_pv(*pend.pop(0))
                    yield
                    for ob in range(3):
                        nh = 6 if ob < 2 else 4
                        o3 = banks[OB[ob]][:, 0:nh * 65].rearrange("p (h e) -> p h e", e=65)
                        S.op("dve", lambda h, ob=ob, nh=nh, o3=o3: h.reciprocal(
                            out=recip[:, ob * 6:ob * 6 + nh].unsqueeze(2), in_=o3[:, :, 64:65]),
                            reads=[bt[OB[ob]]], writes=[t_recip])
                        S.op("dve", lambda h, ob=ob, nh=nh, o3=o3: h.tensor_tensor(
                            out=on[:, ob * 384:ob * 384 + nh * 64].rearrange("p (h v) -> p h v", v=64),
                            in0=o3[:, :, 0:64],
                            in1=recip[:, ob * 6:ob * 6 + nh].unsqueeze(2).to_broadcast([128, nh, 64]), op=ALU.mult),
                            reads=[bt[OB[ob]], t_recip], writes=[t_on])
                    yield
                    for half in range(2):
                        for i in range(4):
                            o = half * 4 + i
                            S.op("pe", lambda h, i=i, o=o: h.transpose(xb_bf[:, i * 128:(i + 1) * 128],
                                                                      on[:, o * 128:(o + 1) * 128], identb[:, :]),
                                 reads=[t_on, t_identb], writes=[bt[XB]], inc=(i == 3))
                        evac_copy(hn[:, half * 4:half * 4 + 4, q0:q0 + 128],
                                  xb_bf[:, 0:512].rearrange("p (a q) -> p a q", a=4), reads=[bt[XB]], writes=[t_hn])
                    yield

                def gen_prest(T=T, tok0=tok0):
                    prot2 = Rot([0, 1])
                    for g in range(2):
                        slot, stok, _ = w_next(sid_att_in(j, g))
                        sv = slot[:, :].rearrange("p (c n) -> p c n", c=8)
                        for oc in range(4):
                            o = g * 4 + oc
                            b = prot2.next()
                            for c in range(NCH):
                                S.op("pe", lambda h, b=b, c=c, oc=oc, sv=sv: h.matmul(
                                    banks[b][:, :], lhsT=sv[:, c, oc * 128:(oc + 1) * 128], rhs=hn[:, c, :],
                                    start=(c == 0), stop=(c == NCH - 1)),
                                    reads=[stok, t_hn], writes=[bt[b]], inc=(c == NCH - 1))
                            evac_copy(qT[:, o, :], banks[b][:, :], reads=[bt[b]], writes=[q_tok[o]])
                            yield
                        w_prefetch()
                    for kl in range(4 if "V" not in SK else 0):
                        kb = T * 4 + kl
                        for half in range(2):
                            b = prot2.next()
                            S.op("pe", lambda h, b=b, kb=kb, half=half: h.matmul(
                                banks[b][:, :], lhsT=cTc[:, kb * 128:(kb + 1) * 128],
                                rhs=attw[:, 2048 + half * 512:2048 + (half + 1) * 512], start=True, stop=True),
                                reads=[c_tok[T], t_attw], writes=[bt[b]])
                            evac_copy(V4[:, kb, half * 8:(half + 1) * 8, 0:64],
                                      banks[b][:, :].rearrange("p (h v) -> p h v", v=64), reads=[bt[b]], writes=[v_tok[kb]])
                            yield
                    yield

                def chain(*gens):
                    for g in gens:
                        yield from g

                def interleave(ga, gb, ra=1, rb=1):
                    done_a = done_b = False
                    while not (done_a and done_b):
                        for _ in range(ra):
                            if not done_a:
                                try:
                                    next(ga)
                                except StopIteration:
                                    done_a = True
                        for _ in range(rb):
                            if not done_b:
                                try:
                                    next(gb)
                                except StopIteration:
                                    done_b = True

                interleave(chain(gen_index(0), gen_select(0, local=True)), gen_prest(), 1, 1)
                for qb in range(4):
                    if qb < 3 and DBG.get("pipeline", True):
                        interleave(gen_attn(qb), chain(gen_index(qb + 1), gen_select(qb + 1)), 1, 1)
                    else:
                        for _ in gen_attn(qb):
                            pass
                        if qb < 3:
                            for _ in chain(gen_index(qb + 1), gen_select(qb + 1)):
                                pass
                for g in range(2):
                    slot, stok, _ = w_next(sid_att_o(j, g))
                    sv = slot[:, :].rearrange("p (c n) -> p c n", c=8)
                    for dc in range(4):
                        d = g * 4 + dc
                        b = prot.next()
                        for c in range(NCH):
                            S.op("pe", lambda h, b=b, c=c, dc=dc, sv=sv: h.matmul(
                                banks[b][:, :], lhsT=sv[:, c, dc * 128:(dc + 1) * 128], rhs=hn[:, c, :],
                                start=(c == 0), stop=(c == NCH - 1)),
                                reads=[stok, t_hn], writes=[bt[b]], inc=(c == NCH - 1))
                        S.op("dve", lambda h, b=b, d=d, T=T: h.tensor_tensor(
                            out=xT[:, d, T * 512:(T + 1) * 512], in0=xT[:, d, T * 512:(T + 1) * 512],
                            in1=banks[b][:, :], op=ALU.add),
                            reads=[bt[b], xtok[d][T]], writes=[xtok[d][T]])
                    w_prefetch()

        def phase_final():
            am = arena()
            nb = norm_bufs(am)
            yf = am.alloc("yf", [128, NCH, 512], F32); t_yf = Tok()
            ost = [(am.alloc("ost", [128, D], F32), Tok()) for _ in range(2)]
            brot = Rot([0, 1, 2, 3, 4, 5])
            n_out = 0
            for T in range(4):
                norm_tile(nb, T * 512, 512, VG_FIN, yf, t_yf, 7)
                for tb in range(4):
                    og, ok = ost[n_out % 2]
                    n_out += 1
                    for half in range(2):
                        b = brot.next()
                        for i in range(4):
                            c = half * 4 + i
                            S.op("pe", lambda h, b=b, i=i, c=c, tb=tb: h.transpose(
                                banks[b][:, i * 128:(i + 1) * 128], yf[:, c, tb * 128:(tb + 1) * 128], ident[:, :]),
                                reads=[t_yf, t_ident], writes=[bt[b]], inc=(i == 3))
                        evac_copy(og[:, half * 512:(half + 1) * 512], banks[b][:, :], reads=[bt[b]], writes=[ok])
                    r0 = T * 512 + tb * 128
                    S.dma("sp", lambda h, og=og, r0=r0: h.dma_start(out=out_d[r0:r0 + 128, :], in_=og[:, :]), reads=[ok])

        phase_load()
        for layer in range(n_layers):
            if layer % 2 == 0:
                phase_rglru(layer)
            elif not DBG.get("skip_dsa"):
                phase_dsa(layer)
            if not DBG.get("skip_mlp"):
                for _rep in range(DBG.get("mlp_rep", 1)):
                    phase_mlp(layer)
        phase_final()
        S.emit(S.all_events(engines=(), queues=("sp",)))
    return nc


def _fm(v):
    v = np.asarray(v, np.float32)
    lead = v.shape[:-1]
    return v.reshape(lead + (8, 128))


def _slot_kn(W, ncol=512):
    K, N = W.shape
    G = N // ncol
    return np.ascontiguousarray(W.reshape(8, 128, G, ncol).transpose(2, 1, 0, 3)).reshape(G, 128, 8 * ncol)


def prepare_shared(inp):
    f = lambda k: np.asarray(inp[k], np.float32)
    vec = np.zeros((128, NV), np.float32)
    vec[:, VG_MIX:VG_MIX + 32] = f("norm_mix_g").reshape(4, 8, 128).transpose(2, 0, 1).reshape(128, 32)
    vec[:, VG_MLP:VG_MLP + 32] = f("norm_mlp_g").reshape(4, 8, 128).transpose(2, 0, 1).reshape(128, 32)
    vec[:, VG_FIN:VG_FIN + 8] = f("final_norm_g").reshape(8, 128).T
    vec[:, V_CONVW:V_CONVW + 64] = f("rec_conv_w").reshape(2, 4, 8, 128).transpose(3, 0, 1, 2).reshape(128, 64)
    for key, col in (("rec_conv_b", V_CONVB), ("rec_b_a", V_BA), ("rec_b_x", V_BX), ("rec_lambda", V_LAM)):
        vec[:, col:col + 16] = f(key).reshape(2, 8, 128).transpose(2, 0, 1).reshape(128, 16)
    vec[:, V_KVG:V_KVG + 2] = f("att_kv_norm_g").T
    vec[:, V_POW:V_POW + NIT + 1] = -(2.0 ** -np.arange(NIT + 1, dtype=np.float64)).astype(np.float32)[None, :]
    ws = np.zeros((NSLOT, 128, 4096), np.float32)
    up, dn = f("mlp_w_up"), f("mlp_w_down")
    for l in range(4):
        ws[sid_mlp_up(l, 0):sid_mlp_up(l, 0) + 8] = _slot_kn(up[l])
        ws[sid_mlp_dn(l, 0):sid_mlp_dn(l, 0) + 8] = dn[l].reshape(32, 128, 8, 128).transpose(2, 1, 0, 3).reshape(8, 128, 4096)
    rin, rout, wa, wx = f("rec_w_in"), f("rec_w_out"), f("rec_w_a"), f("rec_w_x")
    for j in range(2):
        ws[sid_rec_in(j, 0):sid_rec_in(j, 0) + 4] = _slot_kn(rin[j])
        ga = wa[j].reshape(4, 2, 128, 256).transpose(2, 0, 1, 3).reshape(128, 2048)
        gx = wx[j].reshape(4, 2, 128, 256).transpose(2, 0, 1, 3).reshape(128, 2048)
        ws[sid_rec_gate(j)] = np.concatenate([ga, gx], axis=1)
        ws[sid_rec_out(j, 0):sid_rec_out(j, 0) + 2] = _slot_kn(rout[j])
    ain, ao = f("att_w_in"), f("att_w_o")
    for j in range(2):
        W = ain[j]
        ws[sid_att_in(j, 0):sid_att_in(j, 0) + 2] = _slot_kn(W[:, 0:1024])
        Wc = np.concatenate([W[:, 1024:1152], W[:, 1664:1728], W[:, 1664:1728], W[:, 1728:1736],
                             np.zeros((1024, 248), np.float32)], axis=1)
        ws[sid_att_in(j, 2)] = _slot_kn(Wc)[0]
        ws[sid_att_in(j, 3)] = _slot_kn(W[:, 1152:1664])[0]
        ws[sid_att_o(j, 0):sid_att_o(j, 0) + 2] = _slot_kn(ao[j])
    wuk, wuv = f("att_w_uk"), f("att_w_uv")
    attw = np.zeros((2, 128, 3072), np.float32)
    for j in range(2):
        for hh in range(16):
            par = hh % 2
            attw[j, par * 64:(par + 1) * 64, hh * 128:(hh + 1) * 128] = wuk[j, hh]
        attw[j, :, 2048:3072] = wuv[j].transpose(1, 0, 2).reshape(128, 1024)
    rb = f("rel_bias")
    kk = np.arange(128)[:, None, None]
    off = np.arange(2)[None, :, None]
    qq = np.arange(128)[None, None, :]
    dist = off * 128 + qq - kk
    n = np.maximum(dist, 0)
    nf = np.maximum(n, 1).astype(np.float32)
    large = 16 + (np.log(nf / 16) / math.log(128 / 16) * 16).astype(np.int32)
    large = np.minimum(large, 31)
    bucket = np.where(n < 16, n, large)
    btab = rb[bucket]
    btab = np.ascontiguousarray(btab.transpose(0, 1, 3, 2)).reshape(128, 4096)
    farb = np.ascontiguousarray(np.broadcast_to(rb[31][None, :], (128, 16)))
    return dict(vecs=vec, wslots=ws, attw=attw, btab=btab, farb=farb)


_CACHE = {}


def kernel(**inputs):
    x = np.asarray(inputs["x"], np.float32)
    shared = prepare_shared(inputs)
    n_layers = int(_CACHE.get("n_layers", DEPTH))
    nc = build_program(n_layers)
    in_maps = []
    for b in range(8):
        m = dict(shared)
        m["x"] = np.ascontiguousarray(x[b])
        in_maps.append(m)
    res = run_bass_kernel_spmd(nc, in_maps, core_ids=list(range(8)))
    out = np.stack([np.asarray(r["out"], np.float32) for r in res.results], axis=0)
    return out
```
